# Optimizing a Trainium2 kernel written in Bass

```python
import math
import jax, jax.numpy as jnp
from jax import lax
import numpy as np


D_MODEL = 1024
BATCH = 4
SEQ = 4096
DEPTH = 2

EPS = 1e-6
CONV_WIDTH = 4
N_BRANCHES = 3
GMLP_WIDTH = D_MODEL
GMLP_GROUPS = 8
GMLP_GROUP_DIM = GMLP_WIDTH // GMLP_GROUPS
GMLP_CHUNK = 128
LRU_WIDTH = D_MODEL
LRU_HEADS = 8
LRU_HEAD_DIM = LRU_WIDTH // LRU_HEADS
LRU_C = 8.0
SSD_WIDTH = D_MODEL
SSD_HEAD_DIM = 64
SSD_HEADS = SSD_WIDTH // SSD_HEAD_DIM
SSD_GROUPS = 4
SSD_HEADS_PER_GROUP = SSD_HEADS // SSD_GROUPS
SSD_STATE = 128
SSD_CHUNK = 128
SSD_CONV_DIM = SSD_WIDTH + 2 * SSD_GROUPS * SSD_STATE
MLP_HIDDEN = 4 * D_MODEL
GMLP_IN = 2 * GMLP_WIDTH
LRU_IN = 2 * LRU_WIDTH
SSD_IN = SSD_WIDTH + SSD_CONV_DIM + SSD_HEADS
GATE_IN = N_BRANCHES * D_MODEL
D_IN = GMLP_IN + LRU_IN + SSD_IN + GATE_IN
SPLIT_POINTS = (GMLP_IN,
                GMLP_IN + LRU_IN,
                GMLP_IN + LRU_IN + SSD_WIDTH,
                GMLP_IN + LRU_IN + SSD_WIDTH + SSD_CONV_DIM,
                GMLP_IN + LRU_IN + SSD_IN)

kernel_name = 'hybrid_gmlp_rglru_ssd_gated_merge'


def rms_norm(x, g):
    x32 = x.astype(jnp.float32)
    y = x32 * lax.rsqrt(jnp.mean(x32 * x32, axis=-1, keepdims=True) + EPS)
    return (y * g.astype(jnp.float32)).astype(x.dtype)


def layer_norm(x, g, b):
    x32 = x.astype(jnp.float32)
    mu = jnp.mean(x32, axis=-1, keepdims=True)
    xc = x32 - mu
    y = xc * lax.rsqrt(jnp.mean(xc * xc, axis=-1, keepdims=True) + EPS)
    return (y * g.astype(jnp.float32) + b.astype(jnp.float32)).astype(x.dtype)


def causal_dwconv(x, w, b):
    k, c = w.shape
    y = lax.conv_general_dilated(x, w[:, None, :], window_strides=(1,), padding=[(k - 1, 0)],
                                 dimension_numbers=('NWC', 'WIO', 'NWC'), feature_group_count=c)
    return y + b


def gmlp_mixer(za, ln_g, ln_b, w_s, b_s):
    bsz, seq, _ = za.shape
    u, v = jnp.split(jax.nn.gelu(za), 2, axis=-1)
    v = layer_norm(v, ln_g, ln_b)
    nc = seq // GMLP_CHUNK
    vc = v.reshape(bsz, nc, GMLP_CHUNK, GMLP_GROUPS, GMLP_GROUP_DIM)
    causal = jnp.tril(jnp.ones((GMLP_CHUNK, GMLP_CHUNK), dtype=bool))
    w = jnp.where(causal, w_s, 0)
    mixed = jnp.einsum('gts,bcsgd->bctgd', w, vc) + b_s.T[:, :, None]
    return u * mixed.reshape(bsz, seq, GMLP_WIDTH)


def rg_lru_mixer(zb, conv_w, conv_b, w_r, b_r, w_i, b_i, lam):
    bsz, seq, _ = zb.shape
    xb, gate = jnp.split(zb, 2, axis=-1)
    xb = causal_dwconv(xb, conv_w, conv_b)
    xh = xb.reshape(bsz, seq, LRU_HEADS, LRU_HEAD_DIM)
    r = jax.nn.sigmoid(jnp.einsum('bshi,hij->bshj', xh, w_r).reshape(bsz, seq, LRU_WIDTH) + b_r)
    i = jax.nn.sigmoid(jnp.einsum('bshi,hij->bshj', xh, w_i).reshape(bsz, seq, LRU_WIDTH) + b_i)
    log_a = -LRU_C * r.astype(jnp.float32) * jax.nn.softplus(-lam.astype(jnp.float32))
    a = jnp.exp(log_a)
    inp = jnp.sqrt(-jnp.expm1(2.0 * log_a)) * (i * xb).astype(jnp.float32)

    def combine(c1, c2):
        a1, b1 = c1
        a2, b2 = c2
        return a1 * a2, a2 * b1 + b2

    _, h = lax.associative_scan(combine, (a, inp), axis=1)
    return jax.nn.gelu(gate) * h.astype(gate.dtype)


def segsum(x):
    t = x.shape[-1]
    cs = jnp.cumsum(x, axis=-1)
    seg = cs[..., :, None] - cs[..., None, :]
    return jnp.where(jnp.tril(jnp.ones((t, t), dtype=bool)), seg, -jnp.inf)


def ssd_mixer(z, xbc, dt_raw, conv_w, conv_b, dt_bias, a_log, d_skip, norm_g):
    bsz, seq, _ = z.shape
    nc = seq // SSD_CHUNK
    g, r, q = SSD_GROUPS, SSD_HEADS_PER_GROUP, SSD_CHUNK
    xbc = jax.nn.silu(causal_dwconv(xbc, conv_w, conv_b))
    xs, bm, cm = jnp.split(xbc, [SSD_WIDTH, SSD_WIDTH + g * SSD_STATE], axis=-1)
    dt = jax.nn.softplus((dt_raw + dt_bias).astype(jnp.float32))
    a = -jnp.exp(a_log.astype(jnp.float32))
    x32 = xs.astype(jnp.float32).reshape(bsz, nc, q, g, r, SSD_HEAD_DIM)
    xdt = x32 * dt.reshape(bsz, nc, q, g, r)[..., None]
    bc = bm.astype(jnp.float32).reshape(bsz, nc, q, g, SSD_STATE)
    cc = cm.astype(jnp.float32).reshape(bsz, nc, q, g, SSD_STATE)
    adt = (dt * a).reshape(bsz, nc, q, g, r).transpose(0, 3, 4, 1, 2)
    a_cs = jnp.cumsum(adt, axis=-1)
    decay = jnp.exp(segsum(adt))
    cb = jnp.einsum('bclgn,bcsgn->bgcls', cc, bc)
    y_diag = jnp.einsum('bgcls,bgrcls,bcsgrp->bclgrp', cb, decay, xdt)
    decay_states = jnp.exp(a_cs[..., -1:] - a_cs)
    states = jnp.einsum('bcsgn,bgrcs,bcsgrp->cbgrpn', bc, decay_states, xdt)
    chunk_decay = jnp.exp(a_cs[..., -1]).transpose(3, 0, 1, 2)

    def step(h, inp):
        dec, st = inp
        return h * dec[..., None, None] + st, h

    _, prev = lax.scan(step, jnp.zeros_like(states[0]), (chunk_decay, states))
    y_off = jnp.einsum('bclgn,cbgrpn,bgrcl->bclgrp', cc, prev, jnp.exp(a_cs))
    y = y_diag + y_off + x32 * d_skip.astype(jnp.float32).reshape(g, r)[:, :, None]
    y = y.reshape(bsz, seq, SSD_WIDTH) * jax.nn.silu(z.astype(jnp.float32))
    yg = y.reshape(bsz, seq, g, SSD_WIDTH // g)
    yg = yg * lax.rsqrt(jnp.mean(yg * yg, axis=-1, keepdims=True) + EPS)
    y = yg.reshape(bsz, seq, SSD_WIDTH) * norm_g.astype(jnp.float32)
    return y.astype(z.dtype)


def _normal(k, shape, scale):
    return scale * jax.random.normal(k, shape, jnp.float32)


def setup_inputs(seed: int = 0) -> dict:
    key = jax.random.key(seed)
    ks = jax.random.split(key, 32)
    L = DEPTH
    a0 = jax.random.uniform(ks[15], (L, LRU_WIDTH), jnp.float32, minval=0.9, maxval=0.999)
    dt0 = jnp.exp(jax.random.uniform(ks[18], (L, SSD_HEADS), jnp.float32,
                                     minval=math.log(1e-3), maxval=math.log(1e-1)))
    return {
        'x': _normal(ks[0], (BATCH, SEQ, D_MODEL), 1.0),
        'norm_mix_g': 1.0 + _normal(ks[1], (L, D_MODEL), 0.1),
        'w_in': _normal(ks[2], (L, D_MODEL, D_IN), D_MODEL ** -0.5),
        'b_gate': _normal(ks[3], (L, N_BRANCHES, D_MODEL), 0.1),
        'gmlp_ln_g': 1.0 + _normal(ks[4], (L, GMLP_WIDTH), 0.1),
        'gmlp_ln_b': _normal(ks[5], (L, GMLP_WIDTH), 0.1),
        'gmlp_w_s': _normal(ks[6], (L, GMLP_GROUPS, GMLP_CHUNK, GMLP_CHUNK), GMLP_CHUNK ** -0.5),
        'gmlp_b_s': 1.0 + _normal(ks[7], (L, GMLP_GROUPS, GMLP_CHUNK), 0.1),
        'lru_conv_w': _normal(ks[8], (L, CONV_WIDTH, LRU_WIDTH), CONV_WIDTH ** -0.5),
        'lru_conv_b': _normal(ks[9], (L, LRU_WIDTH), 0.1),
        'lru_w_r': _normal(ks[10], (L, LRU_HEADS, LRU_HEAD_DIM, LRU_HEAD_DIM), LRU_HEAD_DIM ** -0.5),
        'lru_b_r': _normal(ks[11], (L, LRU_WIDTH), 0.1),
        'lru_w_i': _normal(ks[12], (L, LRU_HEADS, LRU_HEAD_DIM, LRU_HEAD_DIM), LRU_HEAD_DIM ** -0.5),
        'lru_b_i': _normal(ks[13], (L, LRU_WIDTH), 0.1),
        'lru_lambda': jnp.log(a0) - jnp.log1p(-a0),
        'ssd_conv_w': _normal(ks[16], (L, CONV_WIDTH, SSD_CONV_DIM), CONV_WIDTH ** -0.5),
        'ssd_conv_b': _normal(ks[17], (L, SSD_CONV_DIM), 0.1),
        'ssd_dt_bias': dt0 + jnp.log(-jnp.expm1(-dt0)),
        'ssd_a_log': jnp.log(jax.random.uniform(ks[19], (L, SSD_HEADS), jnp.float32, minval=1.0, maxval=16.0)),
        'ssd_d': 1.0 + _normal(ks[20], (L, SSD_HEADS), 0.1),
        'ssd_norm_g': 1.0 + _normal(ks[21], (L, SSD_WIDTH), 0.1),
        'w_branch_a': _normal(ks[22], (L, GMLP_WIDTH, D_MODEL), GMLP_WIDTH ** -0.5),
        'w_branch_b': _normal(ks[23], (L, LRU_WIDTH, D_MODEL), LRU_WIDTH ** -0.5),
        'w_branch_c': _normal(ks[24], (L, SSD_WIDTH, D_MODEL), SSD_WIDTH ** -0.5),
        'w_out': _normal(ks[25], (L, D_MODEL, D_MODEL), D_MODEL ** -0.5),
        'norm_mlp_g': 1.0 + _normal(ks[26], (L, D_MODEL), 0.1),
        'w_mlp_up': _normal(ks[27], (L, D_MODEL, MLP_HIDDEN), D_MODEL ** -0.5),
        'w_mlp_down': _normal(ks[28], (L, MLP_HIDDEN, D_MODEL), MLP_HIDDEN ** -0.5),
        'final_norm_g': 1.0 + _normal(ks[29], (D_MODEL,), 0.1),
    }


def reference(x, norm_mix_g, w_in, b_gate, gmlp_ln_g, gmlp_ln_b, gmlp_w_s, gmlp_b_s,
              lru_conv_w, lru_conv_b, lru_w_r, lru_b_r, lru_w_i, lru_b_i, lru_lambda,
              ssd_conv_w, ssd_conv_b, ssd_dt_bias, ssd_a_log, ssd_d, ssd_norm_g,
              w_branch_a, w_branch_b, w_branch_c, w_out, norm_mlp_g, w_mlp_up, w_mlp_down,
              final_norm_g):
    bsz, seq, _ = x.shape
    h = x
    for l in range(DEPTH):
        hn = rms_norm(h, norm_mix_g[l])
        proj = jnp.einsum('bsd,de->bse', hn, w_in[l])
        za, zb, zc, xbc, dt_raw, g_raw = jnp.split(proj, SPLIT_POINTS, axis=-1)
        ya = gmlp_mixer(za, gmlp_ln_g[l], gmlp_ln_b[l], gmlp_w_s[l], gmlp_b_s[l])
        yb = rg_lru_mixer(zb, lru_conv_w[l], lru_conv_b[l], lru_w_r[l], lru_b_r[l],
                          lru_w_i[l], lru_b_i[l], lru_lambda[l])
        yc = ssd_mixer(zc, xbc, dt_raw, ssd_conv_w[l], ssd_conv_b[l], ssd_dt_bias[l],
                       ssd_a_log[l], ssd_d[l], ssd_norm_g[l])
        gates = jax.nn.sigmoid(g_raw.reshape(bsz, seq, N_BRANCHES, D_MODEL) + b_gate[l])
        merged = (gates[:, :, 0] * jnp.einsum('bse,ed->bsd', ya, w_branch_a[l])
                  + gates[:, :, 1] * jnp.einsum('bse,ed->bsd', yb, w_branch_b[l])
                  + gates[:, :, 2] * jnp.einsum('bse,ed->bsd', yc, w_branch_c[l]))
        h = h + jnp.einsum('bsd,de->bse', merged, w_out[l])
        hn = rms_norm(h, norm_mlp_g[l])
        up = jax.nn.relu(jnp.einsum('bsd,df->bsf', hn, w_mlp_up[l]))
        h = h + jnp.einsum('bsf,fd->bsd', up * up, w_mlp_down[l])
    return rms_norm(h, final_norm_g)
```

```python
from contextlib import ExitStack

import numpy as np
import concourse.bass as bass
import concourse.mybir as mybir
from concourse.bass_utils import run_bass_kernel_spmd

F32 = mybir.dt.float32
BF16 = mybir.dt.bfloat16
AF = mybir.ActivationFunctionType
ALU = mybir.AluOpType
AX = mybir.AxisListType

D = 1024
SEQ = 4096
BATCH = 4
DEPTH = 2
D_IN = 10256
T = 512
NJ = T // 128
EPS = 1e-6
N_CORES = 8

PV_NMIX, PV_NMLP, PV_LNG, PV_LNB, PV_LCW, PV_LCB, PV_BR, PV_BI, PV_LAM = 0, 8, 16, 24, 32, 64, 72, 80, 88
PV_SCW, PV_SCB, PV_BG, PV_SNG, PV_FIN, NPV = 96, 160, 176, 200, 208, 216
PR_DTB, PR_ALOG, PR_D, NPR = 0, 16, 32, 48
C_U, C_V, C_XB, C_GL, C_Z, C_X, C_B, C_C, C_DT, C_G = 0, 1024, 2048, 3072, 4096, 5120, 6144, 6656, 7168, 7184


class _Op:
    __slots__ = ("eng", "fn", "deps", "signal", "count", "dma_sem", "dma_count")

    def __init__(self, eng, fn, deps):
        self.eng = eng
        self.fn = fn
        self.deps = deps
        self.signal = False
        self.count = None
        self.dma_sem = None
        self.dma_count = None


class Prog:
    ENGS = ("pe", "act", "dve", "pool", "sp")

    def __init__(self, nc):
        self.nc = nc
        self.ops = {e: [] for e in self.ENGS}
        self.last_writer = {}
        self.readers = {}
        self.dma_sems = {}

    def _deps(self, reads, writes):
        deps = []
        for k in reads:
            w = self.last_writer.get(k)
            if w is not None:
                deps.append(w)
        for k in writes:
            w = self.last_writer.get(k)
            if w is not None:
                deps.append(w)
            deps.extend(self.readers.get(k, {}).values())
        return deps

    def _commit(self, op, reads, writes):
        rk = op.eng if op.dma_sem is None else ("dma", id(op))
        for k in reads:
            self.readers.setdefault(k, {})[rk] = op
        for k in writes:
            self.last_writer[k] = op
            self.readers[k] = {}

    def op(self, eng, fn, reads=(), writes=()):
        psr = [k for k in reads if isinstance(k, tuple) and k[0] == "ps"]
        if psr:
            writes = list(writes) + [k for k in psr if k not in writes]
        deps = self._deps(reads, writes)
        o = _Op(eng, fn, deps)
        for d in deps:
            d.signal = True
        self.ops[eng].append(o)
        self._commit(o, reads, writes)
        return o

    def dma(self, eng, fn, sem, reads=(), writes=()):
        deps = self._deps(reads, writes)
        o = _Op(eng, fn, deps)
        for d in deps:
            d.signal = True
        c = self.dma_sems.get(sem, 0) + 16
        self.dma_sems[sem] = c
        o.dma_sem = sem
        o.dma_count = c
        self.ops[eng].append(o)
        self._commit(o, reads, writes)
        return o

    def emit(self, final_waits=()):
        nc = self.nc
        with ExitStack() as es:
            esem = {e: es.enter_context(nc.semaphore("s_" + e)) for e in self.ENGS}
            dsem = {n: es.enter_context(nc.semaphore("d_" + n)) for n in self.dma_sems}
            for o in final_waits:
                o.signal = True
            for e in self.ENGS:
                c = 0
                for o in self.ops[e]:
                    if o.dma_sem is None and o.signal:
                        c += 1
                        o.count = c
            block = es.enter_context(nc.Block())

            def run(e, engine):
                waited = {}
                for o in self.ops[e]:
                    need = {}
                    for d in o.deps:
                        if d.dma_sem is not None:
                            key = ("d", d.dma_sem)
                            val = d.dma_count
                        else:
                            if d.eng == e and e in ("pe", "sp"):
                                continue
                            key = ("e", d.eng)
                            val = d.count
                        if val > need.get(key, 0):
                            need[key] = val
                    for key, val in need.items():
                        if waited.get(key, 0) >= val:
                            continue
                        waited[key] = val
                        s = dsem[key[1]] if key[0] == "d" else esem[key[1]]
                        engine.wait_ge(s, val)
                    ins = o.fn(engine)
                    if o.dma_sem is not None:
                        ins.then_inc(dsem[o.dma_sem], 16)
                    elif o.signal:
                        ins.then_inc(esem[e], 1)
                if e == "sp":
                    for o in final_waits:
                        if o.dma_sem is not None:
                            engine.wait_ge(dsem[o.dma_sem], o.dma_count)
                        else:
                            engine.wait_ge(esem[o.eng], o.count)

            @block.tensor
            def _(eng):
                run("pe", eng)

            @block.scalar
            def _(eng):
                run("act", eng)

            @block.vector
            def _(eng):
                run("dve", eng)

            @block.gpsimd
            def _(eng):
                run("pool", eng)

            @block.sync
            def _(eng):
                run("sp", eng)


class B:
    __slots__ = ("ap", "keys")

    def __init__(self, ap, keys):
        self.ap = ap
        self.keys = list(keys)

    def s(self, ap, keys=None):
        return B(ap, self.keys if keys is None else keys)


def bc(ap, shape, axis):
    return ap.unsqueeze(axis).to_broadcast(list(shape))


def build_program(n_tiles, debug_taps=False):
    nc = bass.Bass("TRN2", target_bir_lowering=False)
    P = Prog(nc)

    def din(name, shape):
        return nc.dram_tensor(name, list(shape), F32, kind="ExternalInput").ap()

    xT = din("xT", [D, SEQ])
    w_in = din("w_in", [DEPTH, D, D_IN])
    w_ba = din("w_branch_a", [DEPTH, D, D])
    w_bb = din("w_branch_b", [DEPTH, D, D])
    w_bc = din("w_branch_c", [DEPTH, D, D])
    w_out = din("w_out", [DEPTH, D, D])
    w_up = din("w_mlp_up", [DEPTH, D, 4 * D])
    w_dn = din("w_mlp_down", [DEPTH, 4 * D, D])
    pvec_d = din("pvec", [DEPTH, 128, NPV])
    prow_d = din("prow", [DEPTH, 128, NPR])
    bsrow_d = din("bsrow", [DEPTH, 1, 1024])
    gwT_d = din("gwT", [DEPTH, 128, 8, 128])
    wr_d = din("wr", [DEPTH, 128, 8, 128])
    wi_d = din("wi", [DEPTH, 128, 8, 128])
    consts_d = din("consts", [128, 3, 128])
    outT = nc.dram_tensor("outT", [D, SEQ], F32, kind="ExternalOutput").ap()
    taps = {}

    with ExitStack() as es:
        def sb(name, shape, dt):
            return es.enter_context(nc.sbuf_tensor("sb_" + name, list(shape), dt))

        consts = sb("consts", [128, 3, 128], F32)
        ident_bf = sb("ident_bf", [128, 128], BF16)
        U_f = consts[:, 1, :]
        Ls_f = consts[:, 2, :]
        ones_bf = sb("ones_bf", [128, 128], BF16)
        ones_f = sb("ones_f", [128, 128], F32)
        cst = sb("cst", [128, 4], F32)
        pv = [sb("pv%d" % l, [128, NPV], F32) for l in range(DEPTH)]
        pr = [sb("pr%d" % l, [128, NPR], F32) for l in range(DEPTH)]
        gw_bf = [sb("gw%d" % l, [128, 8, 128], BF16) for l in range(DEPTH)]
        wr_bf = [sb("wr%d" % l, [128, 8, 128], BF16) for l in range(DEPTH)]
        wi_bf = [sb("wi%d" % l, [128, 8, 128], BF16) for l in range(DEPTH)]
        wdt_bf = [sb("wdt%d" % l, [128, 8, 16], BF16) for l in range(DEPTH)]
        Eg = [sb("Eg%d" % l, [128, 8, 128], F32) for l in range(DEPTH)]
        Dg = [sb("Dg%d" % l, [128, 16, 128], BF16) for l in range(DEPTH)]
        clru = [sb("clru%d" % l, [128, 16], F32) for l in range(DEPTH)]
        arow = [sb("arow%d" % l, [128, 16], F32) for l in range(DEPTH)]
        S = [sb("S%d" % l, [128, 1024], F32) for l in range(DEPTH)]
        S_bf = [sb("Sbf%d" % l, [128, 1024], BF16) for l in range(DEPTH)]
        hst = [sb("hst%d" % l, [128, 8], F32) for l in range(DEPTH)]
        hist_l = [sb("histl%d" % l, [128, 8, 4], F32) for l in range(DEPTH)]
        hist_s = [sb("hists%d" % l, [128, 16, 4], F32) for l in range(DEPTH)]
        h_t = sb("h", [128, 8, T], F32)
        hn_t = sb("hn", [128, 8, T], BF16)
        sqb_t = sb("sqb", [128, 2, T], BF16)
        rstd_t = sb("rstd", [128, T], F32)
        lnt_t = sb("lnt", [128, T], F32)
        uT_t = sb("uT", [128, 8, T], F32)
        up_t = sb("up", [128, 32, T], BF16)
        NW = 3
        wring = [sb("wr_ring%d" % i, [128, 8, 512], BF16) for i in range(NW)]
        ARENA_N = 11776
        arena = sb("arena", [128, ARENA_N], F32)
        small = sb("small", [128, 256], F32)
        ps = es.enter_context(nc.psum_tensor("ps", [128, 4096], F32))
        bsrow = arena

        class Arena:
            def __init__(self):
                self.p = 0

            def reset(self):
                self.p = 0

            def f32(self, n, shape=None):
                a, b = self.p, self.p + n
                assert b <= ARENA_N, "arena overflow"
                self.p = (b + 127) // 128 * 128
                ap = arena[:, a:b]
                if shape is not None:
                    ap = ap.rearrange("p (a b) -> p a b", a=shape[0])
                return B(ap, [("A", g) for g in range(a // 128, (b + 127) // 128)])

            def bf16(self, n, shape=None):
                w = (n + 1) // 2
                a, b = self.p, self.p + w
                assert b <= ARENA_N, "arena overflow"
                self.p = (b + 127) // 128 * 128
                ap = arena[:, a:b].bitcast(BF16)
                if shape is not None:
                    ap = ap.rearrange("p (a b) -> p a b", a=shape[0])
                return B(ap, [("A", g) for g in range(a // 128, (b + 127) // 128)])

        AR = Arena()
        sm_p = [0]

        def sm(n):
            a = sm_p[0]
            sm_p[0] += n
            assert sm_p[0] <= 256
            return B(small[:, a:a + n], [("sm", w) for w in range(a // 8, (a + n - 1) // 8 + 1)])

        bank_p = [0]

        def bank(n=1):
            p = bank_p[0]
            if p % n:
                p += n - p % n
            if p + n > 8:
                p = 0
            bank_p[0] = (p + n) % 8
            return B(ps[:, p * 512:(p + n) * 512], [("ps", p + i) for i in range(n)])

        h = [B(h_t[:, c, :], [("h", c)]) for c in range(8)]
        hn = [B(hn_t[:, c, :], [("hn", c)]) for c in range(8)]
        sqb = [B(sqb_t[:, i, :], [("sqb", i)]) for i in range(2)]
        rstd = B(rstd_t[:], ["rstd"])
        lnt = B(lnt_t[:], ["lnt"])
        uT = [B(uT_t[:, c, :], [("uT", c)]) for c in range(8)]
        up = [B(up_t[:, c, :], [("up", c)]) for c in range(32)]
        ya, yb, yc, mg = up[0:8], up[8:16], up[16:24], up[24:32]
        PARAMS = ["params"]

        def pvc(l, col, n=1):
            return B(pv[l][:, col:col + n], PARAMS)

        def mm(out, lhsT, rhs, start=True, stop=True):
            P.op("pe", lambda e: e.matmul(out.ap, lhsT=lhsT.ap, rhs=rhs.ap, start=start, stop=stop),
                 reads=lhsT.keys + rhs.keys, writes=out.keys)

        def tr(out, in_):
            P.op("pe", lambda e: e.transpose(out=out.ap, in_=in_.ap, identity=ident_bf[:]),
                 reads=in_.keys + PARAMS, writes=out.keys)

        def act(out, in_, func, bias=None, scale=None, accum=None):
            rd = list(in_.keys)
            wr = list(out.keys)
            kw = {}
            if bias is not None:
                if isinstance(bias, B):
                    rd += bias.keys
                    kw["bias"] = bias.ap
                else:
                    kw["bias"] = bias
            if scale is not None:
                if isinstance(scale, B):
                    rd += scale.keys
                    kw["scale"] = scale.ap
                else:
                    kw["scale"] = scale
            if accum is not None:
                wr += accum.keys
                kw["accum_out"] = accum.ap
            P.op("act", lambda e: e.activation(out=out.ap, in_=in_.ap, func=func, **kw), reads=rd, writes=wr)

        def tt(out, in0, in1, op, eng="dve"):
            P.op(eng, lambda e: e.tensor_tensor(out=out.ap, in0=in0.ap, in1=in1.ap, op=op),
                 reads=in0.keys + in1.keys, writes=out.keys)

        def ts(out, in0, s1, op0, s2=None, op1=None, eng="dve"):
            rd = list(in0.keys)
            a1 = s1
            a2 = s2
            if isinstance(s1, B):
                rd += s1.keys
                a1 = s1.ap
            if isinstance(s2, B):
                rd += s2.keys
                a2 = s2.ap
            if op1 is None:
                P.op(eng, lambda e: e.tensor_scalar(out=out.ap, in0=in0.ap, scalar1=a1, scalar2=None, op0=op0),
                     reads=rd, writes=out.keys)
            else:
                P.op(eng, lambda e: e.tensor_scalar(out=out.ap, in0=in0.ap, scalar1=a1, scalar2=a2, op0=op0, op1=op1),
                     reads=rd, writes=out.keys)

        def stt(out, in0, scalar, in1, op0, op1, eng="dve"):
            rd = in0.keys + in1.keys
            a = scalar
            if isinstance(scalar, B):
                rd = rd + scalar.keys
                a = scalar.ap
            P.op(eng, lambda e: e.scalar_tensor_tensor(out=out.ap, in0=in0.ap, scalar=a, in1=in1.ap, op0=op0, op1=op1),
                 reads=rd, writes=out.keys)

        def cp(out, in_, eng="dve"):
            if eng == "act":
                P.op("act", lambda e: e.activation(out=out.ap, in_=in_.ap, func=AF.Copy), reads=in_.keys, writes=out.keys)
            else:
                P.op(eng, lambda e: e.tensor_copy(out=out.ap, in_=in_.ap), reads=in_.keys, writes=out.keys)

        def memset(buf, val, eng="dve"):
            P.op(eng, lambda e: e.memset(buf.ap, val), writes=buf.keys)

        def red_sum(out, in_):
            P.op("dve", lambda e: e.tensor_reduce(out=out.ap, in_=in_.ap, axis=AX.X, op=ALU.add),
                 reads=in_.keys, writes=out.keys)

        def scan(out, d0, d1, init):
            P.op("dve", lambda e: e.tensor_tensor_scan(out=out.ap, data0=d0.ap, data1=d1.ap, initial=init.ap,
                                                       op0=ALU.mult, op1=ALU.add),
                 reads=d0.keys + d1.keys + init.keys, writes=out.keys)

        def tap(name, buf, shape):
            if not debug_taps:
                return
            t = nc.dram_tensor("tap_" + name, list(shape), buf.ap.dtype, kind="ExternalOutput").ap()
            taps[name] = P.dma("sp", lambda e: e.dma_start(out=t, in_=buf.ap), "tap_" + name, reads=buf.keys)

        wcnt = [0]

        def wload(src3, kc0, c0, n=512):
            i = wcnt[0] % NW
            wcnt[0] += 1
            slot = wring[i]
            keys = [("w", i, 0), ("w", i, 1)]
            for hf in range(2):
                a, b = hf * 4, hf * 4 + 4
                dst = slot[:, a:b, 0:n]
                src = src3[:, kc0 + a:kc0 + b, c0:c0 + n]
                P.dma("pool", (lambda dst, src: (lambda e: e.dma_start(out=dst, in_=src)))(dst, src),
                      "w%d_%d" % (i, hf), writes=[keys[hf]])
            return slot, keys

        def wl(slot_keys, kc, c0, n):
            slot, keys = slot_keys
            return B(slot[:, kc, c0:c0 + n], [keys[kc // 4]])

        def w3(w, l):
            return w[l].rearrange("(kc p) e -> p kc e", p=128)

        P.dma("sp", lambda e: e.dma_start(out=consts[:], in_=consts_d), "ld_c", writes=PARAMS)
        for l in range(DEPTH):
            P.dma("sp", (lambda l: (lambda e: e.dma_start(out=pv[l][:], in_=pvec_d[l])))(l), "ld_pv%d" % l, writes=PARAMS)
            P.dma("sp", (lambda l: (lambda e: e.dma_start(out=pr[l][:], in_=prow_d[l])))(l), "ld_pr%d" % l, writes=PARAMS)
            P.dma("sp", (lambda l: (lambda e: e.dma_start(out=bsrow[0:1, l * 1024:(l + 1) * 1024], in_=bsrow_d[l])))(l),
                  "ld_bs%d" % l, writes=PARAMS)
            P.dma("pool", (lambda l: (lambda e: e.dma_start(out=gw_bf[l][:], in_=gwT_d[l])))(l), "ld_gw%d" % l, writes=PARAMS)
            P.dma("pool", (lambda l: (lambda e: e.dma_start(out=wr_bf[l][:], in_=wr_d[l])))(l), "ld_wr%d" % l, writes=PARAMS)
            P.dma("pool", (lambda l: (lambda e: e.dma_start(out=wi_bf[l][:], in_=wi_d[l])))(l), "ld_wi%d" % l, writes=PARAMS)
            P.dma("pool", (lambda l: (lambda e: e.dma_start(out=wdt_bf[l][:], in_=w3(w_in, l)[:, :, C_DT:C_DT + 16])))(l),
                  "ld_wdt%d" % l, writes=PARAMS)
        PB = B(None, PARAMS)
        P.op("dve", lambda e: e.memset(ones_bf[:], 1.0), reads=PARAMS, writes=PARAMS)
        P.op("dve", lambda e: e.memset(ones_f[:], 1.0), reads=PARAMS, writes=PARAMS)
        P.op("dve", lambda e: e.memset(cst[:, 0:1], EPS), reads=PARAMS, writes=PARAMS)
        P.op("dve", lambda e: e.memset(cst[:, 1:2], 1.0), reads=PARAMS, writes=PARAMS)
        P.op("dve", lambda e: e.tensor_copy(out=ident_bf[:], in_=consts[:, 0, :]), reads=PARAMS, writes=PARAMS)
        eps_b = B(cst[:, 0:1], PARAMS)
        one_b = B(cst[:, 1:2], PARAMS)
        for l in range(DEPTH):
            for t_, n_ in ((S[l], 1024), (hst[l], 8)):
                P.op("dve", (lambda t_: (lambda e: e.memset(t_[:], 0.0)))(t_), reads=PARAMS, writes=PARAMS)
            P.op("dve", (lambda l: (lambda e: e.memset(S_bf[l][:], 0.0)))(l), reads=PARAMS, writes=PARAMS)
            P.op("dve", (lambda l: (lambda e: e.memset(hist_l[l][:], 0.0)))(l), reads=PARAMS, writes=PARAMS)
            P.op("dve", (lambda l: (lambda e: e.memset(hist_s[l][:], 0.0)))(l), reads=PARAMS, writes=PARAMS)
            P.op("dve", (lambda l: (lambda e: e.tensor_tensor(out=gw_bf[l][:], in0=gw_bf[l][:], in1=bc(U_f, [128, 8, 128], 1),
                                                               op=ALU.mult)))(l), reads=PARAMS, writes=PARAMS)
            P.op("act", (lambda l: (lambda e: e.activation(out=clru[l][:, 0:8], in_=pv[l][:, PV_LAM:PV_LAM + 8], func=AF.Exp,
                                                           scale=-1.0)))(l), reads=PARAMS, writes=PARAMS)
            P.op("act", (lambda l: (lambda e: e.activation(out=clru[l][:, 0:8], in_=clru[l][:, 0:8], func=AF.Ln,
                                                           bias=cst[:, 1:2], scale=1.0)))(l), reads=PARAMS, writes=PARAMS)
            P.op("dve", (lambda l: (lambda e: e.tensor_scalar(out=clru[l][:, 8:16], in0=clru[l][:, 0:8], scalar1=-16.0,
                                                              scalar2=None, op0=ALU.mult)))(l), reads=PARAMS, writes=PARAMS)
            P.op("dve", (lambda l: (lambda e: e.tensor_scalar(out=clru[l][:, 0:8], in0=clru[l][:, 0:8], scalar1=-8.0,
                                                              scalar2=None, op0=ALU.mult)))(l), reads=PARAMS, writes=PARAMS)
            P.op("act", (lambda l: (lambda e: e.activation(out=arow[l][:], in_=pr[l][:, PR_ALOG:PR_ALOG + 16],
                                                           func=AF.Exp)))(l), reads=PARAMS, writes=PARAMS)
            P.op("dve", (lambda l: (lambda e: e.tensor_scalar(out=arow[l][:], in0=arow[l][:], scalar1=-1.0, scalar2=None,
                                                              op0=ALU.mult)))(l), reads=PARAMS, writes=PARAMS)
            P.op("dve", (lambda l: (lambda e: e.tensor_tensor(out=Dg[l][:], in0=bc(consts[:, 0, :], [128, 16, 128], 1),
                                                               in1=bc(pr[l][:, PR_D:PR_D + 16], [128, 16, 128], 2),
                                                               op=ALU.mult)))(l), reads=PARAMS, writes=PARAMS)
            bkA = bank(2)
            bkB = bank(2)
            for g in range(8):
                P.op("pe", (lambda l, g, bk: (lambda e: e.matmul(bk.ap[:, g * 128:(g + 1) * 128], lhsT=ones_bf[:],
                                                                 rhs=gw_bf[l][:, g, :], start=True, stop=True)))(l, g, bkA),
                     reads=PARAMS, writes=bkA.keys)
            for hf in range(2):
                P.op("pe", (lambda l, hf, bk: (lambda e: e.matmul(bk.ap[:, hf * 512:(hf + 1) * 512], lhsT=ones_f[0:1, :],
                                                                  rhs=bsrow[0:1, l * 1024 + hf * 512:l * 1024 + (hf + 1) * 512],
                                                                  start=True, stop=True)))(l, hf, bkB),
                     reads=PARAMS, writes=bkB.keys)
            P.op("act", (lambda l, bk: (lambda e: e.activation(out=Eg[l][:], in_=bk.ap.rearrange("p (a b) -> p a b", a=8),
                                                               func=AF.Copy)))(l, bkB), reads=bkB.keys + PARAMS, writes=PARAMS)
            for g in range(8):
                P.op("dve", (lambda l, g, bk: (lambda e: e.scalar_tensor_tensor(
                    out=Eg[l][:, g, :], in0=bk.ap[:, g * 128:(g + 1) * 128], scalar=pv[l][:, PV_LNB + g:PV_LNB + g + 1],
                    in1=Eg[l][:, g, :], op0=ALU.mult, op1=ALU.add)))(l, g, bkA), reads=bkA.keys + PARAMS, writes=PARAMS)

        def rmsnorm(l, gcol):
            bk = bank()
            for c in range(8):
                act(sqb[c % 2], h[c], AF.Square)
                mm(bk, B(ones_bf[:], PARAMS), sqb[c % 2], start=(c == 0), stop=(c == 7))
            act(lnt, bk, AF.Ln, bias=eps_b, scale=1.0 / D)
            act(rstd, lnt, AF.Exp, scale=-0.5)
            for c in range(8):
                stt(hn[c], h[c], pvc(l, gcol + c), rstd, ALU.mult, ALU.mult)

        def conv4(dst, xbuf, wcol, bcol, l, acc):
            ts(acc, xbuf.s(xbuf.ap[:, 0:T]), pvc(l, wcol + 0), ALU.mult, pvc(l, bcol), ALU.add)
            for k in (1, 2):
                stt(acc, xbuf.s(xbuf.ap[:, k:k + T]), pvc(l, wcol + k), acc, ALU.mult, ALU.add)
            stt(dst, xbuf.s(xbuf.ap[:, 3:3 + T]), pvc(l, wcol + 3), acc, ALU.mult, ALU.add)

        import os
        STOP = int(os.environ.get("MK_STOP", "99"))

        def layer(l, ti):
            win3 = w3(w_in, l)
            if STOP < 1:
                return
            rmsnorm(l, PV_NMIX)
            if STOP < 2:
                return
            AR.reset()
            vtok = [AR.f32(1024) for _ in range(NJ)]
            vn = [AR.bf16(1024) for _ in range(NJ)]
            tmpA = AR.f32(T)
            junk = AR.bf16(1024)
            for hf in range(2):
                wt = wload(win3, 0, C_V + hf * 512)
                for j in range(NJ):
                    bk = bank()
                    for kc in range(8):
                        mm(bk, hn[kc].s(hn[kc].ap[:, j * 128:(j + 1) * 128]), wl(wt, kc, 0, 512), kc == 0, kc == 7)
                    act(vtok[j].s(vtok[j].ap[:, hf * 512:(hf + 1) * 512]), bk, AF.Gelu_apprx_tanh)
            for j in range(NJ):
                s1 = sm(1)
                s2 = sm(1)
                mean = sm(1)
                var = sm(1)
                msq = sm(1)
                rs = sm(1)
                memset(s2, 0.0)
                red_sum(s1, vtok[j])
                act(junk, vtok[j], AF.Square, accum=s2)
                ts(mean, s1, 1.0 / 1024, ALU.mult)
                tt(msq, mean, mean, ALU.mult)
                stt(var, s2, 1.0 / 1024, msq, ALU.mult, ALU.subtract)
                act(var, var, AF.Ln, bias=eps_b, scale=1.0)
                act(rs, var, AF.Exp, scale=-0.5)
                ts(vn[j], vtok[j], mean, ALU.subtract, rs, ALU.mult)
            sm_p[0] = 0
            for hf in range(2):
                wt = wload(win3, 0, C_U + hf * 512)
                for m4 in range(4):
                    m = hf * 4 + m4
                    bk = bank()
                    for kc in range(8):
                        mm(bk, wl(wt, kc, m4 * 128, 128), hn[kc], kc == 0, kc == 7)
                    act(uT[m], bk, AF.Gelu_apprx_tanh)
            for g in range(8):
                bk = bank()
                for j in range(NJ):
                    mm(bk.s(bk.ap[:, j * 128:(j + 1) * 128]), vn[j].s(vn[j].ap[:, g * 128:(g + 1) * 128]),
                       B(gw_bf[l][:, g, :], PARAMS))
                stt(tmpA.s(tmpA.ap.rearrange("p (a b) -> p a b", a=NJ)), bk.s(bk.ap.rearrange("p (a b) -> p a b", a=NJ)),
                    pvc(l, PV_LNG + g), B(bc(Eg[l][:, g, :], [128, NJ, 128], 1), PARAMS), ALU.mult, ALU.add)
                tt(ya[g], tmpA, uT[g], ALU.mult)
            if STOP < 3:
                return
            AR.reset()
            hseq4 = AR.f32(4 * T, shape=(4, T))
            xbuf = AR.f32(T + 4)
            acc = AR.f32(T)
            xc = AR.f32(T)
            xc_bf = AR.bf16(T)
            r_ = AR.f32(T)
            i_ = AR.f32(T)
            a_ = AR.f32(T)
            q_ = AR.f32(T)
            inp = AR.f32(T)
            gg = AR.f32(T)
            for hf in range(2):
                wt = wload(win3, 0, C_XB + hf * 512)
                for m4 in range(4):
                    hd = hf * 4 + m4
                    bk = bank()
                    for kc in range(8):
                        mm(bk, wl(wt, kc, m4 * 128, 128), hn[kc], kc == 0, kc == 7)
                    hl = B(hist_l[l][:, hd, 0:3], [("hl", l, hd)])
                    cp(xbuf.s(xbuf.ap[:, 0:3]), hl)
                    cp(xbuf.s(xbuf.ap[:, 3:3 + T]), bk, eng="act")
                    cp(hl, xbuf.s(xbuf.ap[:, T:T + 3]))
                    conv4(xc, xbuf, PV_LCW + hd * 4, PV_LCB + hd, l, acc)
                    cp(xc_bf, xc, eng="act")
                    bkr = bank()
                    mm(bkr, B(wr_bf[l][:, hd, :], PARAMS), xc_bf)
                    bki = bank()
                    mm(bki, B(wi_bf[l][:, hd, :], PARAMS), xc_bf)
                    act(r_, bkr, AF.Sigmoid, bias=pvc(l, PV_BR + hd), scale=1.0)
                    act(i_, bki, AF.Sigmoid, bias=pvc(l, PV_BI + hd), scale=1.0)
                    act(a_, r_, AF.Exp, scale=B(clru[l][:, hd:hd + 1], PARAMS))
                    act(q_, r_, AF.Exp, scale=B(clru[l][:, 8 + hd:9 + hd], PARAMS))
                    act(q_, q_, AF.Ln, bias=one_b, scale=-1.0)
                    act(q_, q_, AF.Exp, scale=0.5)
                    tt(inp, i_, xc, ALU.mult)
                    tt(inp, inp, q_, ALU.mult)
                    hq = hseq4.s(hseq4.ap[:, m4, :], hseq4.keys[m4 * 4:(m4 + 1) * 4])
                    hs = B(hst[l][:, hd:hd + 1], [("hst", l, hd)])
                    scan(hq, a_, inp, hs)
                    cp(hs, hq.s(hq.ap[:, T - 1:T]))
                wt = wload(win3, 0, C_GL + hf * 512)
                for m4 in range(4):
                    hd = hf * 4 + m4
                    bk = bank()
                    for kc in range(8):
                        mm(bk, wl(wt, kc, m4 * 128, 128), hn[kc], kc == 0, kc == 7)
                    act(gg, bk, AF.Gelu_apprx_tanh)
                    hq = hseq4.s(hseq4.ap[:, m4, :], hseq4.keys[m4 * 4:(m4 + 1) * 4])
                    tt(yb[hd], gg, hq, ALU.mult)
            if STOP < 4:
                return
            AR.reset()
            xsT = mg
            BT = AR.bf16(4 * T, shape=(4, T))
            CT = AR.bf16(4 * T, shape=(4, T))
            xbuf = AR.f32(T + 4)
            acc = AR.f32(T)
            xcv = AR.f32(T)
            arena_mark = AR.p
            for wi_, (c0, nch) in enumerate(((C_X, 4), (C_X + 512, 4), (C_B, 4), (C_C, 4))):
                wt = wload(win3, 0, c0)
                for m4 in range(4):
                    ch = wi_ * 4 + m4
                    bk = bank()
                    for kc in range(8):
                        mm(bk, wl(wt, kc, m4 * 128, 128), hn[kc], kc == 0, kc == 7)
                    hl = B(hist_s[l][:, ch, 0:3], [("hs", l, ch)])
                    cp(xbuf.s(xbuf.ap[:, 0:3]), hl)
                    cp(xbuf.s(xbuf.ap[:, 3:3 + T]), bk, eng="act")
                    cp(hl, xbuf.s(xbuf.ap[:, T:T + 3]))
                    conv4(xcv, xbuf, PV_SCW + ch * 4, PV_SCB + ch, l, acc)
                    if ch < 8:
                        dst = xsT[ch]
                    elif ch < 12:
                        dst = BT.s(BT.ap[:, ch - 8, :])
                    else:
                        dst = CT.s(CT.ap[:, ch - 12, :])
                    act(dst, xcv, AF.Silu)
            wz = [wload(win3, 0, C_Z + hf * 512) for hf in range(2)]
            for j in range(NJ):
                js = slice(j * 128, (j + 1) * 128)
                AR.p = arena_mark
                sm_p[0] = 0
                Lb = AR.f32(16 * 128, shape=(16, 128))
                Lq = [Lb.s(Lb.ap[:, q * 4:(q + 1) * 4, :], Lb.keys[q * 4:(q + 1) * 4]) for q in range(4)]
                cbm = AR.f32(4 * 128, shape=(4, 128))
                Mb = AR.bf16(16 * 128, shape=(16, 128))
                xs_bf = AR.bf16(1024)
                xdt = AR.bf16(1024)
                xdd = AR.bf16(1024)
                Btok = AR.bf16(512)
                ytmp = AR.f32(1024)
                sz = AR.f32(1024)
                yn = AR.bf16(1024)
                dtx, ax, ex, lgx, dt_, adt, cs_sb, ecs, tmc, ds, cd, dtds = [sm(16) for _ in range(12)]
                ss = sm(4)
                rs4 = sm(4)
                pss = bank()
                for kc in range(8):
                    mm(pss.s(pss.ap[:, 0:16]), hn[kc].s(hn[kc].ap[:, js]), B(wdt_bf[l][:, kc, :], PARAMS), kc == 0, kc == 7)
                tt(dtx, pss.s(pss.ap[:, 0:16]), B(pr[l][:, PR_DTB:PR_DTB + 16], PARAMS), ALU.add)
                ts(ax, dtx, -1.0, ALU.mult)
                tt(ax, ax, dtx, ALU.max)
                act(ex, ax, AF.Exp, scale=-1.0)
                act(lgx, ex, AF.Ln, bias=one_b, scale=1.0)
                ts(dt_, dtx, 0.0, ALU.max)
                tt(dt_, dt_, lgx, ALU.add)
                tt(adt, dt_, B(arow[l][:], PARAMS), ALU.mult)
                mm(pss.s(pss.ap[:, 16:32]), B(U_f, PARAMS), adt)
                mm(pss.s(pss.ap[:, 32:48]), B(ones_f[:], PARAMS), adt)
                cp(cs_sb, pss.s(pss.ap[:, 16:32]))
                act(ecs, cs_sb, AF.Exp)
                tt(tmc, pss.s(pss.ap[:, 32:48]), cs_sb, ALU.subtract)
                act(ds, tmc, AF.Exp)
                act(cd, pss.s(pss.ap[:, 32:48]), AF.Exp)
                tt(dtds, dt_, ds, ALU.mult)
                tt(Lb, B(bc(Ls_f, [128, 16, 128], 1), PARAMS), adt.s(bc(adt.ap, [128, 16, 128], 2)), ALU.mult)
                psc = bank()
                for g in range(4):
                    mm(psc.s(psc.ap[:, g * 128:(g + 1) * 128]), BT.s(BT.ap[:, g, js]), CT.s(CT.ap[:, g, js]))
                tt(cbm, psc.s(psc.ap.rearrange("p (a b) -> p a b", a=4)), B(bc(U_f, [128, 4, 128], 1), PARAMS), ALU.mult)
                for q in range(4):
                    psg = bank()
                    for r in range(4):
                        hd = q * 4 + r
                        mm(psg.s(psg.ap[:, r * 128:(r + 1) * 128]), Lq[q].s(Lb.ap[:, hd, :]), B(U_f, PARAMS))
                    act(Lq[q], psg.s(psg.ap.rearrange("p (a b) -> p a b", a=4)), AF.Exp)
                    tt(Mb.s(Mb.ap[:, q * 4:(q + 1) * 4, :]), Lq[q], cbm.s(bc(cbm.ap[:, q, :], [128, 4, 128], 1)), ALU.mult)
                pst = bank()
                pstb = pst.s(pst.ap.bitcast(BF16))
                for kc in range(8):
                    tr(pstb.s(pstb.ap[:, kc * 128:(kc + 1) * 128]), xsT[kc].s(xsT[kc].ap[:, js]))
                cp(xs_bf, pstb, eng="act")
                v3 = lambda b_: b_.s(b_.ap.rearrange("p (a b) -> p a b", a=16))
                tt(v3(xdt), v3(pstb), dt_.s(bc(dt_.ap, [128, 16, 64], 2)), ALU.mult)
                tt(v3(xdd), v3(pstb), dtds.s(bc(dtds.ap, [128, 16, 64], 2)), ALU.mult)
                psb = bank()
                psbb = psb.s(psb.ap.bitcast(BF16))
                for g in range(4):
                    tr(psbb.s(psbb.ap[:, g * 128:(g + 1) * 128]), BT.s(BT.ap[:, g, js]))
                cp(Btok, psbb.s(psbb.ap[:, 0:512]), eng="act")
                yo = bank(2)
                for g in range(4):
                    mm(yo.s(yo.ap[:, g * 256:(g + 1) * 256]), CT.s(CT.ap[:, g, js]), B(S_bf[l][:, g * 256:(g + 1) * 256], [("Sbf", l)]))
                yd = bank(2)
                for hd in range(16):
                    o_ = yd.s(yd.ap[:, hd * 64:(hd + 1) * 64])
                    mm(o_, Mb.s(Mb.ap[:, hd, :]), xdt.s(xdt.ap[:, hd * 64:(hd + 1) * 64]), True, False)
                    mm(o_, B(Dg[l][:, hd, :], PARAMS), xs_bf.s(xs_bf.ap[:, hd * 64:(hd + 1) * 64]), False, True)
                st = bank(2)
                for g in range(4):
                    mm(st.s(st.ap[:, g * 256:(g + 1) * 256]), Btok.s(Btok.ap[:, g * 128:(g + 1) * 128]),
                       xdd.s(xdd.ap[:, g * 256:(g + 1) * 256]))
                tt(v3(ytmp), v3(yo), ecs.s(bc(ecs.ap, [128, 16, 64], 2)), ALU.mult)
                tt(ytmp, yd, ytmp, ALU.add)
                Sb = B(S[l][:], [("S", l)])
                tt(v3(Sb), v3(Sb), cd.s(bc(cd.ap, [128, 16, 64], 2)), ALU.mult)
                tt(Sb, st, Sb, ALU.add)
                cp(B(S_bf[l][:], [("Sbf", l)]), Sb, eng="act")
                pz = bank(2)
                for hf in range(2):
                    for kc in range(8):
                        mm(pz.s(pz.ap[:, hf * 512:(hf + 1) * 512]), hn[kc].s(hn[kc].ap[:, js]), wl(wz[hf], kc, 0, 512),
                           kc == 0, kc == 7)
                act(sz, pz, AF.Silu)
                tt(ytmp, ytmp, sz, ALU.mult)
                memset(ss, 0.0)
                for g in range(4):
                    act(sz.s(sz.ap[:, g * 256:(g + 1) * 256]), ytmp.s(ytmp.ap[:, g * 256:(g + 1) * 256]), AF.Square,
                        accum=ss.s(ss.ap[:, g:g + 1]))
                act(rs4, ss, AF.Ln, bias=eps_b, scale=1.0 / 256)
                act(rs4, rs4, AF.Exp, scale=-0.5)
                v4 = lambda b_: b_.s(b_.ap.rearrange("p (a b) -> p a b", a=4))
                tt(v4(yn), v4(ytmp), rs4.s(bc(rs4.ap, [128, 4, 256], 2)), ALU.mult)
                pyt = bank()
                pytb = pyt.s(pyt.ap.bitcast(BF16))
                for kc in range(8):
                    tr(pytb.s(pytb.ap[:, kc * 128:(kc + 1) * 128]), yn.s(yn.ap[:, kc * 128:(kc + 1) * 128]))
                for kc in range(8):
                    ts(yc[kc].s(yc[kc].ap[:, js]), pytb.s(pytb.ap[:, kc * 128:(kc + 1) * 128]), pvc(l, PV_SNG + kc), ALU.mult)
            if STOP < 5:
                return
            AR.reset()
            sm_p[0] = 0
            g4 = AR.f32(4 * T, shape=(4, T))
            tmpG = AR.f32(T)
            accm = uT
            for k, (wb, ybr) in enumerate(((w_ba, ya), (w_bb, yb), (w_bc, yc))):
                wb3 = w3(wb, l)
                for hf in range(2):
                    wt = wload(win3, 0, C_G + k * 1024 + hf * 512)
                    for m4 in range(4):
                        m = hf * 4 + m4
                        bk = bank()
                        for kc in range(8):
                            mm(bk, wl(wt, kc, m4 * 128, 128), hn[kc], kc == 0, kc == 7)
                        act(g4.s(g4.ap[:, m4, :], g4.keys[m4 * 4:(m4 + 1) * 4]), bk, AF.Sigmoid, bias=pvc(l, PV_BG + k * 8 + m), scale=1.0)
                    wt = wload(wb3, 0, hf * 512)
                    for m4 in range(4):
                        m = hf * 4 + m4
                        bk = bank()
                        for kc in range(8):
                            mm(bk, wl(wt, kc, m4 * 128, 128), ybr[kc], kc == 0, kc == 7)
                        gm = g4.s(g4.ap[:, m4, :], g4.keys[m4 * 4:(m4 + 1) * 4])
                        if k == 0:
                            tt(accm[m], bk, gm, ALU.mult)
                        elif k == 1:
                            tt(tmpG, bk, gm, ALU.mult)
                            tt(accm[m], accm[m], tmpG, ALU.add)
                        else:
                            tt(tmpG, bk, gm, ALU.mult)
                            tt(mg[m], accm[m], tmpG, ALU.add)
            if STOP < 6:
                return
            wo3 = w3(w_out, l)
            for hf in range(2):
                wt = wload(wo3, 0, hf * 512)
                for m4 in range(4):
                    m = hf * 4 + m4
                    bk = bank()
                    for kc in range(8):
                        mm(bk, wl(wt, kc, m4 * 128, 128), mg[kc], kc == 0, kc == 7)
                    tt(h[m], bk, h[m], ALU.add)
            if STOP < 7:
                return
            rmsnorm(l, PV_NMLP)
            AR.reset()
            rl = [AR.f32(T) for _ in range(2)]
            wu3 = w3(w_up, l)
            for f4 in range(8):
                wt = wload(wu3, 0, f4 * 512)
                for m4 in range(4):
                    f = f4 * 4 + m4
                    bk = bank()
                    for kc in range(8):
                        mm(bk, wl(wt, kc, m4 * 128, 128), hn[kc], kc == 0, kc == 7)
                    act(rl[f % 2], bk, AF.Relu)
                    tt(up[f], rl[f % 2], rl[f % 2], ALU.mult)
            wd3 = w3(w_dn, l)
            for hf in range(2):
                bks = [bank() for _ in range(4)]
                for q in range(4):
                    wt = wload(wd3, q * 8, hf * 512)
                    for kc in range(8):
                        for m4 in range(4):
                            mm(bks[m4], wl(wt, kc, m4 * 128, 128), up[q * 8 + kc], (q == 0 and kc == 0), (q == 3 and kc == 7))
                for m4 in range(4):
                    m = hf * 4 + m4
                    tt(h[m], bks[m4], h[m], ALU.add)

        finals = []
        xT3 = xT.rearrange("(c p) t -> p c t", p=128)
        oT3 = outT.rearrange("(c p) t -> p c t", p=128)
        for ti in range(n_tiles):
            t0 = ti * T
            for c2 in range(2):
                P.dma("sp", (lambda c2, t0: (lambda e: e.dma_start(out=h_t[:, c2 * 4:(c2 + 1) * 4, :],
                                                                   in_=xT3[:, c2 * 4:(c2 + 1) * 4, t0:t0 + T])))(c2, t0),
                      "ld_x%d" % c2, writes=[("h", c) for c in range(c2 * 4, c2 * 4 + 4)])
            for l in range(DEPTH):
                layer(l, ti)
                if debug_taps and ti == 0:
                    tap("h_l%d" % l, B(h_t[:], [("h", c) for c in range(8)]), [128, 8, T])
            bk = bank()
            for c in range(8):
                act(sqb[c % 2], h[c], AF.Square)
                mm(bk, B(ones_bf[:], PARAMS), sqb[c % 2], start=(c == 0), stop=(c == 7))
            act(lnt, bk, AF.Ln, bias=eps_b, scale=1.0 / D)
            act(rstd, lnt, AF.Exp, scale=-0.5)
            for c in range(8):
                stt(uT[c], h[c], pvc(0, PV_FIN + c), rstd, ALU.mult, ALU.mult)
            for c2 in range(2):
                finals.append(P.dma("sp", (lambda c2, t0: (lambda e: e.dma_start(out=oT3[:, c2 * 4:(c2 + 1) * 4, t0:t0 + T],
                                                                                 in_=uT_t[:, c2 * 4:(c2 + 1) * 4, :])))(c2, t0),
                                    "st_o%d" % c2, reads=[("uT", c) for c in range(c2 * 4, c2 * 4 + 4)]))
        finals.extend(taps.values())
        P.emit(final_waits=finals)
    return nc


def _vec8(v):
    return np.ascontiguousarray(v.reshape(-1, 128).T)


def prep_params(inp):
    f = lambda k: np.asarray(inp[k], dtype=np.float32)
    pvec = np.zeros((DEPTH, 128, NPV), np.float32)
    prow = np.zeros((DEPTH, 128, NPR), np.float32)
    for l in range(DEPTH):
        pvec[l, :, PV_NMIX:PV_NMIX + 8] = _vec8(f("norm_mix_g")[l])
        pvec[l, :, PV_NMLP:PV_NMLP + 8] = _vec8(f("norm_mlp_g")[l])
        pvec[l, :, PV_LNG:PV_LNG + 8] = _vec8(f("gmlp_ln_g")[l])
        pvec[l, :, PV_LNB:PV_LNB + 8] = _vec8(f("gmlp_ln_b")[l])
        cw = f("lru_conv_w")[l]
        pvec[l, :, PV_LCW:PV_LCW + 32] = cw.reshape(4, 8, 128).transpose(2, 1, 0).reshape(128, 32)
        pvec[l, :, PV_LCB:PV_LCB + 8] = _vec8(f("lru_conv_b")[l])
        pvec[l, :, PV_BR:PV_BR + 8] = _vec8(f("lru_b_r")[l])
        pvec[l, :, PV_BI:PV_BI + 8] = _vec8(f("lru_b_i")[l])
        pvec[l, :, PV_LAM:PV_LAM + 8] = _vec8(f("lru_lambda")[l])
        sw = f("ssd_conv_w")[l]
        pvec[l, :, PV_SCW:PV_SCW + 64] = sw.reshape(4, 16, 128).transpose(2, 1, 0).reshape(128, 64)
        pvec[l, :, PV_SCB:PV_SCB + 16] = _vec8(f("ssd_conv_b")[l])
        pvec[l, :, PV_BG:PV_BG + 24] = _vec8(f("b_gate")[l].reshape(-1))
        pvec[l, :, PV_SNG:PV_SNG + 8] = _vec8(f("ssd_norm_g")[l])
        pvec[l, :, PV_FIN:PV_FIN + 8] = _vec8(f("final_norm_g"))
        prow[l, :, PR_DTB:PR_DTB + 16] = f("ssd_dt_bias")[l][None, :]
        prow[l, :, PR_ALOG:PR_ALOG + 16] = f("ssd_a_log")[l][None, :]
        prow[l, :, PR_D:PR_D + 16] = f("ssd_d")[l][None, :]
    bsrow = np.ascontiguousarray(f("gmlp_b_s").reshape(DEPTH, 1, 1024))
    gwT = np.ascontiguousarray(f("gmlp_w_s").transpose(0, 3, 1, 2))
    wr = np.ascontiguousarray(f("lru_w_r").transpose(0, 2, 1, 3))
    wi = np.ascontiguousarray(f("lru_w_i").transpose(0, 2, 1, 3))
    consts = np.zeros((128, 3, 128), np.float32)
    consts[:, 0, :] = np.eye(128, dtype=np.float32)
    consts[:, 1, :] = np.triu(np.ones((128, 128), np.float32))
    consts[:, 2, :] = np.tril(np.ones((128, 128), np.float32), -1)
    return dict(pvec=pvec, prow=prow, bsrow=bsrow, gwT=gwT, wr=wr, wi=wi, consts=consts)


_CACHE = {}


def kernel(**inputs):
    x = np.asarray(inputs["x"], dtype=np.float32)
    shared = prep_params(inputs)
    for k in ("w_in", "w_branch_a", "w_branch_b", "w_branch_c", "w_out", "w_mlp_up", "w_mlp_down"):
        shared[k] = np.ascontiguousarray(np.asarray(inputs[k], dtype=np.float32))
    n_tiles = SEQ // T
    if "nc" not in _CACHE:
        _CACHE["nc"] = build_program(n_tiles)
    nc = _CACHE["nc"]
    in_maps = []
    for core in range(N_CORES):
        b = core % BATCH
        m = dict(shared)
        m["xT"] = np.ascontiguousarray(x[b].T)
        in_maps.append(m)
    res = run_bass_kernel_spmd(nc, in_maps, core_ids=list(range(N_CORES)))
    out = np.empty((BATCH, SEQ, D), np.float32)
    for b in range(BATCH):
        out[b] = res.results[b]["outT"].T
    return out
```

```python
from contextlib import ExitStack

import numpy as np
import concourse.bass as bass
import concourse.mybir as mybir
from concourse.bass_utils import run_bass_kernel_spmd

F32 = mybir.dt.float32
BF16 = mybir.dt.bfloat16
AF = mybir.ActivationFunctionType
ALU = mybir.AluOpType
AX = mybir.AxisListType

D = 1024
SEQ = 4096
BATCH = 4
DEPTH = 2
D_IN = 10256
T = 512
NJ = T // 128
EPS = 1e-6
N_CORES = 8

PV_NMIX, PV_NMLP, PV_LNG, PV_LNB, PV_LCW, PV_LCB, PV_BR, PV_BI, PV_LAM = 0, 8, 16, 24, 32, 64, 72, 80, 88
PV_SCW, PV_SCB, PV_BG, PV_SNG, PV_FIN, NPV = 96, 160, 176, 200, 208, 216
PR_DTB, PR_ALOG, PR_D, NPR = 0, 16, 32, 48
C_U, C_V, C_XB, C_GL, C_Z, C_X, C_B, C_C, C_DT, C_G = 0, 1024, 2048, 3072, 4096, 5120, 6144, 6656, 7168, 7184


class _Op:
    __slots__ = ("eng", "fn", "deps", "signal", "count", "dma_sem", "dma_count")

    def __init__(self, eng, fn, deps):
        self.eng = eng
        self.fn = fn
        self.deps = deps
        self.signal = False
        self.count = None
        self.dma_sem = None
        self.dma_count = None


class Prog:
    ENGS = ("pe", "act", "dve", "pool", "sp")

    def __init__(self, nc):
        self.nc = nc
        self.ops = {e: [] for e in self.ENGS}
        self.last_writer = {}
        self.readers = {}
        self.dma_sems = {}

    def _deps(self, reads, writes):
        deps = []
        for k in reads:
            w = self.last_writer.get(k)
            if w is not None:
                deps.append(w)
        for k in writes:
            w = self.last_writer.get(k)
            if w is not None:
                deps.append(w)
            deps.extend(self.readers.get(k, {}).values())
        return deps

    def _commit(self, op, reads, writes):
        rk = op.eng if op.dma_sem is None else ("dma", id(op))
        for k in reads:
            self.readers.setdefault(k, {})[rk] = op
        for k in writes:
            self.last_writer[k] = op
            self.readers[k] = {}

    def op(self, eng, fn, reads=(), writes=()):
        psr = [k for k in reads if isinstance(k, tuple) and k[0] == "ps"]
        if psr:
            writes = list(writes) + [k for k in psr if k not in writes]
        deps = self._deps(reads, writes)
        o = _Op(eng, fn, deps)
        for d in deps:
            d.signal = True
        self.ops[eng].append(o)
        self._commit(o, reads, writes)
        return o

    def dma(self, eng, fn, sem, reads=(), writes=()):
        deps = self._deps(reads, writes)
        o = _Op(eng, fn, deps)
        for d in deps:
            d.signal = True
        c = self.dma_sems.get(sem, 0) + 16
        self.dma_sems[sem] = c
        o.dma_sem = sem
        o.dma_count = c
        self.ops[eng].append(o)
        self._commit(o, reads, writes)
        return o

    def emit(self, final_waits=()):
        nc = self.nc
        with ExitStack() as es:
            esem = {e: es.enter_context(nc.semaphore("s_" + e)) for e in self.ENGS}
            dsem = {n: es.enter_context(nc.semaphore("d_" + n)) for n in self.dma_sems}
            for o in final_waits:
                o.signal = True
            for e in self.ENGS:
                c = 0
                for o in self.ops[e]:
                    if o.dma_sem is None and o.signal:
                        c += 1
                        o.count = c
            block = es.enter_context(nc.Block())

            def run(e, engine):
                waited = {}
                for o in self.ops[e]:
                    need = {}
                    for d in o.deps:
                        if d.dma_sem is not None:
                            key = ("d", d.dma_sem)
                            val = d.dma_count
                        else:
                            if d.eng == e and e in ("pe", "sp"):
                                continue
                            key = ("e", d.eng)
                            val = d.count
                        if val > need.get(key, 0):
                            need[key] = val
                    for key, val in need.items():
                        if waited.get(key, 0) >= val:
                            continue
                        waited[key] = val
                        s = dsem[key[1]] if key[0] == "d" else esem[key[1]]
                        engine.wait_ge(s, val)
                    ins = o.fn(engine)
                    if o.dma_sem is not None:
                        ins.then_inc(dsem[o.dma_sem], 16)
                    elif o.signal:
                        ins.then_inc(esem[e], 1)
                if e == "sp":
                    for o in final_waits:
                        if o.dma_sem is not None:
                            engine.wait_ge(dsem[o.dma_sem], o.dma_count)
                        else:
                            engine.wait_ge(esem[o.eng], o.count)

            @block.tensor
            def _(eng):
                run("pe", eng)

            @block.scalar
            def _(eng):
                run("act", eng)

            @block.vector
            def _(eng):
                run("dve", eng)

            @block.gpsimd
            def _(eng):
                run("pool", eng)

            @block.sync
            def _(eng):
                run("sp", eng)


class B:
    __slots__ = ("ap", "keys")

    def __init__(self, ap, keys):
        self.ap = ap
        self.keys = list(keys)

    def s(self, ap, keys=None):
        return B(ap, self.keys if keys is None else keys)


def bc(ap, shape, axis):
    return ap.unsqueeze(axis).to_broadcast(list(shape))


def build_program(n_tiles, debug_taps=False):
    nc = bass.Bass("TRN2", target_bir_lowering=False)
    P = Prog(nc)

    def din(name, shape):
        return nc.dram_tensor(name, list(shape), F32, kind="ExternalInput").ap()

    xT = din("xT", [D, SEQ])
    w_in = din("w_in", [DEPTH, D, D_IN])
    w_ba = din("w_branch_a", [DEPTH, D, D])
    w_bb = din("w_branch_b", [DEPTH, D, D])
    w_bc = din("w_branch_c", [DEPTH, D, D])
    w_out = din("w_out", [DEPTH, D, D])
    w_up = din("w_mlp_up", [DEPTH, D, 4 * D])
    w_dn = din("w_mlp_down", [DEPTH, 4 * D, D])
    pvec_d = din("pvec", [DEPTH, 128, NPV])
    prow_d = din("prow", [DEPTH, 128, NPR])
    bsrow_d = din("bsrow", [DEPTH, 1, 1024])
    gwT_d = din("gwT", [DEPTH, 128, 8, 128])
    wr_d = din("wr", [DEPTH, 128, 8, 128])
    wi_d = din("wi", [DEPTH, 128, 8, 128])
    consts_d = din("consts", [128, 3, 128])
    outT = nc.dram_tensor("outT", [D, SEQ], F32, kind="ExternalOutput").ap()
    taps = {}

    with ExitStack() as es:
        def sb(name, shape, dt):
            return es.enter_context(nc.sbuf_tensor("sb_" + name, list(shape), dt))

        consts = sb("consts", [128, 3, 128], F32)
        ident_bf = sb("ident_bf", [128, 128], BF16)
        U_f = consts[:, 1, :]
        Ls_f = consts[:, 2, :]
        ones_bf = sb("ones_bf", [128, 128], BF16)
        ones_f = sb("ones_f", [128, 128], F32)
        cst = sb("cst", [128, 4], F32)
        pv = [sb("pv%d" % l, [128, NPV], F32) for l in range(DEPTH)]
        pr = [sb("pr%d" % l, [128, NPR], F32) for l in range(DEPTH)]
        gw_bf = [sb("gw%d" % l, [128, 8, 128], BF16) for l in range(DEPTH)]
        wr_bf = [sb("wr%d" % l, [128, 8, 128], BF16) for l in range(DEPTH)]
        wi_bf = [sb("wi%d" % l, [128, 8, 128], BF16) for l in range(DEPTH)]
        wdt_bf = [sb("wdt%d" % l, [128, 8, 16], BF16) for l in range(DEPTH)]
        Eg = [sb("Eg%d" % l, [128, 8, 128], F32) for l in range(DEPTH)]
        Dg = [sb("Dg%d" % l, [128, 16, 128], BF16) for l in range(DEPTH)]
        clru = [sb("clru%d" % l, [128, 16], F32) for l in range(DEPTH)]
        arow = [sb("arow%d" % l, [128, 16], F32) for l in range(DEPTH)]
        S = [sb("S%d" % l, [128, 1024], F32) for l in range(DEPTH)]
        S_bf = [sb("Sbf%d" % l, [128, 1024], BF16) for l in range(DEPTH)]
        hst = [sb("hst%d" % l, [128, 8], F32) for l in range(DEPTH)]
        hist_l = [sb("histl%d" % l, [128, 8, 4], F32) for l in range(DEPTH)]
        hist_s = [sb("hists%d" % l, [128, 16, 4], F32) for l in range(DEPTH)]
        h_t = sb("h", [128, 8, T], F32)
        hn_t = sb("hn", [128, 8, T], BF16)
        sqb_t = sb("sqb", [128, 2, T], BF16)
        rstd_t = sb("rstd", [128, T], F32)
        lnt_t = sb("lnt", [128, T], F32)
        uT_t = sb("uT", [128, 8, T], F32)
        up_t = sb("up", [128, 32, T], BF16)
        NW = 3
        wring = [sb("wr_ring%d" % i, [128, 8, 512], BF16) for i in range(NW)]
        ARENA_N = 12288
        arena = sb("arena", [128, ARENA_N], F32)
        small = sb("small", [128, 1024], F32)
        ps = es.enter_context(nc.psum_tensor("ps", [128, 4096], F32))
        bsrow = arena

        class Arena:
            def __init__(self):
                self.p = 0

            def reset(self):
                self.p = 0

            def f32(self, n, shape=None):
                a, b = self.p, self.p + n
                assert b <= ARENA_N, "arena overflow"
                self.p = (b + 127) // 128 * 128
                ap = arena[:, a:b]
                if shape is not None:
                    ap = ap.rearrange("p (a b) -> p a b", a=shape[0])
                return B(ap, [("A", g) for g in range(a // 128, (b + 127) // 128)])

            def bf16(self, n, shape=None):
                w = (n + 1) // 2
                a, b = self.p, self.p + w
                assert b <= ARENA_N, "arena overflow"
                self.p = (b + 127) // 128 * 128
                ap = arena[:, a:b].bitcast(BF16)
                if shape is not None:
                    ap = ap.rearrange("p (a b) -> p a b", a=shape[0])
                return B(ap, [("A", g) for g in range(a // 128, (b + 127) // 128)])

        AR = Arena()
        sm_p = [0]

        def sm(n):
            a = sm_p[0]
            sm_p[0] += n
            assert sm_p[0] <= 1024
            return B(small[:, a:a + n], [("sm", w) for w in range(a // 8, (a + n - 1) // 8 + 1)])

        bank_p = [0]

        def bank(n=1):
            p = bank_p[0]
            if p % n:
                p += n - p % n
            if p + n > 8:
                p = 0
            bank_p[0] = (p + n) % 8
            return B(ps[:, p * 512:(p + n) * 512], [("ps", p + i) for i in range(n)])

        h = [B(h_t[:, c, :], [("h", c)]) for c in range(8)]
        hn = [B(hn_t[:, c, :], [("hn", c)]) for c in range(8)]
        sqb = [B(sqb_t[:, i, :], [("sqb", i)]) for i in range(2)]
        rstd = B(rstd_t[:], ["rstd"])
        lnt = B(lnt_t[:], ["lnt"])
        uT = [B(uT_t[:, c, :], [("uT", c)]) for c in range(8)]
        up = [B(up_t[:, c, :], [("up", c)]) for c in range(32)]
        ya, yb, yc, mg = up[0:8], up[8:16], up[16:24], up[24:32]
        PARAMS = ["params"]

        def pvc(l, col, n=1):
            return B(pv[l][:, col:col + n], PARAMS)

        def mm(out, lhsT, rhs, start=True, stop=True):
            P.op("pe", lambda e: e.matmul(out.ap, lhsT=lhsT.ap, rhs=rhs.ap, start=start, stop=stop),
                 reads=lhsT.keys + rhs.keys, writes=out.keys)

        def tr(out, in_):
            P.op("pe", lambda e: e.transpose(out=out.ap, in_=in_.ap, identity=ident_bf[:]),
                 reads=in_.keys + PARAMS, writes=out.keys)

        def act(out, in_, func, bias=None, scale=None, accum=None):
            rd = list(in_.keys)
            wr = list(out.keys)
            kw = {}
            if bias is not None:
                if isinstance(bias, B):
                    rd += bias.keys
                    kw["bias"] = bias.ap
                else:
                    kw["bias"] = bias
            if scale is not None:
                if isinstance(scale, B):
                    rd += scale.keys
                    kw["scale"] = scale.ap
                else:
                    kw["scale"] = scale
            if accum is not None:
                wr += accum.keys
                kw["accum_out"] = accum.ap
            P.op("act", lambda e: e.activation(out=out.ap, in_=in_.ap, func=func, **kw), reads=rd, writes=wr)

        def tt(out, in0, in1, op, eng="dve"):
            P.op(eng, lambda e: e.tensor_tensor(out=out.ap, in0=in0.ap, in1=in1.ap, op=op),
                 reads=in0.keys + in1.keys, writes=out.keys)

        def ts(out, in0, s1, op0, s2=None, op1=None, eng="dve"):
            rd = list(in0.keys)
            a1 = s1
            a2 = s2
            if isinstance(s1, B):
                rd += s1.keys
                a1 = s1.ap
            if isinstance(s2, B):
                rd += s2.keys
                a2 = s2.ap
            if op1 is None:
                P.op(eng, lambda e: e.tensor_scalar(out=out.ap, in0=in0.ap, scalar1=a1, scalar2=None, op0=op0),
                     reads=rd, writes=out.keys)
            else:
                P.op(eng, lambda e: e.tensor_scalar(out=out.ap, in0=in0.ap, scalar1=a1, scalar2=a2, op0=op0, op1=op1),
                     reads=rd, writes=out.keys)

        def stt(out, in0, scalar, in1, op0, op1, eng="dve"):
            rd = in0.keys + in1.keys
            a = scalar
            if isinstance(scalar, B):
                rd = rd + scalar.keys
                a = scalar.ap
            P.op(eng, lambda e: e.scalar_tensor_tensor(out=out.ap, in0=in0.ap, scalar=a, in1=in1.ap, op0=op0, op1=op1),
                 reads=rd, writes=out.keys)

        def cp(out, in_, eng="dve"):
            if eng == "act":
                P.op("act", lambda e: e.activation(out=out.ap, in_=in_.ap, func=AF.Copy), reads=in_.keys, writes=out.keys)
            else:
                P.op(eng, lambda e: e.tensor_copy(out=out.ap, in_=in_.ap), reads=in_.keys, writes=out.keys)

        def memset(buf, val, eng="dve"):
            P.op(eng, lambda e: e.memset(buf.ap, val), writes=buf.keys)

        def red_sum(out, in_):
            P.op("dve", lambda e: e.tensor_reduce(out=out.ap, in_=in_.ap, axis=AX.X, op=ALU.add),
                 reads=in_.keys, writes=out.keys)

        def scan(out, d0, d1, init):
            P.op("dve", lambda e: e.tensor_tensor_scan(out=out.ap, data0=d0.ap, data1=d1.ap, initial=init.ap,
                                                       op0=ALU.mult, op1=ALU.add),
                 reads=d0.keys + d1.keys + init.keys, writes=out.keys)

        def tap(name, buf, shape):
            if not debug_taps:
                return
            t = nc.dram_tensor("tap_" + name, list(shape), buf.ap.dtype, kind="ExternalOutput").ap()
            taps[name] = P.dma("sp", lambda e: e.dma_start(out=t, in_=buf.ap), "tap_" + name, reads=buf.keys)

        wcnt = [0]

        def wload(src3, kc0, c0, n=512):
            i = wcnt[0] % NW
            wcnt[0] += 1
            slot = wring[i]
            keys = [("w", i, 0), ("w", i, 1)]
            for hf in range(2):
                a, b = hf * 4, hf * 4 + 4
                dst = slot[:, a:b, 0:n]
                src = src3[:, kc0 + a:kc0 + b, c0:c0 + n]
                P.dma("pool", (lambda dst, src: (lambda e: e.dma_start(out=dst, in_=src)))(dst, src),
                      "w%d_%d" % (i, hf), writes=[keys[hf]])
            return slot, keys

        def wl(slot_keys, kc, c0, n):
            slot, keys = slot_keys
            return B(slot[:, kc, c0:c0 + n], [keys[kc // 4]])

        def w3(w, l):
            return w[l].rearrange("(kc p) e -> p kc e", p=128)

        P.dma("sp", lambda e: e.dma_start(out=consts[:], in_=consts_d), "ld_c", writes=PARAMS)
        for l in range(DEPTH):
            P.dma("sp", (lambda l: (lambda e: e.dma_start(out=pv[l][:], in_=pvec_d[l])))(l), "ld_pv%d" % l, writes=PARAMS)
            P.dma("sp", (lambda l: (lambda e: e.dma_start(out=pr[l][:], in_=prow_d[l])))(l), "ld_pr%d" % l, writes=PARAMS)
            P.dma("sp", (lambda l: (lambda e: e.dma_start(out=bsrow[0:1, l * 1024:(l + 1) * 1024], in_=bsrow_d[l])))(l),
                  "ld_bs%d" % l, writes=PARAMS)
            P.dma("pool", (lambda l: (lambda e: e.dma_start(out=gw_bf[l][:], in_=gwT_d[l])))(l), "ld_gw%d" % l, writes=PARAMS)
            P.dma("pool", (lambda l: (lambda e: e.dma_start(out=wr_bf[l][:], in_=wr_d[l])))(l), "ld_wr%d" % l, writes=PARAMS)
            P.dma("pool", (lambda l: (lambda e: e.dma_start(out=wi_bf[l][:], in_=wi_d[l])))(l), "ld_wi%d" % l, writes=PARAMS)
            P.dma("pool", (lambda l: (lambda e: e.dma_start(out=wdt_bf[l][:], in_=w3(w_in, l)[:, :, C_DT:C_DT + 16])))(l),
                  "ld_wdt%d" % l, writes=PARAMS)
        PB = B(None, PARAMS)
        P.op("dve", lambda e: e.memset(ones_bf[:], 1.0), reads=PARAMS, writes=PARAMS)
        P.op("dve", lambda e: e.memset(ones_f[:], 1.0), reads=PARAMS, writes=PARAMS)
        P.op("dve", lambda e: e.memset(cst[:, 0:1], EPS), reads=PARAMS, writes=PARAMS)
        P.op("dve", lambda e: e.memset(cst[:, 1:2], 1.0), reads=PARAMS, writes=PARAMS)
        P.op("dve", lambda e: e.tensor_copy(out=ident_bf[:], in_=consts[:, 0, :]), reads=PARAMS, writes=PARAMS)
        eps_b = B(cst[:, 0:1], PARAMS)
        one_b = B(cst[:, 1:2], PARAMS)
        for l in range(DEPTH):
            for t_, n_ in ((S[l], 1024), (hst[l], 8)):
                P.op("dve", (lambda t_: (lambda e: e.memset(t_[:], 0.0)))(t_), reads=PARAMS, writes=PARAMS)
            P.op("dve", (lambda l: (lambda e: e.memset(S_bf[l][:], 0.0)))(l), reads=PARAMS, writes=PARAMS)
            P.op("dve", (lambda l: (lambda e: e.memset(hist_l[l][:], 0.0)))(l), reads=PARAMS, writes=PARAMS)
            P.op("dve", (lambda l: (lambda e: e.memset(hist_s[l][:], 0.0)))(l), reads=PARAMS, writes=PARAMS)
            P.op("dve", (lambda l: (lambda e: e.tensor_tensor(out=gw_bf[l][:], in0=gw_bf[l][:], in1=bc(U_f, [128, 8, 128], 1),
                                                               op=ALU.mult)))(l), reads=PARAMS, writes=PARAMS)
            P.op("act", (lambda l: (lambda e: e.activation(out=clru[l][:, 0:8], in_=pv[l][:, PV_LAM:PV_LAM + 8], func=AF.Exp,
                                                           scale=-1.0)))(l), reads=PARAMS, writes=PARAMS)
            P.op("act", (lambda l: (lambda e: e.activation(out=clru[l][:, 0:8], in_=clru[l][:, 0:8], func=AF.Ln,
                                                           bias=cst[:, 1:2], scale=1.0)))(l), reads=PARAMS, writes=PARAMS)
            P.op("dve", (lambda l: (lambda e: e.tensor_scalar(out=clru[l][:, 8:16], in0=clru[l][:, 0:8], scalar1=-16.0,
                                                              scalar2=None, op0=ALU.mult)))(l), reads=PARAMS, writes=PARAMS)
            P.op("dve", (lambda l: (lambda e: e.tensor_scalar(out=clru[l][:, 0:8], in0=clru[l][:, 0:8], scalar1=-8.0,
                                                              scalar2=None, op0=ALU.mult)))(l), reads=PARAMS, writes=PARAMS)
            P.op("act", (lambda l: (lambda e: e.activation(out=arow[l][:], in_=pr[l][:, PR_ALOG:PR_ALOG + 16],
                                                           func=AF.Exp)))(l), reads=PARAMS, writes=PARAMS)
            P.op("dve", (lambda l: (lambda e: e.tensor_scalar(out=arow[l][:], in0=arow[l][:], scalar1=-1.0, scalar2=None,
                                                              op0=ALU.mult)))(l), reads=PARAMS, writes=PARAMS)
            P.op("dve", (lambda l: (lambda e: e.tensor_tensor(out=Dg[l][:], in0=bc(consts[:, 0, :], [128, 16, 128], 1),
                                                               in1=bc(pr[l][:, PR_D:PR_D + 16], [128, 16, 128], 2),
                                                               op=ALU.mult)))(l), reads=PARAMS, writes=PARAMS)
            bkA = bank(2)
            bkB = bank(2)
            for g in range(8):
                P.op("pe", (lambda l, g, bk: (lambda e: e.matmul(bk.ap[:, g * 128:(g + 1) * 128], lhsT=ones_bf[:],
                                                                 rhs=gw_bf[l][:, g, :], start=True, stop=True)))(l, g, bkA),
                     reads=PARAMS, writes=bkA.keys)
            for hf in range(2):
                P.op("pe", (lambda l, hf, bk: (lambda e: e.matmul(bk.ap[:, hf * 512:(hf + 1) * 512], lhsT=ones_f[0:1, :],
                                                                  rhs=bsrow[0:1, l * 1024 + hf * 512:l * 1024 + (hf + 1) * 512],
                                                                  start=True, stop=True)))(l, hf, bkB),
                     reads=PARAMS, writes=bkB.keys)
            P.op("act", (lambda l, bk: (lambda e: e.activation(out=Eg[l][:], in_=bk.ap.rearrange("p (a b) -> p a b", a=8),
                                                               func=AF.Copy)))(l, bkB), reads=bkB.keys + PARAMS, writes=PARAMS)
            for g in range(8):
                P.op("dve", (lambda l, g, bk: (lambda e: e.scalar_tensor_tensor(
                    out=Eg[l][:, g, :], in0=bk.ap[:, g * 128:(g + 1) * 128], scalar=pv[l][:, PV_LNB + g:PV_LNB + g + 1],
                    in1=Eg[l][:, g, :], op0=ALU.mult, op1=ALU.add)))(l, g, bkA), reads=bkA.keys + PARAMS, writes=PARAMS)

        def rmsnorm(l, gcol):
            bk = bank()
            for c in range(8):
                act(sqb[c % 2], h[c], AF.Square)
                mm(bk, B(ones_bf[:], PARAMS), sqb[c % 2], start=(c == 0), stop=(c == 7))
            act(lnt, bk, AF.Ln, bias=eps_b, scale=1.0 / D)
            act(rstd, lnt, AF.Exp, scale=-0.5)
            for c in range(8):
                stt(hn[c], h[c], pvc(l, gcol + c), rstd, ALU.mult, ALU.mult)

        def conv4(dst, xbuf, wcol, bcol, l, acc):
            ts(acc, xbuf.s(xbuf.ap[:, 0:T]), pvc(l, wcol + 0), ALU.mult, pvc(l, bcol), ALU.add)
            for k in (1, 2):
                stt(acc, xbuf.s(xbuf.ap[:, k:k + T]), pvc(l, wcol + k), acc, ALU.mult, ALU.add)
            stt(dst, xbuf.s(xbuf.ap[:, 3:3 + T]), pvc(l, wcol + 3), acc, ALU.mult, ALU.add)

        import os
        STOP = int(os.environ.get("MK_STOP", "99"))

        def mark(name):
            PHASES.append((name, len(P.ops["pe"])))

        def layer(l, ti):
            win3 = w3(w_in, l)
            mark("norm1")
            if STOP < 1:
                return
            rmsnorm(l, PV_NMIX)
            if STOP < 2:
                return
            mark('A')
            AR.reset()
            vtok = [AR.f32(1024) for _ in range(NJ)]
            vn = [AR.bf16(1024) for _ in range(NJ)]
            tmpA = AR.f32(T)
            junk = AR.bf16(1024)
            for hf in range(2):
                wt = wload(win3, 0, C_V + hf * 512)
                for j in range(NJ):
                    bk = bank()
                    for kc in range(8):
                        mm(bk, hn[kc].s(hn[kc].ap[:, j * 128:(j + 1) * 128]), wl(wt, kc, 0, 512), kc == 0, kc == 7)
                    act(vtok[j].s(vtok[j].ap[:, hf * 512:(hf + 1) * 512]), bk, AF.Gelu_apprx_tanh)
            for j in range(NJ):
                s1 = sm(1)
                s2 = sm(1)
                mean = sm(1)
                var = sm(1)
                msq = sm(1)
                rs = sm(1)
                memset(s2, 0.0)
                red_sum(s1, vtok[j])
                act(junk, vtok[j], AF.Square, accum=s2)
                ts(mean, s1, 1.0 / 1024, ALU.mult)
                tt(msq, mean, mean, ALU.mult)
                stt(var, s2, 1.0 / 1024, msq, ALU.mult, ALU.subtract)
                act(var, var, AF.Ln, bias=eps_b, scale=1.0)
                act(rs, var, AF.Exp, scale=-0.5)
                ts(vn[j], vtok[j], mean, ALU.subtract, rs, ALU.mult)
            sm_p[0] = 0
            for hf in range(2):
                wt = wload(win3, 0, C_U + hf * 512)
                for m4 in range(4):
                    m = hf * 4 + m4
                    bk = bank()
                    for kc in range(8):
                        mm(bk, wl(wt, kc, m4 * 128, 128), hn[kc], kc == 0, kc == 7)
                    act(uT[m], bk, AF.Gelu_apprx_tanh)
            for g in range(8):
                bk = bank()
                for j in range(NJ):
                    mm(bk.s(bk.ap[:, j * 128:(j + 1) * 128]), vn[j].s(vn[j].ap[:, g * 128:(g + 1) * 128]),
                       B(gw_bf[l][:, g, :], PARAMS))
                stt(tmpA.s(tmpA.ap.rearrange("p (a b) -> p a b", a=NJ)), bk.s(bk.ap.rearrange("p (a b) -> p a b", a=NJ)),
                    pvc(l, PV_LNG + g), B(bc(Eg[l][:, g, :], [128, NJ, 128], 1), PARAMS), ALU.mult, ALU.add)
                tt(ya[g], tmpA, uT[g], ALU.mult)
            if STOP < 3:
                return
            mark('B')
            AR.reset()
            hseq4 = AR.f32(4 * T, shape=(4, T))
            xbufs = [AR.f32(T + 4) for _ in range(2)]
            accs = [AR.f32(T) for _ in range(2)]
            xcs = [AR.f32(T) for _ in range(4)]
            xc_bfs = [AR.bf16(T) for _ in range(2)]
            r4 = [AR.f32(T) for _ in range(4)]
            i2 = [AR.f32(T) for _ in range(2)]
            a_ = AR.f32(T)
            q_ = AR.f32(T)
            gg = AR.f32(T)
            for hf in range(2):
                wt = wload(win3, 0, C_XB + hf * 512)
                bks = []
                for m4 in range(4):
                    bk = bank()
                    for kc in range(8):
                        mm(bk, wl(wt, kc, m4 * 128, 128), hn[kc], kc == 0, kc == 7)
                    bks.append(bk)

                def stA(m4):
                    hd = hf * 4 + m4
                    xbuf, acc = xbufs[m4 % 2], accs[m4 % 2]
                    hl = B(hist_l[l][:, hd, 0:3], [("hl", l, hd)])
                    cp(xbuf.s(xbuf.ap[:, 0:3]), hl)
                    cp(hl, bks[m4].s(bks[m4].ap[:, T - 3:T]))
                    cp(xbuf.s(xbuf.ap[:, 3:3 + T]), bks[m4], eng="act")
                    act(acc, bks[m4], AF.Identity, bias=pvc(l, PV_LCB + hd), scale=pvc(l, PV_LCW + hd * 4 + 3))

                def stB(m4):
                    hd = hf * 4 + m4
                    xbuf, acc, xc = xbufs[m4 % 2], accs[m4 % 2], xcs[m4]
                    wcol = PV_LCW + hd * 4
                    stt(acc, xbuf.s(xbuf.ap[:, 0:T]), pvc(l, wcol + 0), acc, ALU.mult, ALU.add)
                    stt(acc, xbuf.s(xbuf.ap[:, 1:1 + T]), pvc(l, wcol + 1), acc, ALU.mult, ALU.add)
                    stt(xc, xbuf.s(xbuf.ap[:, 2:2 + T]), pvc(l, wcol + 2), acc, ALU.mult, ALU.add)
                    cp(xc_bfs[m4 % 2], xc, eng="act")
                    bkr = bank()
                    mm(bkr, B(wr_bf[l][:, hd, :], PARAMS), xc_bfs[m4 % 2])
                    bki = bank()
                    mm(bki, B(wi_bf[l][:, hd, :], PARAMS), xc_bfs[m4 % 2])
                    return bkr, bki

                ri = {}
                stA(0)
                for m4 in range(4):
                    if m4 + 1 < 4:
                        stA(m4 + 1)
                    ri[m4] = stB(m4)
                for m4 in range(4):
                    hd = hf * 4 + m4
                    bkr, bki = ri[m4]
                    act(r4[m4], bkr, AF.Sigmoid, bias=pvc(l, PV_BR + hd), scale=1.0)
                    act(i2[m4 % 2], bki, AF.Sigmoid, bias=pvc(l, PV_BI + hd), scale=1.0)
                    tt(xcs[m4], xcs[m4], i2[m4 % 2], ALU.mult)
                for m4 in range(4):
                    hd = hf * 4 + m4
                    act(a_, r4[m4], AF.Exp, scale=B(clru[l][:, hd:hd + 1], PARAMS))
                    act(q_, r4[m4], AF.Exp, scale=B(clru[l][:, 8 + hd:9 + hd], PARAMS))
                    act(q_, q_, AF.Ln, bias=one_b, scale=-1.0)
                    act(q_, q_, AF.Exp, scale=0.5)
                    tt(xcs[m4], xcs[m4], q_, ALU.mult)
                    hq = hseq4.s(hseq4.ap[:, m4, :], hseq4.keys[m4 * 4:(m4 + 1) * 4])
                    hs = B(hst[l][:, hd:hd + 1], [("hst", l, hd)])
                    scan(hq, a_, xcs[m4], hs)
                    cp(hs, hq.s(hq.ap[:, T - 1:T]))
                wt = wload(win3, 0, C_GL + hf * 512)
                for m4 in range(4):
                    hd = hf * 4 + m4
                    bk = bank()
                    for kc in range(8):
                        mm(bk, wl(wt, kc, m4 * 128, 128), hn[kc], kc == 0, kc == 7)
                    act(gg, bk, AF.Gelu_apprx_tanh)
                    hq = hseq4.s(hseq4.ap[:, m4, :], hseq4.keys[m4 * 4:(m4 + 1) * 4])
                    tt(yb[hd], gg, hq, ALU.mult)
            if STOP < 4:
                return
            mark('Cconv')
            AR.reset()
            xsT = mg
            BT = AR.bf16(4 * T, shape=(4, T))
            CT = AR.bf16(4 * T, shape=(4, T))
            arena_mark = AR.p
            xbufs = [AR.f32(T + 4) for _ in range(2)]
            accs = [AR.f32(T) for _ in range(2)]
            xcvs = [AR.f32(T) for _ in range(2)]
            for wi_, c0 in enumerate((C_X, C_X + 512, C_B, C_C)):
                wt = wload(win3, 0, c0)
                bks = []
                for m4 in range(4):
                    bk = bank()
                    for kc in range(8):
                        mm(bk, wl(wt, kc, m4 * 128, 128), hn[kc], kc == 0, kc == 7)
                    bks.append(bk)

                def stA(m4):
                    ch = wi_ * 4 + m4
                    xbuf, acc = xbufs[ch % 2], accs[ch % 2]
                    hl = B(hist_s[l][:, ch, 0:3], [("hs", l, ch)])
                    cp(xbuf.s(xbuf.ap[:, 0:3]), hl)
                    cp(hl, bks[m4].s(bks[m4].ap[:, T - 3:T]))
                    cp(xbuf.s(xbuf.ap[:, 3:3 + T]), bks[m4], eng="act")
                    act(acc, bks[m4], AF.Identity, bias=pvc(l, PV_SCB + ch), scale=pvc(l, PV_SCW + ch * 4 + 3))

                def stB(m4):
                    ch = wi_ * 4 + m4
                    xbuf, acc, xcv = xbufs[ch % 2], accs[ch % 2], xcvs[ch % 2]
                    wcol = PV_SCW + ch * 4
                    stt(acc, xbuf.s(xbuf.ap[:, 0:T]), pvc(l, wcol + 0), acc, ALU.mult, ALU.add)
                    stt(acc, xbuf.s(xbuf.ap[:, 1:1 + T]), pvc(l, wcol + 1), acc, ALU.mult, ALU.add)
                    stt(xcv, xbuf.s(xbuf.ap[:, 2:2 + T]), pvc(l, wcol + 2), acc, ALU.mult, ALU.add)
                    if ch < 8:
                        dst = xsT[ch]
                    elif ch < 12:
                        dst = BT.s(BT.ap[:, ch - 8, :])
                    else:
                        dst = CT.s(CT.ap[:, ch - 12, :])
                    act(dst, xcv, AF.Silu)

                stA(0)
                for m4 in range(4):
                    if m4 + 1 < 4:
                        stA(m4 + 1)
                    stB(m4)
            mark('Cz')
            szall = [B(uT_t[:, 2 * j:2 * j + 2, :].rearrange("p a b -> p (a b)"), [("uT", 2 * j), ("uT", 2 * j + 1)])
                     for j in range(NJ)]
            for hf in range(2):
                wt = wload(win3, 0, C_Z + hf * 512)
                for j in range(NJ):
                    bk = bank()
                    for kc in range(8):
                        mm(bk, hn[kc].s(hn[kc].ap[:, j * 128:(j + 1) * 128]), wl(wt, kc, 0, 512), kc == 0, kc == 7)
                    act(szall[j].s(szall[j].ap[:, hf * 512:(hf + 1) * 512]), bk, AF.Silu)
            mark('Cdt')
            sm_p[0] = 0
            NH4 = NJ * 16
            dtx, ax, ex, lgx, dt_, adt, cs_sb, ecs, tmc, ds, cd, dtds = [sm(NH4) for _ in range(12)]
            psd = bank()
            for j in range(NJ):
                for kc in range(8):
                    mm(psd.s(psd.ap[:, j * 16:(j + 1) * 16]), hn[kc].s(hn[kc].ap[:, j * 128:(j + 1) * 128]),
                       B(wdt_bf[l][:, kc, :], PARAMS), kc == 0, kc == 7)
            v4j = lambda b_: b_.s(b_.ap.rearrange("p (a b) -> p a b", a=NJ))
            tt(v4j(dtx), psd.s(psd.ap[:, 0:NH4].rearrange("p (a b) -> p a b", a=NJ)),
               B(bc(pr[l][:, PR_DTB:PR_DTB + 16], [128, NJ, 16], 1), PARAMS), ALU.add)
            ts(ax, dtx, -1.0, ALU.mult)
            tt(ax, ax, dtx, ALU.max)
            act(ex, ax, AF.Exp, scale=-1.0)
            act(lgx, ex, AF.Ln, bias=one_b, scale=1.0)
            ts(dt_, dtx, 0.0, ALU.max)
            tt(dt_, dt_, lgx, ALU.add)
            tt(v4j(adt), v4j(dt_), B(bc(arow[l][:], [128, NJ, 16], 1), PARAMS), ALU.mult)
            for j in range(NJ):
                mm(psd.s(psd.ap[:, NH4 + j * 16:NH4 + (j + 1) * 16]), B(U_f, PARAMS), adt.s(adt.ap[:, j * 16:(j + 1) * 16]))
                mm(psd.s(psd.ap[:, 2 * NH4 + j * 16:2 * NH4 + (j + 1) * 16]), B(ones_f[:], PARAMS),
                   adt.s(adt.ap[:, j * 16:(j + 1) * 16]))
            cp(cs_sb, psd.s(psd.ap[:, NH4:2 * NH4]))
            act(ecs, cs_sb, AF.Exp)
            tt(tmc, psd.s(psd.ap[:, 2 * NH4:3 * NH4]), cs_sb, ALU.subtract)
            act(ds, tmc, AF.Exp)
            act(cd, psd.s(psd.ap[:, 2 * NH4:3 * NH4]), AF.Exp)
            tt(dtds, dt_, ds, ALU.mult)
            AR.p = arena_mark
            Lb = AR.f32(16 * 128, shape=(16, 128))
            Lq = [Lb.s(Lb.ap[:, q * 4:(q + 1) * 4, :], Lb.keys[q * 4:(q + 1) * 4]) for q in range(4)]
            cbm = AR.f32(4 * 128, shape=(4, 128))
            FB = []
            for _ in range(2):
                FB.append(dict(Mb=AR.bf16(16 * 128, shape=(16, 128)), xs_bf=AR.bf16(1024), xdt=AR.bf16(1024),
                               xdd=AR.bf16(1024), Btok=AR.bf16(512)))
            ytmp = AR.f32(1024)
            yn = AR.bf16(1024)
            ss = sm(4)
            rs4 = sm(4)
            v3 = lambda b_: b_.s(b_.ap.rearrange("p (a b) -> p a b", a=16))
            v4 = lambda b_: b_.s(b_.ap.rearrange("p (a b) -> p a b", a=4))

            def hs(b_, j):
                return b_.s(b_.ap[:, j * 16:(j + 1) * 16])

            def front(j):
                js = slice(j * 128, (j + 1) * 128)
                f = FB[j % 2]
                adt_j, dt_j, dtds_j = hs(adt, j), hs(dt_, j), hs(dtds, j)
                tt(Lb, B(bc(Ls_f, [128, 16, 128], 1), PARAMS), adt_j.s(bc(adt_j.ap, [128, 16, 128], 2)), ALU.mult)
                psc = bank()
                for g in range(4):
                    mm(psc.s(psc.ap[:, g * 128:(g + 1) * 128]), BT.s(BT.ap[:, g, js]), CT.s(CT.ap[:, g, js]))
                tt(cbm, psc.s(psc.ap.rearrange("p (a b) -> p a b", a=4)), B(bc(U_f, [128, 4, 128], 1), PARAMS), ALU.mult)
                for q in range(4):
                    psg = bank()
                    for r in range(4):
                        hd = q * 4 + r
                        mm(psg.s(psg.ap[:, r * 128:(r + 1) * 128]), Lq[q].s(Lb.ap[:, hd, :]), B(U_f, PARAMS))
                    act(Lq[q], psg.s(psg.ap.rearrange("p (a b) -> p a b", a=4)), AF.Exp)
                    tt(f["Mb"].s(f["Mb"].ap[:, q * 4:(q + 1) * 4, :]), Lq[q], cbm.s(bc(cbm.ap[:, q, :], [128, 4, 128], 1)),
                       ALU.mult)
                pst = bank()
                pstb = pst.s(pst.ap.bitcast(BF16))
                for kc in range(8):
                    tr(pstb.s(pstb.ap[:, kc * 128:(kc + 1) * 128]), xsT[kc].s(xsT[kc].ap[:, js]))
                cp(f["xs_bf"], pstb, eng="act")
                tt(v3(f["xdt"]), v3(pstb), dt_j.s(bc(dt_j.ap, [128, 16, 64], 2)), ALU.mult)
                tt(v3(f["xdd"]), v3(pstb), dtds_j.s(bc(dtds_j.ap, [128, 16, 64], 2)), ALU.mult)
                psb = bank()
                psbb = psb.s(psb.ap.bitcast(BF16))
                for g in range(4):
                    tr(psbb.s(psbb.ap[:, g * 128:(g + 1) * 128]), BT.s(BT.ap[:, g, js]))
                cp(f["Btok"], psbb.s(psbb.ap[:, 0:512]), eng="act")

            def back(j):
                js = slice(j * 128, (j + 1) * 128)
                f = FB[j % 2]
                Mb, xs_bf, xdt, xdd, Btok = f["Mb"], f["xs_bf"], f["xdt"], f["xdd"], f["Btok"]
                ecs_j, cd_j = hs(ecs, j), hs(cd, j)
                yo = bank(2)
                for g in range(4):
                    mm(yo.s(yo.ap[:, g * 256:(g + 1) * 256]), CT.s(CT.ap[:, g, js]),
                       B(S_bf[l][:, g * 256:(g + 1) * 256], [("Sbf", l)]))
                yd = bank(2)
                for hd in range(16):
                    o_ = yd.s(yd.ap[:, hd * 64:(hd + 1) * 64])
                    mm(o_, Mb.s(Mb.ap[:, hd, :]), xdt.s(xdt.ap[:, hd * 64:(hd + 1) * 64]), True, False)
                    mm(o_, B(Dg[l][:, hd, :], PARAMS), xs_bf.s(xs_bf.ap[:, hd * 64:(hd + 1) * 64]), False, True)
                st = bank(2)
                for g in range(4):
                    mm(st.s(st.ap[:, g * 256:(g + 1) * 256]), Btok.s(Btok.ap[:, g * 128:(g + 1) * 128]),
                       xdd.s(xdd.ap[:, g * 256:(g + 1) * 256]))
                tt(v3(ytmp), v3(yo), ecs_j.s(bc(ecs_j.ap, [128, 16, 64], 2)), ALU.mult)
                tt(ytmp, yd, ytmp, ALU.add)
                Sb = B(S[l][:], [("S", l)])
                tt(v3(Sb), v3(Sb), cd_j.s(bc(cd_j.ap, [128, 16, 64], 2)), ALU.mult)
                tt(Sb, st, Sb, ALU.add)
                cp(B(S_bf[l][:], [("Sbf", l)]), Sb, eng="act")
                tt(ytmp, ytmp, szall[j], ALU.mult)
                memset(ss, 0.0)
                for g in range(4):
                    act(yn.s(yn.ap[:, g * 256:(g + 1) * 256]), ytmp.s(ytmp.ap[:, g * 256:(g + 1) * 256]), AF.Square,
                        accum=ss.s(ss.ap[:, g:g + 1]))
                act(rs4, ss, AF.Ln, bias=eps_b, scale=1.0 / 256)
                act(rs4, rs4, AF.Exp, scale=-0.5)
                tt(v4(yn), v4(ytmp), rs4.s(bc(rs4.ap, [128, 4, 256], 2)), ALU.mult)
                pyt = bank()
                pytb = pyt.s(pyt.ap.bitcast(BF16))
                for kc in range(8):
                    tr(pytb.s(pytb.ap[:, kc * 128:(kc + 1) * 128]), yn.s(yn.ap[:, kc * 128:(kc + 1) * 128]))
                for kc in range(8):
                    ts(yc[kc].s(yc[kc].ap[:, js]), pytb.s(pytb.ap[:, kc * 128:(kc + 1) * 128]), pvc(l, PV_SNG + kc), ALU.mult)

            mark('Cchunks')
            front(0)
            for j in range(NJ):
                if j + 1 < NJ:
                    front(j + 1)
                back(j)
            if STOP < 5:
                return
            mark('merge')
            AR.reset()
            sm_p[0] = 0
            g4 = AR.f32(4 * T, shape=(4, T))
            tmpG = AR.f32(T)
            accm = uT
            for k, (wb, ybr) in enumerate(((w_ba, ya), (w_bb, yb), (w_bc, yc))):
                wb3 = w3(wb, l)
                for hf in range(2):
                    wt = wload(win3, 0, C_G + k * 1024 + hf * 512)
                    for m4 in range(4):
                        m = hf * 4 + m4
                        bk = bank()
                        for kc in range(8):
                            mm(bk, wl(wt, kc, m4 * 128, 128), hn[kc], kc == 0, kc == 7)
                        act(g4.s(g4.ap[:, m4, :], g4.keys[m4 * 4:(m4 + 1) * 4]), bk, AF.Sigmoid, bias=pvc(l, PV_BG + k * 8 + m), scale=1.0)
                    wt = wload(wb3, 0, hf * 512)
                    for m4 in range(4):
                        m = hf * 4 + m4
                        bk = bank()
                        for kc in range(8):
                            mm(bk, wl(wt, kc, m4 * 128, 128), ybr[kc], kc == 0, kc == 7)
                        gm = g4.s(g4.ap[:, m4, :], g4.keys[m4 * 4:(m4 + 1) * 4])
                        if k == 0:
                            tt(accm[m], bk, gm, ALU.mult)
                        elif k == 1:
                            tt(tmpG, bk, gm, ALU.mult)
                            tt(accm[m], accm[m], tmpG, ALU.add)
                        else:
                            tt(tmpG, bk, gm, ALU.mult)
                            tt(mg[m], accm[m], tmpG, ALU.add)
            if STOP < 6:
                return
            mark('outproj')
            wo3 = w3(w_out, l)
            for hf in range(2):
                wt = wload(wo3, 0, hf * 512)
                for m4 in range(4):
                    m = hf * 4 + m4
                    bk = bank()
                    for kc in range(8):
                        mm(bk, wl(wt, kc, m4 * 128, 128), mg[kc], kc == 0, kc == 7)
                    tt(h[m], bk, h[m], ALU.add)
            if STOP < 7:
                return
            mark('mlp_up')
            rmsnorm(l, PV_NMLP)
            AR.reset()
            rl = [AR.f32(T) for _ in range(2)]
            wu3 = w3(w_up, l)
            for f4 in range(8):
                wt = wload(wu3, 0, f4 * 512)
                for m4 in range(4):
                    f = f4 * 4 + m4
                    bk = bank()
                    for kc in range(8):
                        mm(bk, wl(wt, kc, m4 * 128, 128), hn[kc], kc == 0, kc == 7)
                    act(rl[f % 2], bk, AF.Relu)
                    tt(up[f], rl[f % 2], rl[f % 2], ALU.mult)
            mark('mlp_down')
            wd3 = w3(w_dn, l)
            for hf in range(2):
                bks = [bank() for _ in range(4)]
                for q in range(4):
                    wt = wload(wd3, q * 8, hf * 512)
                    for kc in range(8):
                        for m4 in range(4):
                            mm(bks[m4], wl(wt, kc, m4 * 128, 128), up[q * 8 + kc], (q == 0 and kc == 0), (q == 3 and kc == 7))
                for m4 in range(4):
                    m = hf * 4 + m4
                    tt(h[m], bks[m4], h[m], ALU.add)

        finals = []
        xT3 = xT.rearrange("(c p) t -> p c t", p=128)
        oT3 = outT.rearrange("(c p) t -> p c t", p=128)
        for ti in range(n_tiles):
            t0 = ti * T
            for c2 in range(2):
                P.dma("sp", (lambda c2, t0: (lambda e: e.dma_start(out=h_t[:, c2 * 4:(c2 + 1) * 4, :],
                                                                   in_=xT3[:, c2 * 4:(c2 + 1) * 4, t0:t0 + T])))(c2, t0),
                      "ld_x%d" % c2, writes=[("h", c) for c in range(c2 * 4, c2 * 4 + 4)])
            for l in range(DEPTH):
                layer(l, ti)
                if debug_taps and ti == 0:
                    tap("h_l%d" % l, B(h_t[:], [("h", c) for c in range(8)]), [128, 8, T])
            mark('final')
            bk = bank()
            for c in range(8):
                act(sqb[c % 2], h[c], AF.Square)
                mm(bk, B(ones_bf[:], PARAMS), sqb[c % 2], start=(c == 0), stop=(c == 7))
            act(lnt, bk, AF.Ln, bias=eps_b, scale=1.0 / D)
            act(rstd, lnt, AF.Exp, scale=-0.5)
            for c in range(8):
                stt(uT[c], h[c], pvc(0, PV_FIN + c), rstd, ALU.mult, ALU.mult)
            for c2 in range(2):
                finals.append(P.dma("sp", (lambda c2, t0: (lambda e: e.dma_start(out=oT3[:, c2 * 4:(c2 + 1) * 4, t0:t0 + T],
                                                                                 in_=uT_t[:, c2 * 4:(c2 + 1) * 4, :])))(c2, t0),
                                    "st_o%d" % c2, reads=[("uT", c) for c in range(c2 * 4, c2 * 4 + 4)]))
        finals.extend(taps.values())
        P.emit(final_waits=finals)
    return nc


def _vec8(v):
    return np.ascontiguousarray(v.reshape(-1, 128).T)


def prep_params(inp):
    f = lambda k: np.asarray(inp[k], dtype=np.float32)
    pvec = np.zeros((DEPTH, 128, NPV), np.float32)
    prow = np.zeros((DEPTH, 128, NPR), np.float32)
    for l in range(DEPTH):
        pvec[l, :, PV_NMIX:PV_NMIX + 8] = _vec8(f("norm_mix_g")[l])
        pvec[l, :, PV_NMLP:PV_NMLP + 8] = _vec8(f("norm_mlp_g")[l])
        pvec[l, :, PV_LNG:PV_LNG + 8] = _vec8(f("gmlp_ln_g")[l])
        pvec[l, :, PV_LNB:PV_LNB + 8] = _vec8(f("gmlp_ln_b")[l])
        cw = f("lru_conv_w")[l]
        pvec[l, :, PV_LCW:PV_LCW + 32] = cw.reshape(4, 8, 128).transpose(2, 1, 0).reshape(128, 32)
        pvec[l, :, PV_LCB:PV_LCB + 8] = _vec8(f("lru_conv_b")[l])
        pvec[l, :, PV_BR:PV_BR + 8] = _vec8(f("lru_b_r")[l])
        pvec[l, :, PV_BI:PV_BI + 8] = _vec8(f("lru_b_i")[l])
        pvec[l, :, PV_LAM:PV_LAM + 8] = _vec8(f("lru_lambda")[l])
        sw = f("ssd_conv_w")[l]
        pvec[l, :, PV_SCW:PV_SCW + 64] = sw.reshape(4, 16, 128).transpose(2, 1, 0).reshape(128, 64)
        pvec[l, :, PV_SCB:PV_SCB + 16] = _vec8(f("ssd_conv_b")[l])
        pvec[l, :, PV_BG:PV_BG + 24] = _vec8(f("b_gate")[l].reshape(-1))
        pvec[l, :, PV_SNG:PV_SNG + 8] = _vec8(f("ssd_norm_g")[l])
        pvec[l, :, PV_FIN:PV_FIN + 8] = _vec8(f("final_norm_g"))
        prow[l, :, PR_DTB:PR_DTB + 16] = f("ssd_dt_bias")[l][None, :]
        prow[l, :, PR_ALOG:PR_ALOG + 16] = f("ssd_a_log")[l][None, :]
        prow[l, :, PR_D:PR_D + 16] = f("ssd_d")[l][None, :]
    bsrow = np.ascontiguousarray(f("gmlp_b_s").reshape(DEPTH, 1, 1024))
    gwT = np.ascontiguousarray(f("gmlp_w_s").transpose(0, 3, 1, 2))
    wr = np.ascontiguousarray(f("lru_w_r").transpose(0, 2, 1, 3))
    wi = np.ascontiguousarray(f("lru_w_i").transpose(0, 2, 1, 3))
    consts = np.zeros((128, 3, 128), np.float32)
    consts[:, 0, :] = np.eye(128, dtype=np.float32)
    consts[:, 1, :] = np.triu(np.ones((128, 128), np.float32))
    consts[:, 2, :] = np.tril(np.ones((128, 128), np.float32), -1)
    return dict(pvec=pvec, prow=prow, bsrow=bsrow, gwT=gwT, wr=wr, wi=wi, consts=consts)


_CACHE = {}
PHASES = []


def kernel(**inputs):
    x = np.asarray(inputs["x"], dtype=np.float32)
    shared = prep_params(inputs)
    for k in ("w_in", "w_branch_a", "w_branch_b", "w_branch_c", "w_out", "w_mlp_up", "w_mlp_down"):
        shared[k] = np.ascontiguousarray(np.asarray(inputs[k], dtype=np.float32))
    n_tiles = SEQ // T
    if "nc" not in _CACHE:
        _CACHE["nc"] = build_program(n_tiles)
    nc = _CACHE["nc"]
    in_maps = []
    for core in range(N_CORES):
        b = core % BATCH
        m = dict(shared)
        m["xT"] = np.ascontiguousarray(x[b].T)
        in_maps.append(m)
    res = run_bass_kernel_spmd(nc, in_maps, core_ids=list(range(N_CORES)))
    out = np.empty((BATCH, SEQ, D), np.float32)
    for b in range(BATCH):
        out[b] = res.results[b]["outT"].T
    return out
```

```python
from contextlib import ExitStack

import numpy as np
import concourse.bass as bass
import concourse.mybir as mybir
from concourse.bass_utils import run_bass_kernel_spmd

F32 = mybir.dt.float32
BF16 = mybir.dt.bfloat16
AF = mybir.ActivationFunctionType
ALU = mybir.AluOpType
AX = mybir.AxisListType

D = 1024
SEQ = 4096
BATCH = 4
DEPTH = 2
D_IN = 10256
T = 512
NJ = T // 128
EPS = 1e-6
N_CORES = 8

PV_NMIX, PV_NMLP, PV_LNG, PV_LNB, PV_LCW, PV_LCB, PV_BR, PV_BI, PV_LAM = 0, 8, 16, 24, 32, 64, 72, 80, 88
PV_SCW, PV_SCB, PV_BG, PV_SNG, PV_FIN, NPV = 96, 160, 176, 200, 208, 216
PR_DTB, PR_ALOG, PR_D, NPR = 0, 16, 32, 48
C_U, C_V, C_XB, C_GL, C_Z, C_X, C_B, C_C, C_DT, C_G = 0, 1024, 2048, 3072, 4096, 5120, 6144, 6656, 7168, 7184


class _Op:
    __slots__ = ("eng", "fn", "deps", "signal", "count", "dma_sem", "dma_count")

    def __init__(self, eng, fn, deps):
        self.eng = eng
        self.fn = fn
        self.deps = deps
        self.signal = False
        self.count = None
        self.dma_sem = None
        self.dma_count = None


class Prog:
    ENGS = ("pe", "act", "dve", "pool", "sp")

    def __init__(self, nc):
        self.nc = nc
        self.ops = {e: [] for e in self.ENGS}
        self.last_writer = {}
        self.readers = {}
        self.dma_sems = {}

    def _deps(self, reads, writes):
        deps = []
        for k in reads:
            w = self.last_writer.get(k)
            if w is not None:
                deps.append(w)
        for k in writes:
            w = self.last_writer.get(k)
            if w is not None:
                deps.append(w)
            deps.extend(self.readers.get(k, {}).values())
        return deps

    def _commit(self, op, reads, writes):
        rk = op.eng if op.dma_sem is None else ("dma", id(op))
        for k in reads:
            self.readers.setdefault(k, {})[rk] = op
        for k in writes:
            self.last_writer[k] = op
            self.readers[k] = {}

    def op(self, eng, fn, reads=(), writes=()):
        psr = [k for k in reads if isinstance(k, tuple) and k[0] == "ps"]
        if psr:
            writes = list(writes) + [k for k in psr if k not in writes]
        deps = self._deps(reads, writes)
        o = _Op(eng, fn, deps)
        for d in deps:
            d.signal = True
        self.ops[eng].append(o)
        self._commit(o, reads, writes)
        return o

    def dma(self, eng, fn, sem, reads=(), writes=()):
        deps = self._deps(reads, writes)
        o = _Op(eng, fn, deps)
        for d in deps:
            d.signal = True
        c = self.dma_sems.get(sem, 0) + 16
        self.dma_sems[sem] = c
        o.dma_sem = sem
        o.dma_count = c
        self.ops[eng].append(o)
        self._commit(o, reads, writes)
        return o

    def emit(self, final_waits=()):
        nc = self.nc
        with ExitStack() as es:
            esem = {e: es.enter_context(nc.semaphore("s_" + e)) for e in self.ENGS}
            dsem = {n: es.enter_context(nc.semaphore("d_" + n)) for n in self.dma_sems}
            for o in final_waits:
                o.signal = True
            for e in self.ENGS:
                c = 0
                for o in self.ops[e]:
                    if o.dma_sem is None and o.signal:
                        c += 1
                        o.count = c
            block = es.enter_context(nc.Block())

            def run(e, engine):
                waited = {}
                for o in self.ops[e]:
                    need = {}
                    for d in o.deps:
                        if d.dma_sem is not None:
                            key = ("d", d.dma_sem)
                            val = d.dma_count
                        else:
                            if d.eng == e and e in ("pe", "sp"):
                                continue
                            key = ("e", d.eng)
                            val = d.count
                        if val > need.get(key, 0):
                            need[key] = val
                    for key, val in need.items():
                        if waited.get(key, 0) >= val:
                            continue
                        waited[key] = val
                        s = dsem[key[1]] if key[0] == "d" else esem[key[1]]
                        engine.wait_ge(s, val)
                    ins = o.fn(engine)
                    if o.dma_sem is not None:
                        ins.then_inc(dsem[o.dma_sem], 16)
                    elif o.signal:
                        ins.then_inc(esem[e], 1)
                if e == "sp":
                    for o in final_waits:
                        if o.dma_sem is not None:
                            engine.wait_ge(dsem[o.dma_sem], o.dma_count)
                        else:
                            engine.wait_ge(esem[o.eng], o.count)

            @block.tensor
            def _(eng):
                run("pe", eng)

            @block.scalar
            def _(eng):
                run("act", eng)

            @block.vector
            def _(eng):
                run("dve", eng)

            @block.gpsimd
            def _(eng):
                run("pool", eng)

            @block.sync
            def _(eng):
                run("sp", eng)


class B:
    __slots__ = ("ap", "keys")

    def __init__(self, ap, keys):
        self.ap = ap
        self.keys = list(keys)

    def s(self, ap, keys=None):
        return B(ap, self.keys if keys is None else keys)


def bc(ap, shape, axis):
    return ap.unsqueeze(axis).to_broadcast(list(shape))


def build_program(n_tiles, debug_taps=False):
    nc = bass.Bass("TRN2", target_bir_lowering=False)
    P = Prog(nc)

    def din(name, shape):
        return nc.dram_tensor(name, list(shape), F32, kind="ExternalInput").ap()

    xT = din("xT", [D, SEQ])
    w_in = din("w_in", [DEPTH, D, D_IN])
    w_ba = din("w_branch_a", [DEPTH, D, D])
    w_bb = din("w_branch_b", [DEPTH, D, D])
    w_bc = din("w_branch_c", [DEPTH, D, D])
    w_out = din("w_out", [DEPTH, D, D])
    w_up = din("w_mlp_up", [DEPTH, D, 4 * D])
    w_dn = din("w_mlp_down", [DEPTH, 4 * D, D])
    pvec_d = din("pvec", [DEPTH, 128, NPV])
    prow_d = din("prow", [DEPTH, 128, NPR])
    bsrow_d = din("bsrow", [DEPTH, 1, 1024])
    gwT_d = din("gwT", [DEPTH, 128, 8, 128])
    wr_d = din("wr", [DEPTH, 128, 8, 128])
    wi_d = din("wi", [DEPTH, 128, 8, 128])
    consts_d = din("consts", [128, 3, 128])
    outT = nc.dram_tensor("outT", [D, SEQ], F32, kind="ExternalOutput").ap()
    taps = {}

    with ExitStack() as es:
        def sb(name, shape, dt):
            return es.enter_context(nc.sbuf_tensor("sb_" + name, list(shape), dt))

        consts = sb("consts", [128, 3, 128], F32)
        ident_bf = sb("ident_bf", [128, 128], BF16)
        U_f = consts[:, 1, :]
        Ls_f = consts[:, 2, :]
        ones_bf = sb("ones_bf", [128, 128], BF16)
        ones_f = sb("ones_f", [128, 128], F32)
        cst = sb("cst", [128, 4], F32)
        pv = [sb("pv%d" % l, [128, NPV], F32) for l in range(DEPTH)]
        pr = [sb("pr%d" % l, [128, NPR], F32) for l in range(DEPTH)]
        gw_bf = [sb("gw%d" % l, [128, 8, 128], BF16) for l in range(DEPTH)]
        wr_bf = [sb("wr%d" % l, [128, 8, 128], BF16) for l in range(DEPTH)]
        wi_bf = [sb("wi%d" % l, [128, 8, 128], BF16) for l in range(DEPTH)]
        wdt_bf = [sb("wdt%d" % l, [128, 8, 16], BF16) for l in range(DEPTH)]
        Eg = [sb("Eg%d" % l, [128, 8, 128], F32) for l in range(DEPTH)]
        Dg = [sb("Dg%d" % l, [128, 16, 128], BF16) for l in range(DEPTH)]
        clru = [sb("clru%d" % l, [128, 16], F32) for l in range(DEPTH)]
        arow = [sb("arow%d" % l, [128, 16], F32) for l in range(DEPTH)]
        S = [sb("S%d" % l, [128, 1024], F32) for l in range(DEPTH)]
        S_bf = [sb("Sbf%d" % l, [128, 1024], BF16) for l in range(DEPTH)]
        hst = [sb("hst%d" % l, [128, 8], F32) for l in range(DEPTH)]
        hist_l = [sb("histl%d" % l, [128, 8, 4], F32) for l in range(DEPTH)]
        hist_s = [sb("hists%d" % l, [128, 16, 4], F32) for l in range(DEPTH)]
        h_t = sb("h", [128, 8, T], F32)
        hn_t = sb("hn", [128, 8, T], BF16)
        sqb_t = sb("sqb", [128, 2, T], BF16)
        rstd_t = sb("rstd", [128, T], F32)
        lnt_t = sb("lnt", [128, T], F32)
        uT_t = sb("uT", [128, 8, T], F32)
        up_t = sb("up", [128, 32, T], BF16)
        NW = 4
        wring = [sb("wr_ring%d" % i, [128, 8, 512], BF16) for i in range(NW)]
        ARENA_N = 12288
        arena = sb("arena", [128, ARENA_N], F32)
        small = sb("small", [128, 896], F32)
        ps = es.enter_context(nc.psum_tensor("ps", [128, 4096], F32))
        bsrow = arena

        class Arena:
            def __init__(self):
                self.p = 0

            def reset(self):
                self.p = 0

            def f32(self, n, shape=None):
                a, b = self.p, self.p + n
                assert b <= ARENA_N, "arena overflow"
                self.p = (b + 127) // 128 * 128
                ap = arena[:, a:b]
                if shape is not None:
                    ap = ap.rearrange("p (a b) -> p a b", a=shape[0])
                return B(ap, [("A", g) for g in range(a // 128, (b + 127) // 128)])

            def bf16(self, n, shape=None):
                w = (n + 1) // 2
                a, b = self.p, self.p + w
                assert b <= ARENA_N, "arena overflow"
                self.p = (b + 127) // 128 * 128
                ap = arena[:, a:b].bitcast(BF16)
                if shape is not None:
                    ap = ap.rearrange("p (a b) -> p a b", a=shape[0])
                return B(ap, [("A", g) for g in range(a // 128, (b + 127) // 128)])

        AR = Arena()
        sm_p = [0]

        def sm(n):
            a = sm_p[0]
            sm_p[0] += n
            assert sm_p[0] <= 896
            return B(small[:, a:a + n], [("sm", w) for w in range(a // 8, (a + n - 1) // 8 + 1)])

        bank_p = [0]

        def bank(n=1):
            p = bank_p[0]
            if p % n:
                p += n - p % n
            if p + n > 8:
                p = 0
            bank_p[0] = (p + n) % 8
            return B(ps[:, p * 512:(p + n) * 512], [("ps", p + i) for i in range(n)])

        h = [B(h_t[:, c, :], [("h", c)]) for c in range(8)]
        hn = [B(hn_t[:, c, :], [("hn", c)]) for c in range(8)]
        sqb = [B(sqb_t[:, i, :], [("sqb", i)]) for i in range(2)]
        rstd = B(rstd_t[:], ["rstd"])
        lnt = B(lnt_t[:], ["lnt"])
        uT = [B(uT_t[:, c, :], [("uT", c)]) for c in range(8)]
        up = [B(up_t[:, c, :], [("up", c)]) for c in range(32)]
        ya, yb, yc, mg = up[0:8], up[8:16], up[16:24], up[24:32]
        PARAMS = ["params"]

        def pvc(l, col, n=1):
            return B(pv[l][:, col:col + n], PARAMS)

        def mm(out, lhsT, rhs, start=True, stop=True):
            P.op("pe", lambda e: e.matmul(out.ap, lhsT=lhsT.ap, rhs=rhs.ap, start=start, stop=stop),
                 reads=lhsT.keys + rhs.keys, writes=out.keys)

        def tr(out, in_):
            P.op("pe", lambda e: e.transpose(out=out.ap, in_=in_.ap, identity=ident_bf[:]),
                 reads=in_.keys + PARAMS, writes=out.keys)

        def act(out, in_, func, bias=None, scale=None, accum=None):
            rd = list(in_.keys)
            wr = list(out.keys)
            kw = {}
            if bias is not None:
                if isinstance(bias, B):
                    rd += bias.keys
                    kw["bias"] = bias.ap
                else:
                    kw["bias"] = bias
            if scale is not None:
                if isinstance(scale, B):
                    rd += scale.keys
                    kw["scale"] = scale.ap
                else:
                    kw["scale"] = scale
            if accum is not None:
                wr += accum.keys
                kw["accum_out"] = accum.ap
            P.op("act", lambda e: e.activation(out=out.ap, in_=in_.ap, func=func, **kw), reads=rd, writes=wr)

        def tt(out, in0, in1, op, eng="dve"):
            P.op(eng, lambda e: e.tensor_tensor(out=out.ap, in0=in0.ap, in1=in1.ap, op=op),
                 reads=in0.keys + in1.keys, writes=out.keys)

        def ts(out, in0, s1, op0, s2=None, op1=None, eng="dve"):
            rd = list(in0.keys)
            a1 = s1
            a2 = s2
            if isinstance(s1, B):
                rd += s1.keys
                a1 = s1.ap
            if isinstance(s2, B):
                rd += s2.keys
                a2 = s2.ap
            if op1 is None:
                P.op(eng, lambda e: e.tensor_scalar(out=out.ap, in0=in0.ap, scalar1=a1, scalar2=None, op0=op0),
                     reads=rd, writes=out.keys)
            else:
                P.op(eng, lambda e: e.tensor_scalar(out=out.ap, in0=in0.ap, scalar1=a1, scalar2=a2, op0=op0, op1=op1),
                     reads=rd, writes=out.keys)

        def stt(out, in0, scalar, in1, op0, op1, eng="dve"):
            rd = in0.keys + in1.keys
            a = scalar
            if isinstance(scalar, B):
                rd = rd + scalar.keys
                a = scalar.ap
            P.op(eng, lambda e: e.scalar_tensor_tensor(out=out.ap, in0=in0.ap, scalar=a, in1=in1.ap, op0=op0, op1=op1),
                 reads=rd, writes=out.keys)

        def cp(out, in_, eng="dve"):
            if eng == "act":
                P.op("act", lambda e: e.activation(out=out.ap, in_=in_.ap, func=AF.Copy), reads=in_.keys, writes=out.keys)
            else:
                P.op(eng, lambda e: e.tensor_copy(out=out.ap, in_=in_.ap), reads=in_.keys, writes=out.keys)

        def memset(buf, val, eng="dve"):
            P.op(eng, lambda e: e.memset(buf.ap, val), writes=buf.keys)

        def red_sum(out, in_):
            P.op("dve", lambda e: e.tensor_reduce(out=out.ap, in_=in_.ap, axis=AX.X, op=ALU.add),
                 reads=in_.keys, writes=out.keys)

        def scan(out, d0, d1, init):
            P.op("dve", lambda e: e.tensor_tensor_scan(out=out.ap, data0=d0.ap, data1=d1.ap, initial=init.ap,
                                                       op0=ALU.mult, op1=ALU.add),
                 reads=d0.keys + d1.keys + init.keys, writes=out.keys)

        def tap(name, buf, shape):
            if not debug_taps:
                return
            t = nc.dram_tensor("tap_" + name, list(shape), buf.ap.dtype, kind="ExternalOutput").ap()
            taps[name] = P.dma("sp", lambda e: e.dma_start(out=t, in_=buf.ap), "tap_" + name, reads=buf.keys)

        def w3(w, l):
            return w[l].rearrange("(kc p) e -> p kc e", p=128)

        def wsched(l):
            win3 = w3(w_in, l)
            sch = []
            for c0 in (C_V, C_V + 512, C_U, C_U + 512, C_XB, C_GL, C_XB + 512, C_GL + 512, C_X, C_X + 512, C_B, C_C,
                       C_Z, C_Z + 512):
                sch.append((win3, 0, c0))
            for k, wb in enumerate((w_ba, w_bb, w_bc)):
                for hf in range(2):
                    sch.append((win3, 0, C_G + k * 1024 + hf * 512))
                    sch.append((w3(wb, l), 0, hf * 512))
            for hf in range(2):
                sch.append((w3(w_out, l), 0, hf * 512))
            for f4 in range(8):
                sch.append((w3(w_up, l), 0, f4 * 512))
            for hf in range(2):
                for q in range(4):
                    sch.append((w3(w_dn, l), q * 8, hf * 512))
            return sch

        WSTREAM = []
        for ti_ in range(n_tiles):
            for l_ in range(DEPTH):
                WSTREAM.extend(wsched(l_))
        PF = NW - 1
        wcur = [0]
        wissued = [0]

        def _issue(i):
            src3, kc0, c0 = WSTREAM[i]
            si = i % NW
            slot = wring[si]
            for hf in range(2):
                a, b = hf * 4, hf * 4 + 4
                dst = slot[:, a:b, :]
                src = src3[:, kc0 + a:kc0 + b, c0:c0 + 512]
                P.dma("pool", (lambda dst, src: (lambda e: e.dma_start(out=dst, in_=src)))(dst, src),
                      "w%d_%d" % (si, hf), writes=[("w", si, hf)])

        def wload(src3, kc0, c0, n=512):
            i = wcur[0]
            wcur[0] += 1
            assert WSTREAM[i][1] == kc0 and WSTREAM[i][2] == c0, ("weight schedule mismatch", i, kc0, c0, WSTREAM[i][1:])
            while wissued[0] < min(len(WSTREAM), i + PF + 1):
                _issue(wissued[0])
                wissued[0] += 1
            si = i % NW
            return wring[si], [("w", si, 0), ("w", si, 1)]

        def wl(slot_keys, kc, c0, n):
            slot, keys = slot_keys
            return B(slot[:, kc, c0:c0 + n], [keys[kc // 4]])

        P.dma("sp", lambda e: e.dma_start(out=consts[:], in_=consts_d), "ld_c", writes=PARAMS)
        for l in range(DEPTH):
            P.dma("sp", (lambda l: (lambda e: e.dma_start(out=pv[l][:], in_=pvec_d[l])))(l), "ld_pv%d" % l, writes=PARAMS)
            P.dma("sp", (lambda l: (lambda e: e.dma_start(out=pr[l][:], in_=prow_d[l])))(l), "ld_pr%d" % l, writes=PARAMS)
            P.dma("sp", (lambda l: (lambda e: e.dma_start(out=bsrow[0:1, l * 1024:(l + 1) * 1024], in_=bsrow_d[l])))(l),
                  "ld_bs%d" % l, writes=PARAMS)
            P.dma("pool", (lambda l: (lambda e: e.dma_start(out=gw_bf[l][:], in_=gwT_d[l])))(l), "ld_gw%d" % l, writes=PARAMS)
            P.dma("pool", (lambda l: (lambda e: e.dma_start(out=wr_bf[l][:], in_=wr_d[l])))(l), "ld_wr%d" % l, writes=PARAMS)
            P.dma("pool", (lambda l: (lambda e: e.dma_start(out=wi_bf[l][:], in_=wi_d[l])))(l), "ld_wi%d" % l, writes=PARAMS)
            P.dma("pool", (lambda l: (lambda e: e.dma_start(out=wdt_bf[l][:], in_=w3(w_in, l)[:, :, C_DT:C_DT + 16])))(l),
                  "ld_wdt%d" % l, writes=PARAMS)
        PB = B(None, PARAMS)
        P.op("dve", lambda e: e.memset(ones_bf[:], 1.0), reads=PARAMS, writes=PARAMS)
        P.op("dve", lambda e: e.memset(ones_f[:], 1.0), reads=PARAMS, writes=PARAMS)
        P.op("dve", lambda e: e.memset(cst[:, 0:1], EPS), reads=PARAMS, writes=PARAMS)
        P.op("dve", lambda e: e.memset(cst[:, 1:2], 1.0), reads=PARAMS, writes=PARAMS)
        P.op("dve", lambda e: e.tensor_copy(out=ident_bf[:], in_=consts[:, 0, :]), reads=PARAMS, writes=PARAMS)
        eps_b = B(cst[:, 0:1], PARAMS)
        one_b = B(cst[:, 1:2], PARAMS)
        for l in range(DEPTH):
            for t_, n_ in ((S[l], 1024), (hst[l], 8)):
                P.op("dve", (lambda t_: (lambda e: e.memset(t_[:], 0.0)))(t_), reads=PARAMS, writes=PARAMS)
            P.op("dve", (lambda l: (lambda e: e.memset(S_bf[l][:], 0.0)))(l), reads=PARAMS, writes=PARAMS)
            P.op("dve", (lambda l: (lambda e: e.memset(hist_l[l][:], 0.0)))(l), reads=PARAMS, writes=PARAMS)
            P.op("dve", (lambda l: (lambda e: e.memset(hist_s[l][:], 0.0)))(l), reads=PARAMS, writes=PARAMS)
            P.op("dve", (lambda l: (lambda e: e.tensor_tensor(out=gw_bf[l][:], in0=gw_bf[l][:], in1=bc(U_f, [128, 8, 128], 1),
                                                               op=ALU.mult)))(l), reads=PARAMS, writes=PARAMS)
            P.op("act", (lambda l: (lambda e: e.activation(out=clru[l][:, 0:8], in_=pv[l][:, PV_LAM:PV_LAM + 8], func=AF.Exp,
                                                           scale=-1.0)))(l), reads=PARAMS, writes=PARAMS)
            P.op("act", (lambda l: (lambda e: e.activation(out=clru[l][:, 0:8], in_=clru[l][:, 0:8], func=AF.Ln,
                                                           bias=cst[:, 1:2], scale=1.0)))(l), reads=PARAMS, writes=PARAMS)
            P.op("dve", (lambda l: (lambda e: e.tensor_scalar(out=clru[l][:, 8:16], in0=clru[l][:, 0:8], scalar1=-16.0,
                                                              scalar2=None, op0=ALU.mult)))(l), reads=PARAMS, writes=PARAMS)
            P.op("dve", (lambda l: (lambda e: e.tensor_scalar(out=clru[l][:, 0:8], in0=clru[l][:, 0:8], scalar1=-8.0,
                                                              scalar2=None, op0=ALU.mult)))(l), reads=PARAMS, writes=PARAMS)
            P.op("act", (lambda l: (lambda e: e.activation(out=arow[l][:], in_=pr[l][:, PR_ALOG:PR_ALOG + 16],
                                                           func=AF.Exp)))(l), reads=PARAMS, writes=PARAMS)
            P.op("dve", (lambda l: (lambda e: e.tensor_scalar(out=arow[l][:], in0=arow[l][:], scalar1=-1.0, scalar2=None,
                                                              op0=ALU.mult)))(l), reads=PARAMS, writes=PARAMS)
            P.op("dve", (lambda l: (lambda e: e.tensor_tensor(out=Dg[l][:], in0=bc(consts[:, 0, :], [128, 16, 128], 1),
                                                               in1=bc(pr[l][:, PR_D:PR_D + 16], [128, 16, 128], 2),
                                                               op=ALU.mult)))(l), reads=PARAMS, writes=PARAMS)
            bkA = bank(2)
            bkB = bank(2)
            for g in range(8):
                P.op("pe", (lambda l, g, bk: (lambda e: e.matmul(bk.ap[:, g * 128:(g + 1) * 128], lhsT=ones_bf[:],
                                                                 rhs=gw_bf[l][:, g, :], start=True, stop=True)))(l, g, bkA),
                     reads=PARAMS, writes=bkA.keys)
            for hf in range(2):
                P.op("pe", (lambda l, hf, bk: (lambda e: e.matmul(bk.ap[:, hf * 512:(hf + 1) * 512], lhsT=ones_f[0:1, :],
                                                                  rhs=bsrow[0:1, l * 1024 + hf * 512:l * 1024 + (hf + 1) * 512],
                                                                  start=True, stop=True)))(l, hf, bkB),
                     reads=PARAMS, writes=bkB.keys)
            P.op("act", (lambda l, bk: (lambda e: e.activation(out=Eg[l][:], in_=bk.ap.rearrange("p (a b) -> p a b", a=8),
                                                               func=AF.Copy)))(l, bkB), reads=bkB.keys + PARAMS, writes=PARAMS)
            for g in range(8):
                P.op("dve", (lambda l, g, bk: (lambda e: e.scalar_tensor_tensor(
                    out=Eg[l][:, g, :], in0=bk.ap[:, g * 128:(g + 1) * 128], scalar=pv[l][:, PV_LNB + g:PV_LNB + g + 1],
                    in1=Eg[l][:, g, :], op0=ALU.mult, op1=ALU.add)))(l, g, bkA), reads=bkA.keys + PARAMS, writes=PARAMS)

        def rmsnorm(l, gcol):
            bk = bank()
            for c in range(8):
                act(sqb[c % 2], h[c], AF.Square)
                mm(bk, B(ones_bf[:], PARAMS), sqb[c % 2], start=(c == 0), stop=(c == 7))
            act(lnt, bk, AF.Ln, bias=eps_b, scale=1.0 / D)
            act(rstd, lnt, AF.Exp, scale=-0.5)
            for c in range(8):
                stt(hn[c], h[c], pvc(l, gcol + c), rstd, ALU.mult, ALU.mult)

        def conv4(dst, xbuf, wcol, bcol, l, acc):
            ts(acc, xbuf.s(xbuf.ap[:, 0:T]), pvc(l, wcol + 0), ALU.mult, pvc(l, bcol), ALU.add)
            for k in (1, 2):
                stt(acc, xbuf.s(xbuf.ap[:, k:k + T]), pvc(l, wcol + k), acc, ALU.mult, ALU.add)
            stt(dst, xbuf.s(xbuf.ap[:, 3:3 + T]), pvc(l, wcol + 3), acc, ALU.mult, ALU.add)

        import os
        STOP = int(os.environ.get("MK_STOP", "99"))

        def mark(name):
            PHASES.append((name, len(P.ops["pe"])))

        def layer(l, ti):
            win3 = w3(w_in, l)
            mark("norm1")
            if STOP < 1:
                return
            rmsnorm(l, PV_NMIX)
            if STOP < 2:
                return
            mark('A')
            AR.reset()
            vtok = [AR.f32(1024) for _ in range(NJ)]
            vn = [AR.bf16(1024) for _ in range(NJ)]
            tmpA = AR.f32(T)
            junk = AR.bf16(1024)
            for hf in range(2):
                wt = wload(win3, 0, C_V + hf * 512)
                for j in range(NJ):
                    bk = bank()
                    for kc in range(8):
                        mm(bk, hn[kc].s(hn[kc].ap[:, j * 128:(j + 1) * 128]), wl(wt, kc, 0, 512), kc == 0, kc == 7)
                    act(vtok[j].s(vtok[j].ap[:, hf * 512:(hf + 1) * 512]), bk, AF.Gelu_apprx_tanh)
            def sm1():
                b_ = sm(8)
                return b_.s(b_.ap[:, 0:1])

            st_ = [[sm1() for _ in range(6)] for j in range(NJ)]
            for j in range(NJ):
                memset(st_[j][1], 0.0)
            for j in range(NJ):
                red_sum(st_[j][0], vtok[j])
                act(junk, vtok[j], AF.Square, accum=st_[j][1])
            for j in range(NJ):
                ts(st_[j][2], st_[j][0], 1.0 / 1024, ALU.mult)
            for j in range(NJ):
                tt(st_[j][4], st_[j][2], st_[j][2], ALU.mult)
            for j in range(NJ):
                stt(st_[j][3], st_[j][1], 1.0 / 1024, st_[j][4], ALU.mult, ALU.subtract)
            for j in range(NJ):
                act(st_[j][3], st_[j][3], AF.Ln, bias=eps_b, scale=1.0)
            for j in range(NJ):
                act(st_[j][5], st_[j][3], AF.Exp, scale=-0.5)
            for j in range(NJ):
                ts(vn[j], vtok[j], st_[j][2], ALU.subtract, st_[j][5], ALU.mult)
            sm_p[0] = 0
            for hf in range(2):
                wt = wload(win3, 0, C_U + hf * 512)
                for m4 in range(4):
                    m = hf * 4 + m4
                    bk = bank()
                    for kc in range(8):
                        mm(bk, wl(wt, kc, m4 * 128, 128), hn[kc], kc == 0, kc == 7)
                    act(uT[m], bk, AF.Gelu_apprx_tanh)
            for g in range(8):
                bk = bank()
                for j in range(NJ):
                    mm(bk.s(bk.ap[:, j * 128:(j + 1) * 128]), vn[j].s(vn[j].ap[:, g * 128:(g + 1) * 128]),
                       B(gw_bf[l][:, g, :], PARAMS))
                stt(tmpA.s(tmpA.ap.rearrange("p (a b) -> p a b", a=NJ)), bk.s(bk.ap.rearrange("p (a b) -> p a b", a=NJ)),
                    pvc(l, PV_LNG + g), B(bc(Eg[l][:, g, :], [128, NJ, 128], 1), PARAMS), ALU.mult, ALU.add)
                tt(ya[g], tmpA, uT[g], ALU.mult)
            if STOP < 3:
                return
            mark('B')
            AR.reset()
            hseq4 = AR.f32(4 * T, shape=(4, T))
            xbufs = [AR.f32(T + 4) for _ in range(4)]
            accs = [AR.f32(T) for _ in range(4)]
            xcs = [AR.f32(T) for _ in range(4)]
            xc_bfs = [AR.bf16(T) for _ in range(2)]
            r4 = [AR.f32(T) for _ in range(4)]
            i2 = [AR.f32(T) for _ in range(2)]
            a4 = accs
            gg = xbufs[0].s(xbufs[0].ap[:, 0:T])
            for hf in range(2):
                wt = wload(win3, 0, C_XB + hf * 512)
                bks = []
                for m4 in range(4):
                    bk = bank()
                    for kc in range(8):
                        mm(bk, wl(wt, kc, m4 * 128, 128), hn[kc], kc == 0, kc == 7)
                    bks.append(bk)

                def stA(m4):
                    hd = hf * 4 + m4
                    xbuf, acc = xbufs[m4], accs[m4]
                    hl = B(hist_l[l][:, hd, 0:3], [("hl", l, hd)])
                    cp(xbuf.s(xbuf.ap[:, 0:3]), hl)
                    cp(hl, bks[m4].s(bks[m4].ap[:, T - 3:T]))
                    cp(xbuf.s(xbuf.ap[:, 3:3 + T]), bks[m4], eng="act")
                    act(acc, bks[m4], AF.Identity, bias=pvc(l, PV_LCB + hd), scale=pvc(l, PV_LCW + hd * 4 + 3))

                def stB2(ms):
                    for k in range(3):
                        for m4 in ms:
                            hd = hf * 4 + m4
                            dst = xcs[m4] if k == 2 else accs[m4]
                            stt(dst, xbufs[m4].s(xbufs[m4].ap[:, k:k + T]), pvc(l, PV_LCW + hd * 4 + k), accs[m4],
                                ALU.mult, ALU.add)
                    out = {}
                    for m4 in ms:
                        cp(xc_bfs[m4 % 2], xcs[m4], eng="act")
                    for m4 in ms:
                        hd = hf * 4 + m4
                        bkr = bank()
                        mm(bkr, B(wr_bf[l][:, hd, :], PARAMS), xc_bfs[m4 % 2])
                        bki = bank()
                        mm(bki, B(wi_bf[l][:, hd, :], PARAMS), xc_bfs[m4 % 2])
                        out[m4] = (bkr, bki)
                    return out

                ri = {}
                for m4 in range(4):
                    stA(m4)
                ri.update(stB2((0, 1)))
                ri.update(stB2((2, 3)))
                for m4 in range(4):
                    hd = hf * 4 + m4
                    bkr, bki = ri[m4]
                    act(r4[m4], bkr, AF.Sigmoid, bias=pvc(l, PV_BR + hd), scale=1.0)
                    act(i2[m4 % 2], bki, AF.Sigmoid, bias=pvc(l, PV_BI + hd), scale=1.0)
                    tt(xcs[m4], xcs[m4], i2[m4 % 2], ALU.mult)
                for m4 in range(4):
                    hd = hf * 4 + m4
                    act(a4[m4], r4[m4], AF.Exp, scale=B(clru[l][:, hd:hd + 1], PARAMS))
                for m4 in range(4):
                    hd = hf * 4 + m4
                    act(r4[m4], r4[m4], AF.Exp, scale=B(clru[l][:, 8 + hd:9 + hd], PARAMS))
                for m4 in range(4):
                    act(r4[m4], r4[m4], AF.Ln, bias=one_b, scale=-1.0)
                for m4 in range(4):
                    act(r4[m4], r4[m4], AF.Exp, scale=0.5)
                for m4 in range(4):
                    tt(xcs[m4], xcs[m4], r4[m4], ALU.mult)
                for m4 in range(4):
                    hd = hf * 4 + m4
                    hq = hseq4.s(hseq4.ap[:, m4, :], hseq4.keys[m4 * 4:(m4 + 1) * 4])
                    hs = B(hst[l][:, hd:hd + 1], [("hst", l, hd)])
                    scan(hq, a4[m4], xcs[m4], hs)
                    cp(hs, hq.s(hq.ap[:, T - 1:T]))
                wt = wload(win3, 0, C_GL + hf * 512)
                for m4 in range(4):
                    hd = hf * 4 + m4
                    bk = bank()
                    for kc in range(8):
                        mm(bk, wl(wt, kc, m4 * 128, 128), hn[kc], kc == 0, kc == 7)
                    act(gg, bk, AF.Gelu_apprx_tanh)
                    hq = hseq4.s(hseq4.ap[:, m4, :], hseq4.keys[m4 * 4:(m4 + 1) * 4])
                    tt(yb[hd], gg, hq, ALU.mult)
            if STOP < 4:
                return
            mark('Cconv')
            AR.reset()
            xsT = mg
            BT = AR.bf16(4 * T, shape=(4, T))
            CT = AR.bf16(4 * T, shape=(4, T))
            arena_mark = AR.p
            xbufs = [AR.f32(T + 4) for _ in range(4)]
            accs = [AR.f32(T) for _ in range(4)]
            xcvs = [AR.f32(T) for _ in range(4)]
            cbks = {}
            ctiles = (C_X, C_X + 512, C_B, C_C)
            nproj = [0]

            def cproj_upto(ch):
                while nproj[0] <= min(ch, 15):
                    wt = wload(win3, 0, ctiles[nproj[0] // 4])
                    for m4 in range(4):
                        bk = bank()
                        for kc in range(8):
                            mm(bk, wl(wt, kc, m4 * 128, 128), hn[kc], kc == 0, kc == 7)
                        cbks[nproj[0]] = bk
                        nproj[0] += 1

            def cstA(ch):
                xbuf, acc = xbufs[ch % 4], accs[ch % 4]
                hl = B(hist_s[l][:, ch, 0:3], [("hs", l, ch)])
                cp(xbuf.s(xbuf.ap[:, 0:3]), hl)
                cp(hl, cbks[ch].s(cbks[ch].ap[:, T - 3:T]))
                cp(xbuf.s(xbuf.ap[:, 3:3 + T]), cbks[ch], eng="act")
                act(acc, cbks[ch], AF.Identity, bias=pvc(l, PV_SCB + ch), scale=pvc(l, PV_SCW + ch * 4 + 3))

            def cstB2(chs):
                for k in range(3):
                    for ch in chs:
                        dst = xcvs[ch % 4] if k == 2 else accs[ch % 4]
                        stt(dst, xbufs[ch % 4].s(xbufs[ch % 4].ap[:, k:k + T]), pvc(l, PV_SCW + ch * 4 + k), accs[ch % 4],
                            ALU.mult, ALU.add)
                for ch in chs:
                    if ch < 8:
                        dst = xsT[ch]
                    elif ch < 12:
                        dst = BT.s(BT.ap[:, ch - 8, :])
                    else:
                        dst = CT.s(CT.ap[:, ch - 12, :])
                    act(dst, xcvs[ch % 4], AF.Silu)

            cproj_upto(1)
            cstA(0)
            cstA(1)
            for p in range(8):
                if p + 1 < 8:
                    cproj_upto(2 * p + 3)
                    cstA(2 * p + 2)
                    cstA(2 * p + 3)
                cstB2((2 * p, 2 * p + 1))
            mark('Cdt')
            sm_p[0] = 0
            NH4 = NJ * 16
            dtx, ax, ex, lgx, dt_, adt, cs_sb, ecs, tmc, ds, cd, dtds = [sm(NH4) for _ in range(12)]
            psd = bank()
            for j in range(NJ):
                for kc in range(8):
                    mm(psd.s(psd.ap[:, j * 16:(j + 1) * 16]), hn[kc].s(hn[kc].ap[:, j * 128:(j + 1) * 128]),
                       B(wdt_bf[l][:, kc, :], PARAMS), kc == 0, kc == 7)
            v4j = lambda b_: b_.s(b_.ap.rearrange("p (a b) -> p a b", a=NJ))
            tt(v4j(dtx), psd.s(psd.ap[:, 0:NH4].rearrange("p (a b) -> p a b", a=NJ)),
               B(bc(pr[l][:, PR_DTB:PR_DTB + 16], [128, NJ, 16], 1), PARAMS), ALU.add)
            ts(ax, dtx, -1.0, ALU.mult)
            tt(ax, ax, dtx, ALU.max)
            act(ex, ax, AF.Exp, scale=-1.0)
            act(lgx, ex, AF.Ln, bias=one_b, scale=1.0)
            ts(dt_, dtx, 0.0, ALU.max)
            tt(dt_, dt_, lgx, ALU.add)
            tt(v4j(adt), v4j(dt_), B(bc(arow[l][:], [128, NJ, 16], 1), PARAMS), ALU.mult)
            mark('Cz')
            szall = [B(uT_t[:, 2 * j:2 * j + 2, :].rearrange("p a b -> p (a b)"), [("uT", 2 * j), ("uT", 2 * j + 1)])
                     for j in range(NJ)]
            for hf in range(2):
                wt = wload(win3, 0, C_Z + hf * 512)
                for j in range(NJ):
                    bk = bank()
                    for kc in range(8):
                        mm(bk, hn[kc].s(hn[kc].ap[:, j * 128:(j + 1) * 128]), wl(wt, kc, 0, 512), kc == 0, kc == 7)
                    act(szall[j].s(szall[j].ap[:, hf * 512:(hf + 1) * 512]), bk, AF.Silu)
            mark('Cdt2')
            for j in range(NJ):
                mm(psd.s(psd.ap[:, NH4 + j * 16:NH4 + (j + 1) * 16]), B(U_f, PARAMS), adt.s(adt.ap[:, j * 16:(j + 1) * 16]))
                mm(psd.s(psd.ap[:, 2 * NH4 + j * 16:2 * NH4 + (j + 1) * 16]), B(ones_f[:], PARAMS),
                   adt.s(adt.ap[:, j * 16:(j + 1) * 16]))
            cp(cs_sb, psd.s(psd.ap[:, NH4:2 * NH4]))
            act(ecs, cs_sb, AF.Exp)
            tt(tmc, psd.s(psd.ap[:, 2 * NH4:3 * NH4]), cs_sb, ALU.subtract)
            act(ds, tmc, AF.Exp)
            act(cd, psd.s(psd.ap[:, 2 * NH4:3 * NH4]), AF.Exp)
            tt(dtds, dt_, ds, ALU.mult)
            AR.p = arena_mark
            Lb = AR.f32(16 * 128, shape=(16, 128))
            Lq = [Lb.s(Lb.ap[:, q * 4:(q + 1) * 4, :], Lb.keys[q * 4:(q + 1) * 4]) for q in range(4)]
            cbm = AR.f32(4 * 128, shape=(4, 128))
            FB = []
            for _ in range(2):
                FB.append(dict(Mb=AR.bf16(16 * 128, shape=(16, 128)), xs_bf=AR.bf16(1024), xdt=AR.bf16(1024),
                               xdd=AR.bf16(1024), Btok=AR.bf16(512)))
            ytmp = AR.f32(1024)
            yn = AR.bf16(1024)
            ss = sm(4)
            rs4 = sm(4)
            v3 = lambda b_: b_.s(b_.ap.rearrange("p (a b) -> p a b", a=16))
            v4 = lambda b_: b_.s(b_.ap.rearrange("p (a b) -> p a b", a=4))

            def hs(b_, j):
                return b_.s(b_.ap[:, j * 16:(j + 1) * 16])

            def front(j):
                js = slice(j * 128, (j + 1) * 128)
                f = FB[j % 2]
                adt_j, dt_j, dtds_j = hs(adt, j), hs(dt_, j), hs(dtds, j)
                tt(Lb, B(bc(Ls_f, [128, 16, 128], 1), PARAMS), adt_j.s(bc(adt_j.ap, [128, 16, 128], 2)), ALU.mult, eng="pool")
                psc = bank()
                for g in range(4):
                    mm(psc.s(psc.ap[:, g * 128:(g + 1) * 128]), BT.s(BT.ap[:, g, js]), CT.s(CT.ap[:, g, js]))
                tt(cbm, psc.s(psc.ap.rearrange("p (a b) -> p a b", a=4)), B(bc(U_f, [128, 4, 128], 1), PARAMS), ALU.mult)
                for q in range(4):
                    psg = bank()
                    for r in range(4):
                        hd = q * 4 + r
                        mm(psg.s(psg.ap[:, r * 128:(r + 1) * 128]), Lq[q].s(Lb.ap[:, hd, :]), B(U_f, PARAMS))
                    act(Lq[q], psg.s(psg.ap.rearrange("p (a b) -> p a b", a=4)), AF.Exp)
                    tt(f["Mb"].s(f["Mb"].ap[:, q * 4:(q + 1) * 4, :]), Lq[q], cbm.s(bc(cbm.ap[:, q, :], [128, 4, 128], 1)),
                       ALU.mult, eng="pool")
                pst = bank()
                pstb = pst.s(pst.ap.bitcast(BF16))
                for kc in range(8):
                    tr(pstb.s(pstb.ap[:, kc * 128:(kc + 1) * 128]), xsT[kc].s(xsT[kc].ap[:, js]))
                cp(f["xs_bf"], pstb, eng="act")
                tt(v3(f["xdt"]), v3(pstb), dt_j.s(bc(dt_j.ap, [128, 16, 64], 2)), ALU.mult)
                tt(v3(f["xdd"]), v3(pstb), dtds_j.s(bc(dtds_j.ap, [128, 16, 64], 2)), ALU.mult)
                psb = bank()
                psbb = psb.s(psb.ap.bitcast(BF16))
                for g in range(4):
                    tr(psbb.s(psbb.ap[:, g * 128:(g + 1) * 128]), BT.s(BT.ap[:, g, js]))
                cp(f["Btok"], psbb.s(psbb.ap[:, 0:512]), eng="act")

            def back(j):
                js = slice(j * 128, (j + 1) * 128)
                f = FB[j % 2]
                Mb, xs_bf, xdt, xdd, Btok = f["Mb"], f["xs_bf"], f["xdt"], f["xdd"], f["Btok"]
                ecs_j, cd_j = hs(ecs, j), hs(cd, j)
                yo = bank(2)
                for g in range(4):
                    mm(yo.s(yo.ap[:, g * 256:(g + 1) * 256]), CT.s(CT.ap[:, g, js]),
                       B(S_bf[l][:, g * 256:(g + 1) * 256], [("Sbf", l)]))
                yd = bank(2)
                for hd in range(16):
                    o_ = yd.s(yd.ap[:, hd * 64:(hd + 1) * 64])
                    mm(o_, Mb.s(Mb.ap[:, hd, :]), xdt.s(xdt.ap[:, hd * 64:(hd + 1) * 64]), True, False)
                    mm(o_, B(Dg[l][:, hd, :], PARAMS), xs_bf.s(xs_bf.ap[:, hd * 64:(hd + 1) * 64]), False, True)
                st = bank(2)
                for g in range(4):
                    mm(st.s(st.ap[:, g * 256:(g + 1) * 256]), Btok.s(Btok.ap[:, g * 128:(g + 1) * 128]),
                       xdd.s(xdd.ap[:, g * 256:(g + 1) * 256]))
                tt(v3(ytmp), v3(yo), ecs_j.s(bc(ecs_j.ap, [128, 16, 64], 2)), ALU.mult)
                tt(ytmp, yd, ytmp, ALU.add)
                Sb = B(S[l][:], [("S", l)])
                tt(v3(Sb), v3(Sb), cd_j.s(bc(cd_j.ap, [128, 16, 64], 2)), ALU.mult)
                tt(Sb, st, Sb, ALU.add)
                cp(B(S_bf[l][:], [("Sbf", l)]), Sb, eng="act")
                tt(ytmp, ytmp, szall[j], ALU.mult)
                memset(ss, 0.0)
                for g in range(4):
                    act(yn.s(yn.ap[:, g * 256:(g + 1) * 256]), ytmp.s(ytmp.ap[:, g * 256:(g + 1) * 256]), AF.Square,
                        accum=ss.s(ss.ap[:, g:g + 1]))
                act(rs4, ss, AF.Ln, bias=eps_b, scale=1.0 / 256)
                act(rs4, rs4, AF.Exp, scale=-0.5)
                tt(v4(yn), v4(ytmp), rs4.s(bc(rs4.ap, [128, 4, 256], 2)), ALU.mult)
                pyt = bank()
                pytb = pyt.s(pyt.ap.bitcast(BF16))
                for kc in range(8):
                    tr(pytb.s(pytb.ap[:, kc * 128:(kc + 1) * 128]), yn.s(yn.ap[:, kc * 128:(kc + 1) * 128]))
                for kc in range(8):
                    act(yc[kc].s(yc[kc].ap[:, js]), pytb.s(pytb.ap[:, kc * 128:(kc + 1) * 128]), AF.Copy,
                        scale=pvc(l, PV_SNG + kc))

            mark('Cchunks')
            front(0)
            for j in range(NJ):
                if j + 1 < NJ:
                    front(j + 1)
                back(j)
            if STOP < 5:
                return
            mark('merge')
            AR.reset()
            sm_p[0] = 0
            g4 = AR.f32(4 * T, shape=(4, T))
            tmpG = AR.f32(T)
            accm = uT
            for k, (wb, ybr) in enumerate(((w_ba, ya), (w_bb, yb), (w_bc, yc))):
                wb3 = w3(wb, l)
                for hf in range(2):
                    wt = wload(win3, 0, C_G + k * 1024 + hf * 512)
                    for m4 in range(4):
                        m = hf * 4 + m4
                        bk = bank()
                        for kc in range(8):
                            mm(bk, wl(wt, kc, m4 * 128, 128), hn[kc], kc == 0, kc == 7)
                        act(g4.s(g4.ap[:, m4, :], g4.keys[m4 * 4:(m4 + 1) * 4]), bk, AF.Sigmoid, bias=pvc(l, PV_BG + k * 8 + m), scale=1.0)
                    wt = wload(wb3, 0, hf * 512)
                    for m4 in range(4):
                        m = hf * 4 + m4
                        bk = bank()
                        for kc in range(8):
                            mm(bk, wl(wt, kc, m4 * 128, 128), ybr[kc], kc == 0, kc == 7)
                        gm = g4.s(g4.ap[:, m4, :], g4.keys[m4 * 4:(m4 + 1) * 4])
                        if k == 0:
                            tt(accm[m], bk, gm, ALU.mult)
                        elif k == 1:
                            tt(tmpG, bk, gm, ALU.mult)
                            tt(accm[m], accm[m], tmpG, ALU.add)
                        else:
                            tt(tmpG, bk, gm, ALU.mult)
                            tt(mg[m], accm[m], tmpG, ALU.add)
            if STOP < 6:
                return
            mark('outproj')
            wo3 = w3(w_out, l)
            for hf in range(2):
                wt = wload(wo3, 0, hf * 512)
                for m4 in range(4):
                    m = hf * 4 + m4
                    bk = bank()
                    for kc in range(8):
                        mm(bk, wl(wt, kc, m4 * 128, 128), mg[kc], kc == 0, kc == 7)
                    tt(h[m], bk, h[m], ALU.add)
            if STOP < 7:
                return
            mark('mlp_up')
            rmsnorm(l, PV_NMLP)
            AR.reset()
            rl = [AR.f32(T) for _ in range(2)]
            wu3 = w3(w_up, l)
            for f4 in range(8):
                wt = wload(wu3, 0, f4 * 512)
                for m4 in range(4):
                    f = f4 * 4 + m4
                    bk = bank()
                    for kc in range(8):
                        mm(bk, wl(wt, kc, m4 * 128, 128), hn[kc], kc == 0, kc == 7)
                    act(rl[f % 2], bk, AF.Relu)
                    tt(up[f], rl[f % 2], rl[f % 2], ALU.mult)
            mark('mlp_down')
            wd3 = w3(w_dn, l)
            for hf in range(2):
                bks = [bank() for _ in range(4)]
                for q in range(4):
                    wt = wload(wd3, q * 8, hf * 512)
                    for kc in range(8):
                        for m4 in range(4):
                            mm(bks[m4], wl(wt, kc, m4 * 128, 128), up[q * 8 + kc], (q == 0 and kc == 0), (q == 3 and kc == 7))
                for m4 in range(4):
                    m = hf * 4 + m4
                    tt(h[m], bks[m4], h[m], ALU.add)

        finals = []
        xT3 = xT.rearrange("(c p) t -> p c t", p=128)
        oT3 = outT.rearrange("(c p) t -> p c t", p=128)
        for ti in range(n_tiles):
            t0 = ti * T
            for c2 in range(2):
                P.dma("sp", (lambda c2, t0: (lambda e: e.dma_start(out=h_t[:, c2 * 4:(c2 + 1) * 4, :],
                                                                   in_=xT3[:, c2 * 4:(c2 + 1) * 4, t0:t0 + T])))(c2, t0),
                      "ld_x%d" % c2, writes=[("h", c) for c in range(c2 * 4, c2 * 4 + 4)])
            for l in range(DEPTH):
                layer(l, ti)
                if debug_taps and ti == 0:
                    tap("h_l%d" % l, B(h_t[:], [("h", c) for c in range(8)]), [128, 8, T])
            mark('final')
            bk = bank()
            for c in range(8):
                act(sqb[c % 2], h[c], AF.Square)
                mm(bk, B(ones_bf[:], PARAMS), sqb[c % 2], start=(c == 0), stop=(c == 7))
            act(lnt, bk, AF.Ln, bias=eps_b, scale=1.0 / D)
            act(rstd, lnt, AF.Exp, scale=-0.5)
            for c in range(8):
                stt(uT[c], h[c], pvc(0, PV_FIN + c), rstd, ALU.mult, ALU.mult)
            for c2 in range(2):
                finals.append(P.dma("sp", (lambda c2, t0: (lambda e: e.dma_start(out=oT3[:, c2 * 4:(c2 + 1) * 4, t0:t0 + T],
                                                                                 in_=uT_t[:, c2 * 4:(c2 + 1) * 4, :])))(c2, t0),
                                    "st_o%d" % c2, reads=[("uT", c) for c in range(c2 * 4, c2 * 4 + 4)]))
        finals.extend(taps.values())
        P.emit(final_waits=finals)
    return nc


def _vec8(v):
    return np.ascontiguousarray(v.reshape(-1, 128).T)


def prep_params(inp):
    f = lambda k: np.asarray(inp[k], dtype=np.float32)
    pvec = np.zeros((DEPTH, 128, NPV), np.float32)
    prow = np.zeros((DEPTH, 128, NPR), np.float32)
    for l in range(DEPTH):
        pvec[l, :, PV_NMIX:PV_NMIX + 8] = _vec8(f("norm_mix_g")[l])
        pvec[l, :, PV_NMLP:PV_NMLP + 8] = _vec8(f("norm_mlp_g")[l])
        pvec[l, :, PV_LNG:PV_LNG + 8] = _vec8(f("gmlp_ln_g")[l])
        pvec[l, :, PV_LNB:PV_LNB + 8] = _vec8(f("gmlp_ln_b")[l])
        cw = f("lru_conv_w")[l]
        pvec[l, :, PV_LCW:PV_LCW + 32] = cw.reshape(4, 8, 128).transpose(2, 1, 0).reshape(128, 32)
        pvec[l, :, PV_LCB:PV_LCB + 8] = _vec8(f("lru_conv_b")[l])
        pvec[l, :, PV_BR:PV_BR + 8] = _vec8(f("lru_b_r")[l])
        pvec[l, :, PV_BI:PV_BI + 8] = _vec8(f("lru_b_i")[l])
        pvec[l, :, PV_LAM:PV_LAM + 8] = _vec8(f("lru_lambda")[l])
        sw = f("ssd_conv_w")[l]
        pvec[l, :, PV_SCW:PV_SCW + 64] = sw.reshape(4, 16, 128).transpose(2, 1, 0).reshape(128, 64)
        pvec[l, :, PV_SCB:PV_SCB + 16] = _vec8(f("ssd_conv_b")[l])
        pvec[l, :, PV_BG:PV_BG + 24] = _vec8(f("b_gate")[l].reshape(-1))
        pvec[l, :, PV_SNG:PV_SNG + 8] = _vec8(f("ssd_norm_g")[l])
        pvec[l, :, PV_FIN:PV_FIN + 8] = _vec8(f("final_norm_g"))
        prow[l, :, PR_DTB:PR_DTB + 16] = f("ssd_dt_bias")[l][None, :]
        prow[l, :, PR_ALOG:PR_ALOG + 16] = f("ssd_a_log")[l][None, :]
        prow[l, :, PR_D:PR_D + 16] = f("ssd_d")[l][None, :]
    bsrow = np.ascontiguousarray(f("gmlp_b_s").reshape(DEPTH, 1, 1024))
    gwT = np.ascontiguousarray(f("gmlp_w_s").transpose(0, 3, 1, 2))
    wr = np.ascontiguousarray(f("lru_w_r").transpose(0, 2, 1, 3))
    wi = np.ascontiguousarray(f("lru_w_i").transpose(0, 2, 1, 3))
    consts = np.zeros((128, 3, 128), np.float32)
    consts[:, 0, :] = np.eye(128, dtype=np.float32)
    consts[:, 1, :] = np.triu(np.ones((128, 128), np.float32))
    consts[:, 2, :] = np.tril(np.ones((128, 128), np.float32), -1)
    return dict(pvec=pvec, prow=prow, bsrow=bsrow, gwT=gwT, wr=wr, wi=wi, consts=consts)


_CACHE = {}
PHASES = []


def kernel(**inputs):
    x = np.asarray(inputs["x"], dtype=np.float32)
    shared = prep_params(inputs)
    for k in ("w_in", "w_branch_a", "w_branch_b", "w_branch_c", "w_out", "w_mlp_up", "w_mlp_down"):
        shared[k] = np.ascontiguousarray(np.asarray(inputs[k], dtype=np.float32))
    n_tiles = SEQ // T
    if "nc" not in _CACHE:
        _CACHE["nc"] = build_program(n_tiles)
    nc = _CACHE["nc"]
    in_maps = []
    for core in range(N_CORES):
        b = core % BATCH
        m = dict(shared)
        m["xT"] = np.ascontiguousarray(x[b].T)
        in_maps.append(m)
    res = run_bass_kernel_spmd(nc, in_maps, core_ids=list(range(N_CORES)))
    out = np.empty((BATCH, SEQ, D), np.float32)
    for b in range(BATCH):
        out[b] = res.results[b]["outT"].T
    return out
```

```python
from contextlib import ExitStack

import numpy as np
import concourse.bass as bass
import concourse.mybir as mybir
from concourse.bass_utils import run_bass_kernel_spmd

F32 = mybir.dt.float32
BF16 = mybir.dt.bfloat16
AF = mybir.ActivationFunctionType
ALU = mybir.AluOpType
AX = mybir.AxisListType

D = 1024
SEQ = 4096
BATCH = 4
DEPTH = 2
D_IN = 10256
T = 512
NJ = T // 128
EPS = 1e-6
N_CORES = 8

PV_NMIX, PV_NMLP, PV_LNG, PV_LNB, PV_LCW, PV_LCB, PV_BR, PV_BI, PV_LAM = 0, 8, 16, 24, 32, 64, 72, 80, 88
PV_SCW, PV_SCB, PV_BG, PV_SNG, PV_FIN, NPV = 96, 160, 176, 200, 208, 216
PR_DTB, PR_ALOG, PR_D, NPR = 0, 16, 32, 48
C_U, C_V, C_XB, C_GL, C_Z, C_X, C_B, C_C, C_DT, C_G = 0, 1024, 2048, 3072, 4096, 5120, 6144, 6656, 7168, 7184


class _Op:
    __slots__ = ("eng", "fn", "deps", "signal", "count", "dma_sem", "dma_count")

    def __init__(self, eng, fn, deps):
        self.eng = eng
        self.fn = fn
        self.deps = deps
        self.signal = False
        self.count = None
        self.dma_sem = None
        self.dma_count = None


class Prog:
    ENGS = ("pe", "act", "dve", "pool", "sp")

    def __init__(self, nc):
        self.nc = nc
        self.ops = {e: [] for e in self.ENGS}
        self.last_writer = {}
        self.readers = {}
        self.dma_sems = {}

    def _deps(self, reads, writes):
        deps = []
        for k in reads:
            w = self.last_writer.get(k)
            if w is not None:
                deps.append(w)
        for k in writes:
            w = self.last_writer.get(k)
            if w is not None:
                deps.append(w)
            deps.extend(self.readers.get(k, {}).values())
        return deps

    def _commit(self, op, reads, writes):
        rk = op.eng if op.dma_sem is None else ("dma", id(op))
        for k in reads:
            self.readers.setdefault(k, {})[rk] = op
        for k in writes:
            self.last_writer[k] = op
            self.readers[k] = {}

    def op(self, eng, fn, reads=(), writes=()):
        psr = [k for k in reads if isinstance(k, tuple) and k[0] == "ps"]
        if psr:
            writes = list(writes) + [k for k in psr if k not in writes]
        deps = self._deps(reads, writes)
        o = _Op(eng, fn, deps)
        for d in deps:
            d.signal = True
        self.ops[eng].append(o)
        self._commit(o, reads, writes)
        return o

    def dma(self, eng, fn, sem, reads=(), writes=()):
        deps = self._deps(reads, writes)
        o = _Op(eng, fn, deps)
        for d in deps:
            d.signal = True
        c = self.dma_sems.get(sem, 0) + 16
        self.dma_sems[sem] = c
        o.dma_sem = sem
        o.dma_count = c
        self.ops[eng].append(o)
        self._commit(o, reads, writes)
        return o

    def emit(self, final_waits=()):
        nc = self.nc
        with ExitStack() as es:
            esem = {e: es.enter_context(nc.semaphore("s_" + e)) for e in self.ENGS}
            dsem = {n: es.enter_context(nc.semaphore("d_" + n)) for n in self.dma_sems}
            for o in final_waits:
                o.signal = True
            for e in self.ENGS:
                c = 0
                for o in self.ops[e]:
                    if o.dma_sem is None and o.signal:
                        c += 1
                        o.count = c
            block = es.enter_context(nc.Block())

            def run(e, engine):
                waited = {}
                for o in self.ops[e]:
                    need = {}
                    for d in o.deps:
                        if d.dma_sem is not None:
                            key = ("d", d.dma_sem)
                            val = d.dma_count
                        else:
                            if d.eng == e and e in ("pe", "sp"):
                                continue
                            key = ("e", d.eng)
                            val = d.count
                        if val > need.get(key, 0):
                            need[key] = val
                    for key, val in need.items():
                        if waited.get(key, 0) >= val:
                            continue
                        waited[key] = val
                        s = dsem[key[1]] if key[0] == "d" else esem[key[1]]
                        engine.wait_ge(s, val)
                    ins = o.fn(engine)
                    if o.dma_sem is not None:
                        ins.then_inc(dsem[o.dma_sem], 16)
                    elif o.signal:
                        ins.then_inc(esem[e], 1)
                if e == "sp":
                    for o in final_waits:
                        if o.dma_sem is not None:
                            engine.wait_ge(dsem[o.dma_sem], o.dma_count)
                        else:
                            engine.wait_ge(esem[o.eng], o.count)

            @block.tensor
            def _(eng):
                run("pe", eng)

            @block.scalar
            def _(eng):
                run("act", eng)

            @block.vector
            def _(eng):
                run("dve", eng)

            @block.gpsimd
            def _(eng):
                run("pool", eng)

            @block.sync
            def _(eng):
                run("sp", eng)


class B:
    __slots__ = ("ap", "keys")

    def __init__(self, ap, keys):
        self.ap = ap
        self.keys = list(keys)

    def s(self, ap, keys=None):
        return B(ap, self.keys if keys is None else keys)


def bc(ap, shape, axis):
    return ap.unsqueeze(axis).to_broadcast(list(shape))


def build_program(n_tiles, debug_taps=False):
    nc = bass.Bass("TRN2", target_bir_lowering=False)
    P = Prog(nc)

    def din(name, shape):
        return nc.dram_tensor(name, list(shape), F32, kind="ExternalInput").ap()

    xT = din("xT", [D, SEQ])
    w_in = din("w_in", [DEPTH, D, D_IN])
    w_ba = din("w_branch_a", [DEPTH, D, D])
    w_bb = din("w_branch_b", [DEPTH, D, D])
    w_bc = din("w_branch_c", [DEPTH, D, D])
    w_out = din("w_out", [DEPTH, D, D])
    w_up = din("w_mlp_up", [DEPTH, D, 4 * D])
    w_dn = din("w_mlp_down", [DEPTH, 4 * D, D])
    pvec_d = din("pvec", [DEPTH, 128, NPV])
    prow_d = din("prow", [DEPTH, 128, NPR])
    bsrow_d = din("bsrow", [DEPTH, 1, 1024])
    gwT_d = din("gwT", [DEPTH, 128, 8, 128])
    wr_d = din("wr", [DEPTH, 128, 8, 128])
    wi_d = din("wi", [DEPTH, 128, 8, 128])
    consts_d = din("consts", [128, 3, 128])
    outT = nc.dram_tensor("outT", [D, SEQ], F32, kind="ExternalOutput").ap()
    taps = {}

    with ExitStack() as es:
        def sb(name, shape, dt):
            return es.enter_context(nc.sbuf_tensor("sb_" + name, list(shape), dt))

        consts = sb("consts", [128, 3, 128], F32)
        ident_bf = sb("ident_bf", [128, 128], BF16)
        U_f = consts[:, 1, :]
        Ls_f = consts[:, 2, :]
        ones_bf = sb("ones_bf", [128, 128], BF16)
        ones_f = sb("ones_f", [128, 128], F32)
        cst = sb("cst", [128, 4], F32)
        pv = [sb("pv%d" % l, [128, NPV], F32) for l in range(DEPTH)]
        pr = [sb("pr%d" % l, [128, NPR], F32) for l in range(DEPTH)]
        gw_bf = [sb("gw%d" % l, [128, 8, 128], BF16) for l in range(DEPTH)]
        wr_bf = [sb("wr%d" % l, [128, 8, 128], BF16) for l in range(DEPTH)]
        wi_bf = [sb("wi%d" % l, [128, 8, 128], BF16) for l in range(DEPTH)]
        wdt_bf = [sb("wdt%d" % l, [128, 8, 16], BF16) for l in range(DEPTH)]
        Eg = [sb("Eg%d" % l, [128, 8, 128], F32) for l in range(DEPTH)]
        Dg = [sb("Dg%d" % l, [128, 16, 128], BF16) for l in range(DEPTH)]
        clru = [sb("clru%d" % l, [128, 16], F32) for l in range(DEPTH)]
        arow = [sb("arow%d" % l, [128, 16], F32) for l in range(DEPTH)]
        S = [sb("S%d" % l, [128, 1024], F32) for l in range(DEPTH)]
        S_bf = [sb("Sbf%d" % l, [128, 1024], BF16) for l in range(DEPTH)]
        hst = [sb("hst%d" % l, [128, 8], F32) for l in range(DEPTH)]
        hist_l = [sb("histl%d" % l, [128, 8, 4], F32) for l in range(DEPTH)]
        hist_s = [sb("hists%d" % l, [128, 16, 4], F32) for l in range(DEPTH)]
        h_t = sb("h", [128, 8, T], F32)
        hn_t = sb("hn", [128, 8, T], BF16)
        sqb_t = sb("sqb", [128, 2, T], BF16)
        rstd_t = sb("rstd", [128, T], F32)
        lnt_t = sb("lnt", [128, T], F32)
        uT_t = sb("uT", [128, 8, T], F32)
        up_t = sb("up", [128, 32, T], BF16)
        NW = 4
        wring = [sb("wr_ring%d" % i, [128, 8, 512], BF16) for i in range(NW)]
        ARENA_N = 12288
        arena = sb("arena", [128, ARENA_N], F32)
        small = sb("small", [128, 896], F32)
        ps = es.enter_context(nc.psum_tensor("ps", [128, 4096], F32))
        bsrow = arena

        class Arena:
            def __init__(self):
                self.p = 0

            def reset(self):
                self.p = 0

            def f32(self, n, shape=None):
                a, b = self.p, self.p + n
                assert b <= ARENA_N, "arena overflow"
                self.p = (b + 127) // 128 * 128
                ap = arena[:, a:b]
                if shape is not None:
                    ap = ap.rearrange("p (a b) -> p a b", a=shape[0])
                return B(ap, [("A", g) for g in range(a // 128, (b + 127) // 128)])

            def bf16(self, n, shape=None):
                w = (n + 1) // 2
                a, b = self.p, self.p + w
                assert b <= ARENA_N, "arena overflow"
                self.p = (b + 127) // 128 * 128
                ap = arena[:, a:b].bitcast(BF16)
                if shape is not None:
                    ap = ap.rearrange("p (a b) -> p a b", a=shape[0])
                return B(ap, [("A", g) for g in range(a // 128, (b + 127) // 128)])

        AR = Arena()
        sm_p = [0]

        def sm(n):
            a = sm_p[0]
            sm_p[0] += n
            assert sm_p[0] <= 896
            return B(small[:, a:a + n], [("sm", w) for w in range(a // 8, (a + n - 1) // 8 + 1)])

        bank_p = [0]
        bank_res = set()

        def bank(n=1):
            p = bank_p[0]
            for _ in range(32):
                if p % n:
                    p += n - p % n
                if p + n > 8:
                    p = 0
                if any((p + i) in bank_res for i in range(n)):
                    p += 1
                    continue
                break
            else:
                raise AssertionError("no free psum bank")
            bank_p[0] = (p + n) % 8
            return B(ps[:, p * 512:(p + n) * 512], [("ps", p + i) for i in range(n)])

        def bank_reserve():
            b_ = bank()
            bank_res.add(b_.keys[0][1])
            return b_

        def bank_release(b_):
            bank_res.discard(b_.keys[0][1])

        h = [B(h_t[:, c, :], [("h", c)]) for c in range(8)]
        hn = [B(hn_t[:, c, :], [("hn", c)]) for c in range(8)]
        sqb = [B(sqb_t[:, i, :], [("sqb", i)]) for i in range(2)]
        rstd = B(rstd_t[:], ["rstd"])
        lnt = B(lnt_t[:], ["lnt"])
        uT = [B(uT_t[:, c, :], [("uT", c)]) for c in range(8)]
        up = [B(up_t[:, c, :], [("up", c)]) for c in range(32)]
        ya, yb, yc, mg = up[0:8], up[8:16], up[16:24], up[24:32]
        PARAMS = ["params"]

        def pvc(l, col, n=1):
            return B(pv[l][:, col:col + n], PARAMS)

        def mm(out, lhsT, rhs, start=True, stop=True):
            P.op("pe", lambda e: e.matmul(out.ap, lhsT=lhsT.ap, rhs=rhs.ap, start=start, stop=stop),
                 reads=lhsT.keys + rhs.keys, writes=out.keys)

        def tr(out, in_):
            P.op("pe", lambda e: e.transpose(out=out.ap, in_=in_.ap, identity=ident_bf[:]),
                 reads=in_.keys + PARAMS, writes=out.keys)

        def act(out, in_, func, bias=None, scale=None, accum=None):
            rd = list(in_.keys)
            wr = list(out.keys)
            kw = {}
            if bias is not None:
                if isinstance(bias, B):
                    rd += bias.keys
                    kw["bias"] = bias.ap
                else:
                    kw["bias"] = bias
            if scale is not None:
                if isinstance(scale, B):
                    rd += scale.keys
                    kw["scale"] = scale.ap
                else:
                    kw["scale"] = scale
            if accum is not None:
                wr += accum.keys
                kw["accum_out"] = accum.ap
            P.op("act", lambda e: e.activation(out=out.ap, in_=in_.ap, func=func, **kw), reads=rd, writes=wr)

        def tt(out, in0, in1, op, eng="dve"):
            P.op(eng, lambda e: e.tensor_tensor(out=out.ap, in0=in0.ap, in1=in1.ap, op=op),
                 reads=in0.keys + in1.keys, writes=out.keys)

        def ts(out, in0, s1, op0, s2=None, op1=None, eng="dve"):
            rd = list(in0.keys)
            a1 = s1
            a2 = s2
            if isinstance(s1, B):
                rd += s1.keys
                a1 = s1.ap
            if isinstance(s2, B):
                rd += s2.keys
                a2 = s2.ap
            if op1 is None:
                P.op(eng, lambda e: e.tensor_scalar(out=out.ap, in0=in0.ap, scalar1=a1, scalar2=None, op0=op0),
                     reads=rd, writes=out.keys)
            else:
                P.op(eng, lambda e: e.tensor_scalar(out=out.ap, in0=in0.ap, scalar1=a1, scalar2=a2, op0=op0, op1=op1),
                     reads=rd, writes=out.keys)

        def stt(out, in0, scalar, in1, op0, op1, eng="dve"):
            rd = in0.keys + in1.keys
            a = scalar
            if isinstance(scalar, B):
                rd = rd + scalar.keys
                a = scalar.ap
            P.op(eng, lambda e: e.scalar_tensor_tensor(out=out.ap, in0=in0.ap, scalar=a, in1=in1.ap, op0=op0, op1=op1),
                 reads=rd, writes=out.keys)

        def cp(out, in_, eng="dve"):
            if eng == "act":
                P.op("act", lambda e: e.activation(out=out.ap, in_=in_.ap, func=AF.Copy), reads=in_.keys, writes=out.keys)
            else:
                P.op(eng, lambda e: e.tensor_copy(out=out.ap, in_=in_.ap), reads=in_.keys, writes=out.keys)

        def memset(buf, val, eng="dve"):
            P.op(eng, lambda e: e.memset(buf.ap, val), writes=buf.keys)

        def red_sum(out, in_):
            P.op("dve", lambda e: e.tensor_reduce(out=out.ap, in_=in_.ap, axis=AX.X, op=ALU.add),
                 reads=in_.keys, writes=out.keys)

        def scan(out, d0, d1, init):
            P.op("dve", lambda e: e.tensor_tensor_scan(out=out.ap, data0=d0.ap, data1=d1.ap, initial=init.ap,
                                                       op0=ALU.mult, op1=ALU.add),
                 reads=d0.keys + d1.keys + init.keys, writes=out.keys)

        def tap(name, buf, shape):
            if not debug_taps:
                return
            t = nc.dram_tensor("tap_" + name, list(shape), buf.ap.dtype, kind="ExternalOutput").ap()
            taps[name] = P.dma("sp", lambda e: e.dma_start(out=t, in_=buf.ap), "tap_" + name, reads=buf.keys)

        def w3(w, l):
            return w[l].rearrange("(kc p) e -> p kc e", p=128)

        def wsched(l):
            win3 = w3(w_in, l)
            sch = []
            for c0 in (C_V, C_V + 512, C_U, C_U + 512, C_XB, C_GL, C_XB + 512, C_GL + 512, C_X, C_X + 512, C_B, C_C,
                       C_Z, C_Z + 512):
                sch.append((win3, 0, c0))
            for k, wb in enumerate((w_ba, w_bb, w_bc)):
                for hf in range(2):
                    sch.append((win3, 0, C_G + k * 1024 + hf * 512))
                    sch.append((w3(wb, l), 0, hf * 512))
            for hf in range(2):
                sch.append((w3(w_out, l), 0, hf * 512))
            for f4 in range(8):
                sch.append((w3(w_up, l), 0, f4 * 512))
            for hf in range(2):
                for q in range(4):
                    sch.append((w3(w_dn, l), q * 8, hf * 512))
            return sch

        WSTREAM = []
        for ti_ in range(n_tiles):
            for l_ in range(DEPTH):
                WSTREAM.extend(wsched(l_))
        PF = NW - 1
        wcur = [0]
        wissued = [0]

        def _issue(i):
            src3, kc0, c0 = WSTREAM[i]
            si = i % NW
            slot = wring[si]
            for hf in range(2):
                a, b = hf * 4, hf * 4 + 4
                dst = slot[:, a:b, :]
                src = src3[:, kc0 + a:kc0 + b, c0:c0 + 512]
                P.dma("pool", (lambda dst, src: (lambda e: e.dma_start(out=dst, in_=src)))(dst, src),
                      "w%d_%d" % (si, hf), writes=[("w", si, hf)])

        def wload(src3, kc0, c0, n=512):
            i = wcur[0]
            wcur[0] += 1
            assert WSTREAM[i][1] == kc0 and WSTREAM[i][2] == c0, ("weight schedule mismatch", i, kc0, c0, WSTREAM[i][1:])
            while wissued[0] < min(len(WSTREAM), i + PF + 1):
                _issue(wissued[0])
                wissued[0] += 1
            si = i % NW
            return wring[si], [("w", si, 0), ("w", si, 1)]

        def wl(slot_keys, kc, c0, n):
            slot, keys = slot_keys
            return B(slot[:, kc, c0:c0 + n], [keys[kc // 4]])

        P.dma("sp", lambda e: e.dma_start(out=consts[:], in_=consts_d), "ld_c", writes=PARAMS)
        for l in range(DEPTH):
            P.dma("sp", (lambda l: (lambda e: e.dma_start(out=pv[l][:], in_=pvec_d[l])))(l), "ld_pv%d" % l, writes=PARAMS)
            P.dma("sp", (lambda l: (lambda e: e.dma_start(out=pr[l][:], in_=prow_d[l])))(l), "ld_pr%d" % l, writes=PARAMS)
            P.dma("sp", (lambda l: (lambda e: e.dma_start(out=bsrow[0:1, l * 1024:(l + 1) * 1024], in_=bsrow_d[l])))(l),
                  "ld_bs%d" % l, writes=PARAMS)
            P.dma("pool", (lambda l: (lambda e: e.dma_start(out=gw_bf[l][:], in_=gwT_d[l])))(l), "ld_gw%d" % l, writes=PARAMS)
            P.dma("pool", (lambda l: (lambda e: e.dma_start(out=wr_bf[l][:], in_=wr_d[l])))(l), "ld_wr%d" % l, writes=PARAMS)
            P.dma("pool", (lambda l: (lambda e: e.dma_start(out=wi_bf[l][:], in_=wi_d[l])))(l), "ld_wi%d" % l, writes=PARAMS)
            P.dma("pool", (lambda l: (lambda e: e.dma_start(out=wdt_bf[l][:], in_=w3(w_in, l)[:, :, C_DT:C_DT + 16])))(l),
                  "ld_wdt%d" % l, writes=PARAMS)
        PB = B(None, PARAMS)
        P.op("dve", lambda e: e.memset(ones_bf[:], 1.0), reads=PARAMS, writes=PARAMS)
        P.op("dve", lambda e: e.memset(ones_f[:], 1.0), reads=PARAMS, writes=PARAMS)
        P.op("dve", lambda e: e.memset(cst[:, 0:1], EPS), reads=PARAMS, writes=PARAMS)
        P.op("dve", lambda e: e.memset(cst[:, 1:2], 1.0), reads=PARAMS, writes=PARAMS)
        P.op("dve", lambda e: e.tensor_copy(out=ident_bf[:], in_=consts[:, 0, :]), reads=PARAMS, writes=PARAMS)
        eps_b = B(cst[:, 0:1], PARAMS)
        one_b = B(cst[:, 1:2], PARAMS)
        for l in range(DEPTH):
            for t_, n_ in ((S[l], 1024), (hst[l], 8)):
                P.op("dve", (lambda t_: (lambda e: e.memset(t_[:], 0.0)))(t_), reads=PARAMS, writes=PARAMS)
            P.op("dve", (lambda l: (lambda e: e.memset(S_bf[l][:], 0.0)))(l), reads=PARAMS, writes=PARAMS)
            P.op("dve", (lambda l: (lambda e: e.memset(hist_l[l][:], 0.0)))(l), reads=PARAMS, writes=PARAMS)
            P.op("dve", (lambda l: (lambda e: e.memset(hist_s[l][:], 0.0)))(l), reads=PARAMS, writes=PARAMS)
            P.op("dve", (lambda l: (lambda e: e.tensor_tensor(out=gw_bf[l][:], in0=gw_bf[l][:], in1=bc(U_f, [128, 8, 128], 1),
                                                               op=ALU.mult)))(l), reads=PARAMS, writes=PARAMS)
            P.op("act", (lambda l: (lambda e: e.activation(out=clru[l][:, 0:8], in_=pv[l][:, PV_LAM:PV_LAM + 8], func=AF.Exp,
                                                           scale=-1.0)))(l), reads=PARAMS, writes=PARAMS)
            P.op("act", (lambda l: (lambda e: e.activation(out=clru[l][:, 0:8], in_=clru[l][:, 0:8], func=AF.Ln,
                                                           bias=cst[:, 1:2], scale=1.0)))(l), reads=PARAMS, writes=PARAMS)
            P.op("dve", (lambda l: (lambda e: e.tensor_scalar(out=clru[l][:, 8:16], in0=clru[l][:, 0:8], scalar1=-16.0,
                                                              scalar2=None, op0=ALU.mult)))(l), reads=PARAMS, writes=PARAMS)
            P.op("dve", (lambda l: (lambda e: e.tensor_scalar(out=clru[l][:, 0:8], in0=clru[l][:, 0:8], scalar1=-8.0,
                                                              scalar2=None, op0=ALU.mult)))(l), reads=PARAMS, writes=PARAMS)
            P.op("act", (lambda l: (lambda e: e.activation(out=arow[l][:], in_=pr[l][:, PR_ALOG:PR_ALOG + 16],
                                                           func=AF.Exp)))(l), reads=PARAMS, writes=PARAMS)
            P.op("dve", (lambda l: (lambda e: e.tensor_scalar(out=arow[l][:], in0=arow[l][:], scalar1=-1.0, scalar2=None,
                                                              op0=ALU.mult)))(l), reads=PARAMS, writes=PARAMS)
            P.op("dve", (lambda l: (lambda e: e.tensor_tensor(out=Dg[l][:], in0=bc(consts[:, 0, :], [128, 16, 128], 1),
                                                               in1=bc(pr[l][:, PR_D:PR_D + 16], [128, 16, 128], 2),
                                                               op=ALU.mult)))(l), reads=PARAMS, writes=PARAMS)
            bkA = bank(2)
            bkB = bank(2)
            for g in range(8):
                P.op("pe", (lambda l, g, bk: (lambda e: e.matmul(bk.ap[:, g * 128:(g + 1) * 128], lhsT=ones_bf[:],
                                                                 rhs=gw_bf[l][:, g, :], start=True, stop=True)))(l, g, bkA),
                     reads=PARAMS, writes=bkA.keys)
            for hf in range(2):
                P.op("pe", (lambda l, hf, bk: (lambda e: e.matmul(bk.ap[:, hf * 512:(hf + 1) * 512], lhsT=ones_f[0:1, :],
                                                                  rhs=bsrow[0:1, l * 1024 + hf * 512:l * 1024 + (hf + 1) * 512],
                                                                  start=True, stop=True)))(l, hf, bkB),
                     reads=PARAMS, writes=bkB.keys)
            P.op("act", (lambda l, bk: (lambda e: e.activation(out=Eg[l][:], in_=bk.ap.rearrange("p (a b) -> p a b", a=8),
                                                               func=AF.Copy)))(l, bkB), reads=bkB.keys + PARAMS, writes=PARAMS)
            for g in range(8):
                P.op("dve", (lambda l, g, bk: (lambda e: e.scalar_tensor_tensor(
                    out=Eg[l][:, g, :], in0=bk.ap[:, g * 128:(g + 1) * 128], scalar=pv[l][:, PV_LNB + g:PV_LNB + g + 1],
                    in1=Eg[l][:, g, :], op0=ALU.mult, op1=ALU.add)))(l, g, bkA), reads=bkA.keys + PARAMS, writes=PARAMS)

        nsum = {"bk": None}

        def sumsq_start():
            nsum["bk"] = bank_reserve()
            nsum["pend"] = []

        def sumsq_chunk(c):
            act(sqb[c % 2], h[c], AF.Square)
            nsum["pend"].append(c)
            sumsq_flush(keep=1)

        def sumsq_flush(keep=0):
            while len(nsum["pend"]) > keep:
                c = nsum["pend"].pop(0)
                mm(nsum["bk"], B(ones_bf[:], PARAMS), sqb[c % 2], start=(c == 0), stop=(c == 7))

        def rmsnorm(l, gcol, out=None):
            out = hn if out is None else out
            if nsum["bk"] is None:
                sumsq_start()
                for c in range(8):
                    sumsq_chunk(c)
            sumsq_flush()
            bk = nsum["bk"]
            act(lnt, bk, AF.Ln, bias=eps_b, scale=1.0 / D)
            act(rstd, lnt, AF.Exp, scale=-0.5)
            bank_release(bk)
            nsum["bk"] = None
            for c in range(8):
                stt(out[c], h[c], pvc(l, gcol + c), rstd, ALU.mult, ALU.mult)

        def conv4(dst, xbuf, wcol, bcol, l, acc):
            ts(acc, xbuf.s(xbuf.ap[:, 0:T]), pvc(l, wcol + 0), ALU.mult, pvc(l, bcol), ALU.add)
            for k in (1, 2):
                stt(acc, xbuf.s(xbuf.ap[:, k:k + T]), pvc(l, wcol + k), acc, ALU.mult, ALU.add)
            stt(dst, xbuf.s(xbuf.ap[:, 3:3 + T]), pvc(l, wcol + 3), acc, ALU.mult, ALU.add)

        import os
        STOP = int(os.environ.get("MK_STOP", "99"))

        def mark(name):
            PHASES.append((name, len(P.ops["pe"])))

        def layer(l, ti):
            win3 = w3(w_in, l)
            mark("norm1")
            if STOP < 1:
                return
            rmsnorm(l, PV_NMIX)
            if STOP < 2:
                return
            mark('A')
            AR.reset()
            vtok = [AR.f32(1024) for _ in range(NJ)]
            vn = [AR.bf16(1024) for _ in range(NJ)]
            tmpA = AR.f32(T)
            junk = AR.bf16(1024)
            for hf in range(2):
                wt = wload(win3, 0, C_V + hf * 512)
                for j in range(NJ):
                    bk = bank()
                    for kc in range(8):
                        mm(bk, hn[kc].s(hn[kc].ap[:, j * 128:(j + 1) * 128]), wl(wt, kc, 0, 512), kc == 0, kc == 7)
                    act(vtok[j].s(vtok[j].ap[:, hf * 512:(hf + 1) * 512]), bk, AF.Gelu_apprx_tanh)
            def sm1():
                b_ = sm(8)
                return b_.s(b_.ap[:, 0:1])

            st_ = [[sm1() for _ in range(6)] for j in range(NJ)]
            for j in range(NJ):
                memset(st_[j][1], 0.0)
            for j in range(NJ):
                red_sum(st_[j][0], vtok[j])
                act(junk, vtok[j], AF.Square, accum=st_[j][1])
            for j in range(NJ):
                ts(st_[j][2], st_[j][0], 1.0 / 1024, ALU.mult)
            for j in range(NJ):
                tt(st_[j][4], st_[j][2], st_[j][2], ALU.mult)
            for j in range(NJ):
                stt(st_[j][3], st_[j][1], 1.0 / 1024, st_[j][4], ALU.mult, ALU.subtract)
            for j in range(NJ):
                act(st_[j][3], st_[j][3], AF.Ln, bias=eps_b, scale=1.0)
            for j in range(NJ):
                act(st_[j][5], st_[j][3], AF.Exp, scale=-0.5)
            for j in range(NJ):
                ts(vn[j], vtok[j], st_[j][2], ALU.subtract, st_[j][5], ALU.mult)
            sm_p[0] = 0
            for hf in range(2):
                wt = wload(win3, 0, C_U + hf * 512)
                for m4 in range(4):
                    m = hf * 4 + m4
                    bk = bank()
                    for kc in range(8):
                        mm(bk, wl(wt, kc, m4 * 128, 128), hn[kc], kc == 0, kc == 7)
                    act(uT[m], bk, AF.Gelu_apprx_tanh)
            for g in range(8):
                bk = bank()
                for j in range(NJ):
                    mm(bk.s(bk.ap[:, j * 128:(j + 1) * 128]), vn[j].s(vn[j].ap[:, g * 128:(g + 1) * 128]),
                       B(gw_bf[l][:, g, :], PARAMS))
                stt(tmpA.s(tmpA.ap.rearrange("p (a b) -> p a b", a=NJ)), bk.s(bk.ap.rearrange("p (a b) -> p a b", a=NJ)),
                    pvc(l, PV_LNG + g), B(bc(Eg[l][:, g, :], [128, NJ, 128], 1), PARAMS), ALU.mult, ALU.add)
                tt(ya[g], tmpA, uT[g], ALU.mult)
            if STOP < 3:
                return
            mark('B')
            AR.reset()
            hseq4 = AR.f32(4 * T, shape=(4, T))
            xbufs = [AR.f32(T + 4) for _ in range(4)]
            accs = [AR.f32(T) for _ in range(4)]
            xcs = [AR.f32(T) for _ in range(4)]
            xc_bfs = [AR.bf16(T) for _ in range(2)]
            r4 = [AR.f32(T) for _ in range(4)]
            i2 = [AR.f32(T) for _ in range(2)]
            a4 = accs
            gg = xbufs[0].s(xbufs[0].ap[:, 0:T])
            for hf in range(2):
                wt = wload(win3, 0, C_XB + hf * 512)
                bks = []
                for m4 in range(4):
                    bk = bank()
                    for kc in range(8):
                        mm(bk, wl(wt, kc, m4 * 128, 128), hn[kc], kc == 0, kc == 7)
                    bks.append(bk)

                def stA(m4):
                    hd = hf * 4 + m4
                    xbuf, acc = xbufs[m4], accs[m4]
                    hl = B(hist_l[l][:, hd, 0:3], [("hl", l, hd)])
                    cp(xbuf.s(xbuf.ap[:, 0:3]), hl)
                    cp(hl, bks[m4].s(bks[m4].ap[:, T - 3:T]))
                    cp(xbuf.s(xbuf.ap[:, 3:3 + T]), bks[m4], eng="act")
                    act(acc, bks[m4], AF.Identity, bias=pvc(l, PV_LCB + hd), scale=pvc(l, PV_LCW + hd * 4 + 3))

                def stB2(ms):
                    for k in range(3):
                        for m4 in ms:
                            hd = hf * 4 + m4
                            dst = xcs[m4] if k == 2 else accs[m4]
                            stt(dst, xbufs[m4].s(xbufs[m4].ap[:, k:k + T]), pvc(l, PV_LCW + hd * 4 + k), accs[m4],
                                ALU.mult, ALU.add)
                    out = {}
                    for m4 in ms:
                        cp(xc_bfs[m4 % 2], xcs[m4], eng="act")
                    for m4 in ms:
                        hd = hf * 4 + m4
                        bkr = bank()
                        mm(bkr, B(wr_bf[l][:, hd, :], PARAMS), xc_bfs[m4 % 2])
                        bki = bank()
                        mm(bki, B(wi_bf[l][:, hd, :], PARAMS), xc_bfs[m4 % 2])
                        out[m4] = (bkr, bki)
                    return out

                ri = {}
                for m4 in range(4):
                    stA(m4)
                ri.update(stB2((0, 1)))
                ri.update(stB2((2, 3)))
                for m4 in range(4):
                    hd = hf * 4 + m4
                    bkr, bki = ri[m4]
                    act(r4[m4], bkr, AF.Sigmoid, bias=pvc(l, PV_BR + hd), scale=1.0)
                    act(i2[m4 % 2], bki, AF.Sigmoid, bias=pvc(l, PV_BI + hd), scale=1.0)
                    tt(xcs[m4], xcs[m4], i2[m4 % 2], ALU.mult)
                for m4 in range(4):
                    hd = hf * 4 + m4
                    act(a4[m4], r4[m4], AF.Exp, scale=B(clru[l][:, hd:hd + 1], PARAMS))
                for m4 in range(4):
                    hd = hf * 4 + m4
                    act(r4[m4], r4[m4], AF.Exp, scale=B(clru[l][:, 8 + hd:9 + hd], PARAMS))
                for m4 in range(4):
                    act(r4[m4], r4[m4], AF.Ln, bias=one_b, scale=-1.0)
                for m4 in range(4):
                    act(r4[m4], r4[m4], AF.Exp, scale=0.5)
                for m4 in range(4):
                    tt(xcs[m4], xcs[m4], r4[m4], ALU.mult)
                for m4 in range(4):
                    hd = hf * 4 + m4
                    hq = hseq4.s(hseq4.ap[:, m4, :], hseq4.keys[m4 * 4:(m4 + 1) * 4])
                    hs = B(hst[l][:, hd:hd + 1], [("hst", l, hd)])
                    scan(hq, a4[m4], xcs[m4], hs)
                    cp(hs, hq.s(hq.ap[:, T - 1:T]))
                wt = wload(win3, 0, C_GL + hf * 512)
                for m4 in range(4):
                    hd = hf * 4 + m4
                    bk = bank()
                    for kc in range(8):
                        mm(bk, wl(wt, kc, m4 * 128, 128), hn[kc], kc == 0, kc == 7)
                    act(gg, bk, AF.Gelu_apprx_tanh)
                    hq = hseq4.s(hseq4.ap[:, m4, :], hseq4.keys[m4 * 4:(m4 + 1) * 4])
                    tt(yb[hd], gg, hq, ALU.mult)
            if STOP < 4:
                return
            mark('Cconv')
            AR.reset()
            xsT = mg
            BT = AR.bf16(4 * T, shape=(4, T))
            CT = AR.bf16(4 * T, shape=(4, T))
            arena_mark = AR.p
            xbufs = [AR.f32(T + 4) for _ in range(4)]
            accs = [AR.f32(T) for _ in range(4)]
            xcvs = [AR.f32(T) for _ in range(4)]
            cbks = {}
            ctiles = (C_X, C_X + 512, C_B, C_C)
            nproj = [0]

            def cproj_upto(ch):
                while nproj[0] <= min(ch, 15):
                    wt = wload(win3, 0, ctiles[nproj[0] // 4])
                    for m4 in range(4):
                        bk = bank()
                        for kc in range(8):
                            mm(bk, wl(wt, kc, m4 * 128, 128), hn[kc], kc == 0, kc == 7)
                        cbks[nproj[0]] = bk
                        nproj[0] += 1

            def cstA(ch):
                xbuf, acc = xbufs[ch % 4], accs[ch % 4]
                hl = B(hist_s[l][:, ch, 0:3], [("hs", l, ch)])
                cp(xbuf.s(xbuf.ap[:, 0:3]), hl)
                cp(hl, cbks[ch].s(cbks[ch].ap[:, T - 3:T]))
                cp(xbuf.s(xbuf.ap[:, 3:3 + T]), cbks[ch], eng="act")
                act(acc, cbks[ch], AF.Identity, bias=pvc(l, PV_SCB + ch), scale=pvc(l, PV_SCW + ch * 4 + 3))

            def cstB2(chs):
                for k in range(3):
                    for ch in chs:
                        dst = xcvs[ch % 4] if k == 2 else accs[ch % 4]
                        stt(dst, xbufs[ch % 4].s(xbufs[ch % 4].ap[:, k:k + T]), pvc(l, PV_SCW + ch * 4 + k), accs[ch % 4],
                            ALU.mult, ALU.add)
                for ch in chs:
                    if ch < 8:
                        dst = xsT[ch]
                    elif ch < 12:
                        dst = BT.s(BT.ap[:, ch - 8, :])
                    else:
                        dst = CT.s(CT.ap[:, ch - 12, :])
                    act(dst, xcvs[ch % 4], AF.Silu)

            cproj_upto(1)
            cstA(0)
            cstA(1)
            for p in range(8):
                if p + 1 < 8:
                    cproj_upto(2 * p + 3)
                    cstA(2 * p + 2)
                    cstA(2 * p + 3)
                cstB2((2 * p, 2 * p + 1))
            mark('Cdt')
            sm_p[0] = 0
            NH4 = NJ * 16
            dtx, ax, ex, lgx, dt_, adt, cs_sb, ecs, tmc, ds, cd, dtds = [sm(NH4) for _ in range(12)]
            psd = bank()
            for j in range(NJ):
                for kc in range(8):
                    mm(psd.s(psd.ap[:, j * 16:(j + 1) * 16]), hn[kc].s(hn[kc].ap[:, j * 128:(j + 1) * 128]),
                       B(wdt_bf[l][:, kc, :], PARAMS), kc == 0, kc == 7)
            v4j = lambda b_: b_.s(b_.ap.rearrange("p (a b) -> p a b", a=NJ))
            tt(v4j(dtx), psd.s(psd.ap[:, 0:NH4].rearrange("p (a b) -> p a b", a=NJ)),
               B(bc(pr[l][:, PR_DTB:PR_DTB + 16], [128, NJ, 16], 1), PARAMS), ALU.add)
            ts(ax, dtx, -1.0, ALU.mult)
            tt(ax, ax, dtx, ALU.max)
            act(ex, ax, AF.Exp, scale=-1.0)
            act(lgx, ex, AF.Ln, bias=one_b, scale=1.0)
            ts(dt_, dtx, 0.0, ALU.max)
            tt(dt_, dt_, lgx, ALU.add)
            tt(v4j(adt), v4j(dt_), B(bc(arow[l][:], [128, NJ, 16], 1), PARAMS), ALU.mult)
            mark('Cz')
            szall = [B(uT_t[:, 2 * j:2 * j + 2, :].rearrange("p a b -> p (a b)"), [("uT", 2 * j), ("uT", 2 * j + 1)])
                     for j in range(NJ)]
            for hf in range(2):
                wt = wload(win3, 0, C_Z + hf * 512)
                for j in range(NJ):
                    bk = bank()
                    for kc in range(8):
                        mm(bk, hn[kc].s(hn[kc].ap[:, j * 128:(j + 1) * 128]), wl(wt, kc, 0, 512), kc == 0, kc == 7)
                    act(szall[j].s(szall[j].ap[:, hf * 512:(hf + 1) * 512]), bk, AF.Silu)
            mark('Cdt2')
            for j in range(NJ):
                mm(psd.s(psd.ap[:, NH4 + j * 16:NH4 + (j + 1) * 16]), B(U_f, PARAMS), adt.s(adt.ap[:, j * 16:(j + 1) * 16]))
                mm(psd.s(psd.ap[:, 2 * NH4 + j * 16:2 * NH4 + (j + 1) * 16]), B(ones_f[:], PARAMS),
                   adt.s(adt.ap[:, j * 16:(j + 1) * 16]))
            cp(cs_sb, psd.s(psd.ap[:, NH4:2 * NH4]))
            act(ecs, cs_sb, AF.Exp)
            tt(tmc, psd.s(psd.ap[:, 2 * NH4:3 * NH4]), cs_sb, ALU.subtract)
            act(ds, tmc, AF.Exp)
            act(cd, psd.s(psd.ap[:, 2 * NH4:3 * NH4]), AF.Exp)
            tt(dtds, dt_, ds, ALU.mult)
            AR.p = arena_mark
            Lb = AR.f32(16 * 128, shape=(16, 128))
            Lq = [Lb.s(Lb.ap[:, q * 4:(q + 1) * 4, :], Lb.keys[q * 4:(q + 1) * 4]) for q in range(4)]
            cbm = AR.f32(4 * 128, shape=(4, 128))
            FB = []
            for _ in range(2):
                FB.append(dict(Mb=AR.bf16(16 * 128, shape=(16, 128)), xs_bf=AR.bf16(1024), xdt=AR.bf16(1024),
                               xdd=AR.bf16(1024), Btok=AR.bf16(512)))
            ytmp = AR.f32(1024)
            yn = AR.bf16(1024)
            ss = sm(4)
            rs4 = sm(4)
            v3 = lambda b_: b_.s(b_.ap.rearrange("p (a b) -> p a b", a=16))
            v4 = lambda b_: b_.s(b_.ap.rearrange("p (a b) -> p a b", a=4))

            def hs(b_, j):
                return b_.s(b_.ap[:, j * 16:(j + 1) * 16])

            def front(j):
                js = slice(j * 128, (j + 1) * 128)
                f = FB[j % 2]
                adt_j, dt_j, dtds_j = hs(adt, j), hs(dt_, j), hs(dtds, j)
                tt(Lb, B(bc(Ls_f, [128, 16, 128], 1), PARAMS), adt_j.s(bc(adt_j.ap, [128, 16, 128], 2)), ALU.mult, eng="pool")
                psc = bank()
                for g in range(4):
                    mm(psc.s(psc.ap[:, g * 128:(g + 1) * 128]), BT.s(BT.ap[:, g, js]), CT.s(CT.ap[:, g, js]))
                tt(cbm, psc.s(psc.ap.rearrange("p (a b) -> p a b", a=4)), B(bc(U_f, [128, 4, 128], 1), PARAMS), ALU.mult)
                for q in range(4):
                    psg = bank()
                    for r in range(4):
                        hd = q * 4 + r
                        mm(psg.s(psg.ap[:, r * 128:(r + 1) * 128]), Lq[q].s(Lb.ap[:, hd, :]), B(U_f, PARAMS))
                    act(Lq[q], psg.s(psg.ap.rearrange("p (a b) -> p a b", a=4)), AF.Exp)
                    tt(f["Mb"].s(f["Mb"].ap[:, q * 4:(q + 1) * 4, :]), Lq[q], cbm.s(bc(cbm.ap[:, q, :], [128, 4, 128], 1)),
                       ALU.mult, eng="pool")
                pst = bank()
                pstb = pst.s(pst.ap.bitcast(BF16))
                for kc in range(8):
                    tr(pstb.s(pstb.ap[:, kc * 128:(kc + 1) * 128]), xsT[kc].s(xsT[kc].ap[:, js]))
                cp(f["xs_bf"], pstb, eng="act")
                tt(v3(f["xdt"]), v3(pstb), dt_j.s(bc(dt_j.ap, [128, 16, 64], 2)), ALU.mult)
                tt(v3(f["xdd"]), v3(pstb), dtds_j.s(bc(dtds_j.ap, [128, 16, 64], 2)), ALU.mult)
                psb = bank()
                psbb = psb.s(psb.ap.bitcast(BF16))
                for g in range(4):
                    tr(psbb.s(psbb.ap[:, g * 128:(g + 1) * 128]), BT.s(BT.ap[:, g, js]))
                cp(f["Btok"], psbb.s(psbb.ap[:, 0:512]), eng="act")

            def back(j):
                js = slice(j * 128, (j + 1) * 128)
                f = FB[j % 2]
                Mb, xs_bf, xdt, xdd, Btok = f["Mb"], f["xs_bf"], f["xdt"], f["xdd"], f["Btok"]
                ecs_j, cd_j = hs(ecs, j), hs(cd, j)
                yo = bank(2)
                for g in range(4):
                    mm(yo.s(yo.ap[:, g * 256:(g + 1) * 256]), CT.s(CT.ap[:, g, js]),
                       B(S_bf[l][:, g * 256:(g + 1) * 256], [("Sbf", l)]))
                yd = bank(2)
                for hd in range(16):
                    o_ = yd.s(yd.ap[:, hd * 64:(hd + 1) * 64])
                    mm(o_, Mb.s(Mb.ap[:, hd, :]), xdt.s(xdt.ap[:, hd * 64:(hd + 1) * 64]), True, False)
                    mm(o_, B(Dg[l][:, hd, :], PARAMS), xs_bf.s(xs_bf.ap[:, hd * 64:(hd + 1) * 64]), False, True)
                st = bank(2)
                for g in range(4):
                    mm(st.s(st.ap[:, g * 256:(g + 1) * 256]), Btok.s(Btok.ap[:, g * 128:(g + 1) * 128]),
                       xdd.s(xdd.ap[:, g * 256:(g + 1) * 256]))
                tt(v3(ytmp), v3(yo), ecs_j.s(bc(ecs_j.ap, [128, 16, 64], 2)), ALU.mult)
                tt(ytmp, yd, ytmp, ALU.add)
                Sb = B(S[l][:], [("S", l)])
                tt(v3(Sb), v3(Sb), cd_j.s(bc(cd_j.ap, [128, 16, 64], 2)), ALU.mult)
                tt(Sb, st, Sb, ALU.add)
                cp(B(S_bf[l][:], [("Sbf", l)]), Sb, eng="act")
                tt(ytmp, ytmp, szall[j], ALU.mult)
                memset(ss, 0.0)
                for g in range(4):
                    act(yn.s(yn.ap[:, g * 256:(g + 1) * 256]), ytmp.s(ytmp.ap[:, g * 256:(g + 1) * 256]), AF.Square,
                        accum=ss.s(ss.ap[:, g:g + 1]))
                act(rs4, ss, AF.Ln, bias=eps_b, scale=1.0 / 256)
                act(rs4, rs4, AF.Exp, scale=-0.5)
                tt(v4(yn), v4(ytmp), rs4.s(bc(rs4.ap, [128, 4, 256], 2)), ALU.mult)

            def back2(j):
                js = slice(j * 128, (j + 1) * 128)
                pyt = bank()
                pytb = pyt.s(pyt.ap.bitcast(BF16))
                for kc in range(8):
                    tr(pytb.s(pytb.ap[:, kc * 128:(kc + 1) * 128]), yn.s(yn.ap[:, kc * 128:(kc + 1) * 128]))
                for kc in range(8):
                    act(yc[kc].s(yc[kc].ap[:, js]), pytb.s(pytb.ap[:, kc * 128:(kc + 1) * 128]), AF.Copy,
                        scale=pvc(l, PV_SNG + kc))

            mark('Cchunks')
            front(0)
            front(1)
            back(0)
            for j in range(1, NJ):
                if j + 1 < NJ:
                    front(j + 1)
                back2(j - 1)
                back(j)
            back2(NJ - 1)
            if STOP < 5:
                return
            mark('merge')
            AR.reset()
            sm_p[0] = 0
            g4 = AR.f32(4 * T, shape=(4, T))
            tmpG2 = [AR.f32(T) for _ in range(2)]
            accm = uT
            for k, (wb, ybr) in enumerate(((w_ba, ya), (w_bb, yb), (w_bc, yc))):
                wb3 = w3(wb, l)
                for hf in range(2):
                    wt = wload(win3, 0, C_G + k * 1024 + hf * 512)
                    for m4 in range(4):
                        m = hf * 4 + m4
                        bk = bank()
                        for kc in range(8):
                            mm(bk, wl(wt, kc, m4 * 128, 128), hn[kc], kc == 0, kc == 7)
                        act(g4.s(g4.ap[:, m4, :], g4.keys[m4 * 4:(m4 + 1) * 4]), bk, AF.Sigmoid, bias=pvc(l, PV_BG + k * 8 + m), scale=1.0)
                    wt = wload(wb3, 0, hf * 512)
                    for pr_ in range(2):
                        bkp = []
                        for m4 in (2 * pr_, 2 * pr_ + 1):
                            bk = bank()
                            for kc in range(8):
                                mm(bk, wl(wt, kc, m4 * 128, 128), ybr[kc], kc == 0, kc == 7)
                            bkp.append((m4, bk))
                        for m4, bk in bkp:
                            m = hf * 4 + m4
                            gm = g4.s(g4.ap[:, m4, :], g4.keys[m4 * 4:(m4 + 1) * 4])
                            if k == 0:
                                tt(accm[m], bk, gm, ALU.mult)
                            else:
                                tt(tmpG2[m4 % 2], bk, gm, ALU.mult)
                        if k > 0:
                            for m4, bk in bkp:
                                m = hf * 4 + m4
                                tt(accm[m] if k == 1 else mg[m], accm[m], tmpG2[m4 % 2], ALU.add)
            if STOP < 6:
                return
            mark('outproj')
            wo3 = w3(w_out, l)
            sumsq_start()
            for hf in range(2):
                wt = wload(wo3, 0, hf * 512)
                for m4 in range(4):
                    m = hf * 4 + m4
                    bk = bank()
                    for kc in range(8):
                        mm(bk, wl(wt, kc, m4 * 128, 128), mg[kc], kc == 0, kc == 7)
                    tt(h[m], bk, h[m], ALU.add)
                    sumsq_chunk(m)
            if STOP < 7:
                return
            mark('mlp_up')
            rmsnorm(l, PV_NMLP)
            AR.reset()
            rl = [AR.f32(T) for _ in range(2)]
            wu3 = w3(w_up, l)
            for f4 in range(8):
                wt = wload(wu3, 0, f4 * 512)
                for m4 in range(4):
                    f = f4 * 4 + m4
                    bk = bank()
                    for kc in range(8):
                        mm(bk, wl(wt, kc, m4 * 128, 128), hn[kc], kc == 0, kc == 7)
                    act(rl[f % 2], bk, AF.Relu)
                    tt(up[f], rl[f % 2], rl[f % 2], ALU.mult)
            mark('mlp_down')
            wd3 = w3(w_dn, l)
            sumsq_start()
            for hf in range(2):
                bks = [bank() for _ in range(4)]
                for q in range(4):
                    wt = wload(wd3, q * 8, hf * 512)
                    for kc in range(8):
                        for m4 in range(4):
                            mm(bks[m4], wl(wt, kc, m4 * 128, 128), up[q * 8 + kc], (q == 0 and kc == 0), (q == 3 and kc == 7))
                for m4 in range(4):
                    m = hf * 4 + m4
                    tt(h[m], bks[m4], h[m], ALU.add)
                    sumsq_chunk(m)

        finals = []
        xT3 = xT.rearrange("(c p) t -> p c t", p=128)
        oT3 = outT.rearrange("(c p) t -> p c t", p=128)
        for ti in range(n_tiles):
            t0 = ti * T
            for c2 in range(2):
                P.dma("sp", (lambda c2, t0: (lambda e: e.dma_start(out=h_t[:, c2 * 4:(c2 + 1) * 4, :],
                                                                   in_=xT3[:, c2 * 4:(c2 + 1) * 4, t0:t0 + T])))(c2, t0),
                      "ld_x%d" % c2, writes=[("h", c) for c in range(c2 * 4, c2 * 4 + 4)])
            for l in range(DEPTH):
                layer(l, ti)
                if debug_taps and ti == 0:
                    tap("h_l%d" % l, B(h_t[:], [("h", c) for c in range(8)]), [128, 8, T])
            mark('final')
            rmsnorm(0, PV_FIN, out=uT)
            for c2 in range(2):
                finals.append(P.dma("sp", (lambda c2, t0: (lambda e: e.dma_start(out=oT3[:, c2 * 4:(c2 + 1) * 4, t0:t0 + T],
                                                                                 in_=uT_t[:, c2 * 4:(c2 + 1) * 4, :])))(c2, t0),
                                    "st_o%d" % c2, reads=[("uT", c) for c in range(c2 * 4, c2 * 4 + 4)]))
        finals.extend(taps.values())
        P.emit(final_waits=finals)
    return nc


def _vec8(v):
    return np.ascontiguousarray(v.reshape(-1, 128).T)


def prep_params(inp):
    f = lambda k: np.asarray(inp[k], dtype=np.float32)
    pvec = np.zeros((DEPTH, 128, NPV), np.float32)
    prow = np.zeros((DEPTH, 128, NPR), np.float32)
    for l in range(DEPTH):
        pvec[l, :, PV_NMIX:PV_NMIX + 8] = _vec8(f("norm_mix_g")[l])
        pvec[l, :, PV_NMLP:PV_NMLP + 8] = _vec8(f("norm_mlp_g")[l])
        pvec[l, :, PV_LNG:PV_LNG + 8] = _vec8(f("gmlp_ln_g")[l])
        pvec[l, :, PV_LNB:PV_LNB + 8] = _vec8(f("gmlp_ln_b")[l])
        cw = f("lru_conv_w")[l]
        pvec[l, :, PV_LCW:PV_LCW + 32] = cw.reshape(4, 8, 128).transpose(2, 1, 0).reshape(128, 32)
        pvec[l, :, PV_LCB:PV_LCB + 8] = _vec8(f("lru_conv_b")[l])
        pvec[l, :, PV_BR:PV_BR + 8] = _vec8(f("lru_b_r")[l])
        pvec[l, :, PV_BI:PV_BI + 8] = _vec8(f("lru_b_i")[l])
        pvec[l, :, PV_LAM:PV_LAM + 8] = _vec8(f("lru_lambda")[l])
        sw = f("ssd_conv_w")[l]
        pvec[l, :, PV_SCW:PV_SCW + 64] = sw.reshape(4, 16, 128).transpose(2, 1, 0).reshape(128, 64)
        pvec[l, :, PV_SCB:PV_SCB + 16] = _vec8(f("ssd_conv_b")[l])
        pvec[l, :, PV_BG:PV_BG + 24] = _vec8(f("b_gate")[l].reshape(-1))
        pvec[l, :, PV_SNG:PV_SNG + 8] = _vec8(f("ssd_norm_g")[l])
        pvec[l, :, PV_FIN:PV_FIN + 8] = _vec8(f("final_norm_g"))
        prow[l, :, PR_DTB:PR_DTB + 16] = f("ssd_dt_bias")[l][None, :]
        prow[l, :, PR_ALOG:PR_ALOG + 16] = f("ssd_a_log")[l][None, :]
        prow[l, :, PR_D:PR_D + 16] = f("ssd_d")[l][None, :]
    bsrow = np.ascontiguousarray(f("gmlp_b_s").reshape(DEPTH, 1, 1024))
    gwT = np.ascontiguousarray(f("gmlp_w_s").transpose(0, 3, 1, 2))
    wr = np.ascontiguousarray(f("lru_w_r").transpose(0, 2, 1, 3))
    wi = np.ascontiguousarray(f("lru_w_i").transpose(0, 2, 1, 3))
    consts = np.zeros((128, 3, 128), np.float32)
    consts[:, 0, :] = np.eye(128, dtype=np.float32)
    consts[:, 1, :] = np.triu(np.ones((128, 128), np.float32))
    consts[:, 2, :] = np.tril(np.ones((128, 128), np.float32), -1)
    return dict(pvec=pvec, prow=prow, bsrow=bsrow, gwT=gwT, wr=wr, wi=wi, consts=consts)


_CACHE = {}
PHASES = []


def kernel(**inputs):
    x = np.asarray(inputs["x"], dtype=np.float32)
    shared = prep_params(inputs)
    for k in ("w_in", "w_branch_a", "w_branch_b", "w_branch_c", "w_out", "w_mlp_up", "w_mlp_down"):
        shared[k] = np.ascontiguousarray(np.asarray(inputs[k], dtype=np.float32))
    n_tiles = SEQ // T
    if "nc" not in _CACHE:
        _CACHE["nc"] = build_program(n_tiles)
    nc = _CACHE["nc"]
    in_maps = []
    for core in range(N_CORES):
        b = core % BATCH
        m = dict(shared)
        m["xT"] = np.ascontiguousarray(x[b].T)
        in_maps.append(m)
    res = run_bass_kernel_spmd(nc, in_maps, core_ids=list(range(N_CORES)))
    out = np.empty((BATCH, SEQ, D), np.float32)
    for b in range(BATCH):
        out[b] = res.results[b]["outT"].T
    return out
```

```python
from contextlib import ExitStack

import numpy as np
import concourse.bass as bass
import concourse.mybir as mybir
from concourse.bass_utils import run_bass_kernel_spmd

F32 = mybir.dt.float32
BF16 = mybir.dt.bfloat16
AF = mybir.ActivationFunctionType
ALU = mybir.AluOpType
AX = mybir.AxisListType

D = 1024
SEQ = 4096
BATCH = 4
DEPTH = 2
D_IN = 10256
T = 512
NJ = T // 128
EPS = 1e-6
N_CORES = 8

PV_NMIX, PV_NMLP, PV_LNG, PV_LNB, PV_LCW, PV_LCB, PV_BR, PV_BI, PV_LAM = 0, 8, 16, 24, 32, 64, 72, 80, 88
PV_SCW, PV_SCB, PV_BG, PV_SNG, PV_FIN, NPV = 96, 160, 176, 200, 208, 216
PR_DTB, PR_ALOG, PR_D, NPR = 0, 16, 32, 48
C_U, C_V, C_XB, C_GL, C_Z, C_X, C_B, C_C, C_DT, C_G = 0, 1024, 2048, 3072, 4096, 5120, 6144, 6656, 7168, 7184


class _Op:
    __slots__ = ("eng", "fn", "deps", "signal", "count", "dma_sem", "dma_count")

    def __init__(self, eng, fn, deps):
        self.eng = eng
        self.fn = fn
        self.deps = deps
        self.signal = False
        self.count = None
        self.dma_sem = None
        self.dma_count = None


class Prog:
    ENGS = ("pe", "act", "dve", "pool", "sp")

    def __init__(self, nc):
        self.nc = nc
        self.ops = {e: [] for e in self.ENGS}
        self.last_writer = {}
        self.readers = {}
        self.dma_sems = {}

    def _deps(self, reads, writes):
        deps = []
        for k in reads:
            w = self.last_writer.get(k)
            if w is not None:
                deps.append(w)
        for k in writes:
            w = self.last_writer.get(k)
            if w is not None:
                deps.append(w)
            deps.extend(self.readers.get(k, {}).values())
        return deps

    def _commit(self, op, reads, writes):
        rk = op.eng if op.dma_sem is None else ("dma", id(op))
        for k in reads:
            self.readers.setdefault(k, {})[rk] = op
        for k in writes:
            self.last_writer[k] = op
            self.readers[k] = {}

    def op(self, eng, fn, reads=(), writes=()):
        psr = [k for k in reads if isinstance(k, tuple) and k[0] == "ps"]
        if psr:
            writes = list(writes) + [k for k in psr if k not in writes]
        deps = self._deps(reads, writes)
        o = _Op(eng, fn, deps)
        for d in deps:
            d.signal = True
        self.ops[eng].append(o)
        self._commit(o, reads, writes)
        return o

    def dma(self, eng, fn, sem, reads=(), writes=()):
        deps = self._deps(reads, writes)
        o = _Op(eng, fn, deps)
        for d in deps:
            d.signal = True
        c = self.dma_sems.get(sem, 0) + 16
        self.dma_sems[sem] = c
        o.dma_sem = sem
        o.dma_count = c
        self.ops[eng].append(o)
        self._commit(o, reads, writes)
        return o

    def emit(self, final_waits=()):
        nc = self.nc
        with ExitStack() as es:
            esem = {e: es.enter_context(nc.semaphore("s_" + e)) for e in self.ENGS}
            dsem = {n: es.enter_context(nc.semaphore("d_" + n)) for n in self.dma_sems}
            for o in final_waits:
                o.signal = True
            for e in self.ENGS:
                c = 0
                for o in self.ops[e]:
                    if o.dma_sem is None and o.signal:
                        c += 1
                        o.count = c
            block = es.enter_context(nc.Block())

            def run(e, engine):
                waited = {}
                for o in self.ops[e]:
                    need = {}
                    for d in o.deps:
                        if d.dma_sem is not None:
                            key = ("d", d.dma_sem)
                            val = d.dma_count
                        else:
                            if d.eng == e and e in ("pe", "sp"):
                                continue
                            key = ("e", d.eng)
                            val = d.count
                        if val > need.get(key, 0):
                            need[key] = val
                    for key, val in need.items():
                        if waited.get(key, 0) >= val:
                            continue
                        waited[key] = val
                        s = dsem[key[1]] if key[0] == "d" else esem[key[1]]
                        engine.wait_ge(s, val)
                    ins = o.fn(engine)
                    if o.dma_sem is not None:
                        ins.then_inc(dsem[o.dma_sem], 16)
                    elif o.signal:
                        ins.then_inc(esem[e], 1)
                if e == "sp":
                    for o in final_waits:
                        if o.dma_sem is not None:
                            engine.wait_ge(dsem[o.dma_sem], o.dma_count)
                        else:
                            engine.wait_ge(esem[o.eng], o.count)

            @block.tensor
            def _(eng):
                run("pe", eng)

            @block.scalar
            def _(eng):
                run("act", eng)

            @block.vector
            def _(eng):
                run("dve", eng)

            @block.gpsimd
            def _(eng):
                run("pool", eng)

            @block.sync
            def _(eng):
                run("sp", eng)


class B:
    __slots__ = ("ap", "keys")

    def __init__(self, ap, keys):
        self.ap = ap
        self.keys = list(keys)

    def s(self, ap, keys=None):
        return B(ap, self.keys if keys is None else keys)


def bc(ap, shape, axis):
    return ap.unsqueeze(axis).to_broadcast(list(shape))


def build_program(n_tiles, debug_taps=False):
    nc = bass.Bass("TRN2", target_bir_lowering=False)
    P = Prog(nc)

    def din(name, shape):
        return nc.dram_tensor(name, list(shape), F32, kind="ExternalInput").ap()

    xT = din("xT", [D, SEQ])
    w_in = din("w_in", [DEPTH, D, D_IN])
    w_ba = din("w_branch_a", [DEPTH, D, D])
    w_bb = din("w_branch_b", [DEPTH, D, D])
    w_bc = din("w_branch_c", [DEPTH, D, D])
    w_out = din("w_out", [DEPTH, D, D])
    w_up = din("w_mlp_up", [DEPTH, D, 4 * D])
    w_dn = din("w_mlp_down", [DEPTH, 4 * D, D])
    pvec_d = din("pvec", [DEPTH, 128, NPV])
    prow_d = din("prow", [DEPTH, 128, NPR])
    bsrow_d = din("bsrow", [DEPTH, 1, 1024])
    gwT_d = din("gwT", [DEPTH, 128, 8, 128])
    wr_d = din("wr", [DEPTH, 128, 8, 128])
    wi_d = din("wi", [DEPTH, 128, 8, 128])
    consts_d = din("consts", [128, 3, 128])
    outT = nc.dram_tensor("outT", [D, SEQ], F32, kind="ExternalOutput").ap()
    taps = {}

    with ExitStack() as es:
        def sb(name, shape, dt):
            return es.enter_context(nc.sbuf_tensor("sb_" + name, list(shape), dt))

        consts = sb("consts", [128, 3, 128], F32)
        ident_bf = sb("ident_bf", [128, 128], BF16)
        U_f = consts[:, 1, :]
        Ls_f = consts[:, 2, :]
        ones_bf = sb("ones_bf", [128, 128], BF16)
        ones_f = sb("ones_f", [128, 128], F32)
        cst = sb("cst", [128, 4], F32)
        pv = [sb("pv%d" % l, [128, NPV], F32) for l in range(DEPTH)]
        pr = [sb("pr%d" % l, [128, NPR], F32) for l in range(DEPTH)]
        gw_bf = [sb("gw%d" % l, [128, 8, 128], BF16) for l in range(DEPTH)]
        wr_bf = [sb("wr%d" % l, [128, 8, 128], BF16) for l in range(DEPTH)]
        wi_bf = [sb("wi%d" % l, [128, 8, 128], BF16) for l in range(DEPTH)]
        wdt_bf = [sb("wdt%d" % l, [128, 8, 16], BF16) for l in range(DEPTH)]
        Eg = [sb("Eg%d" % l, [128, 8, 128], F32) for l in range(DEPTH)]
        Dg = [sb("Dg%d" % l, [128, 16, 128], BF16) for l in range(DEPTH)]
        clru = [sb("clru%d" % l, [128, 16], F32) for l in range(DEPTH)]
        arow = [sb("arow%d" % l, [128, 16], F32) for l in range(DEPTH)]
        S = [sb("S%d" % l, [128, 1024], F32) for l in range(DEPTH)]
        S_bf = [sb("Sbf%d" % l, [128, 1024], BF16) for l in range(DEPTH)]
        hst = [sb("hst%d" % l, [128, 8], F32) for l in range(DEPTH)]
        hist_l = [sb("histl%d" % l, [128, 8, 4], F32) for l in range(DEPTH)]
        hist_s = [sb("hists%d" % l, [128, 16, 4], F32) for l in range(DEPTH)]
        h_t = sb("h", [128, 8, T], F32)
        hn_t = sb("hn", [128, 8, T], BF16)
        sqb_t = sb("sqb", [128, 2, T], BF16)
        rstd_t = sb("rstd", [128, T], F32)
        lnt_t = sb("lnt", [128, T], F32)
        uT_t = sb("uT", [128, 8, T], F32)
        up_t = sb("up", [128, 32, T], BF16)
        NW = 4
        wring = [sb("wr_ring%d" % i, [128, 8, 512], BF16) for i in range(NW)]
        ARENA_N = 12288
        arena = sb("arena", [128, ARENA_N], F32)
        small = sb("small", [128, 896], F32)
        ps = es.enter_context(nc.psum_tensor("ps", [128, 4096], F32))
        bsrow = arena

        class Arena:
            def __init__(self):
                self.p = 0

            def reset(self):
                self.p = 0

            def f32(self, n, shape=None):
                a, b = self.p, self.p + n
                assert b <= ARENA_N, "arena overflow"
                self.p = (b + 127) // 128 * 128
                ap = arena[:, a:b]
                if shape is not None:
                    ap = ap.rearrange("p (a b) -> p a b", a=shape[0])
                return B(ap, [("A", g) for g in range(a // 128, (b + 127) // 128)])

            def bf16(self, n, shape=None):
                w = (n + 1) // 2
                a, b = self.p, self.p + w
                assert b <= ARENA_N, "arena overflow"
                self.p = (b + 127) // 128 * 128
                ap = arena[:, a:b].bitcast(BF16)
                if shape is not None:
                    ap = ap.rearrange("p (a b) -> p a b", a=shape[0])
                return B(ap, [("A", g) for g in range(a // 128, (b + 127) // 128)])

        AR = Arena()
        sm_p = [0]

        def sm(n):
            a = sm_p[0]
            sm_p[0] += n
            assert sm_p[0] <= 896
            return B(small[:, a:a + n], [("sm", w) for w in range(a // 8, (a + n - 1) // 8 + 1)])

        bank_p = [0]
        bank_res = set()

        def bank(n=1):
            p = bank_p[0]
            for _ in range(32):
                if p % n:
                    p += n - p % n
                if p + n > 8:
                    p = 0
                if any((p + i) in bank_res for i in range(n)):
                    p += 1
                    continue
                break
            else:
                raise AssertionError("no free psum bank")
            bank_p[0] = (p + n) % 8
            return B(ps[:, p * 512:(p + n) * 512], [("ps", p + i) for i in range(n)])

        def bank_reserve():
            b_ = bank()
            bank_res.add(b_.keys[0][1])
            return b_

        def bank_release(b_):
            bank_res.discard(b_.keys[0][1])

        h = [B(h_t[:, c, :], [("h", c)]) for c in range(8)]
        hn = [B(hn_t[:, c, :], [("hn", c)]) for c in range(8)]
        sqb = [B(sqb_t[:, i, :], [("sqb", i)]) for i in range(2)]
        rstd = B(rstd_t[:], ["rstd"])
        lnt = B(lnt_t[:], ["lnt"])
        uT = [B(uT_t[:, c, :], [("uT", c)]) for c in range(8)]
        up = [B(up_t[:, c, :], [("up", c)]) for c in range(32)]
        ya, yb, yc, mg = up[0:8], up[8:16], up[16:24], up[24:32]
        PARAMS = ["params"]

        def pvc(l, col, n=1):
            return B(pv[l][:, col:col + n], PARAMS)

        def mm(out, lhsT, rhs, start=True, stop=True):
            P.op("pe", lambda e: e.matmul(out.ap, lhsT=lhsT.ap, rhs=rhs.ap, start=start, stop=stop),
                 reads=lhsT.keys + rhs.keys, writes=out.keys)

        def tr(out, in_):
            P.op("pe", lambda e: e.transpose(out=out.ap, in_=in_.ap, identity=ident_bf[:]),
                 reads=in_.keys + PARAMS, writes=out.keys)

        def act(out, in_, func, bias=None, scale=None, accum=None):
            rd = list(in_.keys)
            wr = list(out.keys)
            kw = {}
            if bias is not None:
                if isinstance(bias, B):
                    rd += bias.keys
                    kw["bias"] = bias.ap
                else:
                    kw["bias"] = bias
            if scale is not None:
                if isinstance(scale, B):
                    rd += scale.keys
                    kw["scale"] = scale.ap
                else:
                    kw["scale"] = scale
            if accum is not None:
                wr += accum.keys
                kw["accum_out"] = accum.ap
            P.op("act", lambda e: e.activation(out=out.ap, in_=in_.ap, func=func, **kw), reads=rd, writes=wr)

        def tt(out, in0, in1, op, eng="dve"):
            P.op(eng, lambda e: e.tensor_tensor(out=out.ap, in0=in0.ap, in1=in1.ap, op=op),
                 reads=in0.keys + in1.keys, writes=out.keys)

        def ts(out, in0, s1, op0, s2=None, op1=None, eng="dve"):
            rd = list(in0.keys)
            a1 = s1
            a2 = s2
            if isinstance(s1, B):
                rd += s1.keys
                a1 = s1.ap
            if isinstance(s2, B):
                rd += s2.keys
                a2 = s2.ap
            if op1 is None:
                P.op(eng, lambda e: e.tensor_scalar(out=out.ap, in0=in0.ap, scalar1=a1, scalar2=None, op0=op0),
                     reads=rd, writes=out.keys)
            else:
                P.op(eng, lambda e: e.tensor_scalar(out=out.ap, in0=in0.ap, scalar1=a1, scalar2=a2, op0=op0, op1=op1),
                     reads=rd, writes=out.keys)

        def stt(out, in0, scalar, in1, op0, op1, eng="dve"):
            rd = in0.keys + in1.keys
            a = scalar
            if isinstance(scalar, B):
                rd = rd + scalar.keys
                a = scalar.ap
            P.op(eng, lambda e: e.scalar_tensor_tensor(out=out.ap, in0=in0.ap, scalar=a, in1=in1.ap, op0=op0, op1=op1),
                 reads=rd, writes=out.keys)

        def cp(out, in_, eng="dve"):
            if eng == "act":
                P.op("act", lambda e: e.activation(out=out.ap, in_=in_.ap, func=AF.Copy), reads=in_.keys, writes=out.keys)
            else:
                P.op(eng, lambda e: e.tensor_copy(out=out.ap, in_=in_.ap), reads=in_.keys, writes=out.keys)

        def memset(buf, val, eng="dve"):
            P.op(eng, lambda e: e.memset(buf.ap, val), writes=buf.keys)

        def red_sum(out, in_):
            P.op("dve", lambda e: e.tensor_reduce(out=out.ap, in_=in_.ap, axis=AX.X, op=ALU.add),
                 reads=in_.keys, writes=out.keys)

        def scan(out, d0, d1, init):
            P.op("dve", lambda e: e.tensor_tensor_scan(out=out.ap, data0=d0.ap, data1=d1.ap, initial=init.ap,
                                                       op0=ALU.mult, op1=ALU.add),
                 reads=d0.keys + d1.keys + init.keys, writes=out.keys)

        def tap(name, buf, shape):
            if not debug_taps:
                return
            t = nc.dram_tensor("tap_" + name, list(shape), buf.ap.dtype, kind="ExternalOutput").ap()
            taps[name] = P.dma("sp", lambda e: e.dma_start(out=t, in_=buf.ap), "tap_" + name, reads=buf.keys)

        def w3(w, l):
            return w[l].rearrange("(kc p) e -> p kc e", p=128)

        def wsched(l):
            win3 = w3(w_in, l)
            sch = []
            for c0 in (C_V, C_V + 512, C_U, C_U + 512, C_XB, C_GL, C_XB + 512, C_GL + 512, C_X, C_X + 512, C_B, C_C,
                       C_Z, C_Z + 512):
                sch.append((win3, 0, c0))
            for k, wb in enumerate((w_ba, w_bb, w_bc)):
                for hf in range(2):
                    sch.append((win3, 0, C_G + k * 1024 + hf * 512))
                    sch.append((w3(wb, l), 0, hf * 512))
            for hf in range(2):
                sch.append((w3(w_out, l), 0, hf * 512))
            for f4 in range(8):
                sch.append((w3(w_up, l), 0, f4 * 512))
            for hf in range(2):
                for q in range(4):
                    sch.append((w3(w_dn, l), q * 8, hf * 512))
            return sch

        WSTREAM = []
        for ti_ in range(n_tiles):
            for l_ in range(DEPTH):
                WSTREAM.extend(wsched(l_))
        PF = NW - 1
        wcur = [0]
        wissued = [0]

        NWT = len(wsched(0))
        wscr = nc.dram_tensor("wscr", [DEPTH * NWT, 128, 8 * 512], BF16).ap()

        def _issue(i):
            src3, kc0, c0 = WSTREAM[i]
            si = i % NW
            slot = wring[si]
            tl = i // NWT
            ti_, l_ = tl // DEPTH, tl % DEPTH
            sidx = l_ * NWT + (i % NWT)
            scr = wscr[sidx].rearrange("p (k c) -> p k c", k=8)
            for hf in range(2):
                a, b = hf * 4, hf * 4 + 4
                dst = slot[:, a:b, :]
                if ti_ == 0:
                    src = src3[:, kc0 + a:kc0 + b, c0:c0 + 512]
                    P.dma("pool", (lambda dst, src: (lambda e: e.dma_start(out=dst, in_=src)))(dst, src),
                          "w%d_%d" % (si, hf), writes=[("w", si, hf)])
                else:
                    src = scr[:, a:b, :]
                    P.dma("sp", (lambda dst, src: (lambda e: e.dma_start(out=dst, in_=src)))(dst, src),
                          "w%d_%d" % (si, hf), reads=[("wscr", sidx)], writes=[("w", si, hf)])
            if ti_ == 0 and n_tiles > 1:
                P.dma("sp", (lambda scr, slot: (lambda e: e.dma_start(out=scr, in_=slot[:, :, :])))(scr, slot),
                      "wb%d" % si, reads=[("w", si, 0), ("w", si, 1)], writes=[("wscr", sidx)])

        def wload(src3, kc0, c0, n=512):
            i = wcur[0]
            wcur[0] += 1
            assert WSTREAM[i][1] == kc0 and WSTREAM[i][2] == c0, ("weight schedule mismatch", i, kc0, c0, WSTREAM[i][1:])
            while wissued[0] < min(len(WSTREAM), i + PF + 1):
                _issue(wissued[0])
                wissued[0] += 1
            si = i % NW
            return wring[si], [("w", si, 0), ("w", si, 1)]

        def wl(slot_keys, kc, c0, n):
            slot, keys = slot_keys
            return B(slot[:, kc, c0:c0 + n], [keys[kc // 4]])

        P.dma("sp", lambda e: e.dma_start(out=consts[:], in_=consts_d), "ld_c", writes=PARAMS)
        for l in range(DEPTH):
            P.dma("sp", (lambda l: (lambda e: e.dma_start(out=pv[l][:], in_=pvec_d[l])))(l), "ld_pv%d" % l, writes=PARAMS)
            P.dma("sp", (lambda l: (lambda e: e.dma_start(out=pr[l][:], in_=prow_d[l])))(l), "ld_pr%d" % l, writes=PARAMS)
            P.dma("sp", (lambda l: (lambda e: e.dma_start(out=bsrow[0:1, l * 1024:(l + 1) * 1024], in_=bsrow_d[l])))(l),
                  "ld_bs%d" % l, writes=PARAMS)
            P.dma("pool", (lambda l: (lambda e: e.dma_start(out=gw_bf[l][:], in_=gwT_d[l])))(l), "ld_gw%d" % l, writes=PARAMS)
            P.dma("pool", (lambda l: (lambda e: e.dma_start(out=wr_bf[l][:], in_=wr_d[l])))(l), "ld_wr%d" % l, writes=PARAMS)
            P.dma("pool", (lambda l: (lambda e: e.dma_start(out=wi_bf[l][:], in_=wi_d[l])))(l), "ld_wi%d" % l, writes=PARAMS)
            P.dma("pool", (lambda l: (lambda e: e.dma_start(out=wdt_bf[l][:], in_=w3(w_in, l)[:, :, C_DT:C_DT + 16])))(l),
                  "ld_wdt%d" % l, writes=PARAMS)
        PB = B(None, PARAMS)
        P.op("dve", lambda e: e.memset(ones_bf[:], 1.0), reads=PARAMS, writes=PARAMS)
        P.op("dve", lambda e: e.memset(ones_f[:], 1.0), reads=PARAMS, writes=PARAMS)
        P.op("dve", lambda e: e.memset(cst[:, 0:1], EPS), reads=PARAMS, writes=PARAMS)
        P.op("dve", lambda e: e.memset(cst[:, 1:2], 1.0), reads=PARAMS, writes=PARAMS)
        P.op("dve", lambda e: e.tensor_copy(out=ident_bf[:], in_=consts[:, 0, :]), reads=PARAMS, writes=PARAMS)
        eps_b = B(cst[:, 0:1], PARAMS)
        one_b = B(cst[:, 1:2], PARAMS)
        for l in range(DEPTH):
            for t_, n_ in ((S[l], 1024), (hst[l], 8)):
                P.op("dve", (lambda t_: (lambda e: e.memset(t_[:], 0.0)))(t_), reads=PARAMS, writes=PARAMS)
            P.op("dve", (lambda l: (lambda e: e.memset(S_bf[l][:], 0.0)))(l), reads=PARAMS, writes=PARAMS)
            P.op("dve", (lambda l: (lambda e: e.memset(hist_l[l][:], 0.0)))(l), reads=PARAMS, writes=PARAMS)
            P.op("dve", (lambda l: (lambda e: e.memset(hist_s[l][:], 0.0)))(l), reads=PARAMS, writes=PARAMS)
            P.op("dve", (lambda l: (lambda e: e.tensor_tensor(out=gw_bf[l][:], in0=gw_bf[l][:], in1=bc(U_f, [128, 8, 128], 1),
                                                               op=ALU.mult)))(l), reads=PARAMS, writes=PARAMS)
            P.op("act", (lambda l: (lambda e: e.activation(out=clru[l][:, 0:8], in_=pv[l][:, PV_LAM:PV_LAM + 8], func=AF.Exp,
                                                           scale=-1.0)))(l), reads=PARAMS, writes=PARAMS)
            P.op("act", (lambda l: (lambda e: e.activation(out=clru[l][:, 0:8], in_=clru[l][:, 0:8], func=AF.Ln,
                                                           bias=cst[:, 1:2], scale=1.0)))(l), reads=PARAMS, writes=PARAMS)
            P.op("dve", (lambda l: (lambda e: e.tensor_scalar(out=clru[l][:, 8:16], in0=clru[l][:, 0:8], scalar1=-16.0,
                                                              scalar2=None, op0=ALU.mult)))(l), reads=PARAMS, writes=PARAMS)
            P.op("dve", (lambda l: (lambda e: e.tensor_scalar(out=clru[l][:, 0:8], in0=clru[l][:, 0:8], scalar1=-8.0,
                                                              scalar2=None, op0=ALU.mult)))(l), reads=PARAMS, writes=PARAMS)
            P.op("act", (lambda l: (lambda e: e.activation(out=arow[l][:], in_=pr[l][:, PR_ALOG:PR_ALOG + 16],
                                                           func=AF.Exp)))(l), reads=PARAMS, writes=PARAMS)
            P.op("dve", (lambda l: (lambda e: e.tensor_scalar(out=arow[l][:], in0=arow[l][:], scalar1=-1.0, scalar2=None,
                                                              op0=ALU.mult)))(l), reads=PARAMS, writes=PARAMS)
            P.op("dve", (lambda l: (lambda e: e.tensor_tensor(out=Dg[l][:], in0=bc(consts[:, 0, :], [128, 16, 128], 1),
                                                               in1=bc(pr[l][:, PR_D:PR_D + 16], [128, 16, 128], 2),
                                                               op=ALU.mult)))(l), reads=PARAMS, writes=PARAMS)
            bkA = bank(2)
            bkB = bank(2)
            for g in range(8):
                P.op("pe", (lambda l, g, bk: (lambda e: e.matmul(bk.ap[:, g * 128:(g + 1) * 128], lhsT=ones_bf[:],
                                                                 rhs=gw_bf[l][:, g, :], start=True, stop=True)))(l, g, bkA),
                     reads=PARAMS, writes=bkA.keys)
            for hf in range(2):
                P.op("pe", (lambda l, hf, bk: (lambda e: e.matmul(bk.ap[:, hf * 512:(hf + 1) * 512], lhsT=ones_f[0:1, :],
                                                                  rhs=bsrow[0:1, l * 1024 + hf * 512:l * 1024 + (hf + 1) * 512],
                                                                  start=True, stop=True)))(l, hf, bkB),
                     reads=PARAMS, writes=bkB.keys)
            P.op("act", (lambda l, bk: (lambda e: e.activation(out=Eg[l][:], in_=bk.ap.rearrange("p (a b) -> p a b", a=8),
                                                               func=AF.Copy)))(l, bkB), reads=bkB.keys + PARAMS, writes=PARAMS)
            for g in range(8):
                P.op("dve", (lambda l, g, bk: (lambda e: e.scalar_tensor_tensor(
                    out=Eg[l][:, g, :], in0=bk.ap[:, g * 128:(g + 1) * 128], scalar=pv[l][:, PV_LNB + g:PV_LNB + g + 1],
                    in1=Eg[l][:, g, :], op0=ALU.mult, op1=ALU.add)))(l, g, bkA), reads=bkA.keys + PARAMS, writes=PARAMS)

        nsum = {"bk": None}

        def sumsq_start():
            nsum["bk"] = bank_reserve()
            nsum["pend"] = []

        def sumsq_chunk(c):
            act(sqb[c % 2], h[c], AF.Square)
            nsum["pend"].append(c)
            sumsq_flush(keep=1)

        def sumsq_flush(keep=0):
            while len(nsum["pend"]) > keep:
                c = nsum["pend"].pop(0)
                mm(nsum["bk"], B(ones_bf[:], PARAMS), sqb[c % 2], start=(c == 0), stop=(c == 7))

        def rmsnorm(l, gcol, out=None):
            out = hn if out is None else out
            if nsum["bk"] is None:
                sumsq_start()
                for c in range(8):
                    sumsq_chunk(c)
            sumsq_flush()
            bk = nsum["bk"]
            act(lnt, bk, AF.Ln, bias=eps_b, scale=1.0 / D)
            act(rstd, lnt, AF.Exp, scale=-0.5)
            bank_release(bk)
            nsum["bk"] = None
            for c in range(8):
                stt(out[c], h[c], pvc(l, gcol + c), rstd, ALU.mult, ALU.mult)

        def conv4(dst, xbuf, wcol, bcol, l, acc):
            ts(acc, xbuf.s(xbuf.ap[:, 0:T]), pvc(l, wcol + 0), ALU.mult, pvc(l, bcol), ALU.add)
            for k in (1, 2):
                stt(acc, xbuf.s(xbuf.ap[:, k:k + T]), pvc(l, wcol + k), acc, ALU.mult, ALU.add)
            stt(dst, xbuf.s(xbuf.ap[:, 3:3 + T]), pvc(l, wcol + 3), acc, ALU.mult, ALU.add)

        import os
        STOP = int(os.environ.get("MK_STOP", "99"))

        def mark(name):
            PHASES.append((name, len(P.ops["pe"])))

        def layer(l, ti):
            win3 = w3(w_in, l)
            mark("norm1")
            if STOP < 1:
                return
            rmsnorm(l, PV_NMIX)
            if STOP < 2:
                return
            mark('A')
            AR.reset()
            vtok = [AR.f32(1024) for _ in range(NJ)]
            vn = [AR.bf16(1024) for _ in range(NJ)]
            tmpA = AR.f32(T)
            junk = AR.bf16(1024)
            for hf in range(2):
                wt = wload(win3, 0, C_V + hf * 512)
                for j in range(NJ):
                    bk = bank()
                    for kc in range(8):
                        mm(bk, hn[kc].s(hn[kc].ap[:, j * 128:(j + 1) * 128]), wl(wt, kc, 0, 512), kc == 0, kc == 7)
                    act(vtok[j].s(vtok[j].ap[:, hf * 512:(hf + 1) * 512]), bk, AF.Gelu_apprx_tanh)
            def sm1():
                b_ = sm(8)
                return b_.s(b_.ap[:, 0:1])

            st_ = [[sm1() for _ in range(6)] for j in range(NJ)]
            for j in range(NJ):
                memset(st_[j][1], 0.0)
            for j in range(NJ):
                red_sum(st_[j][0], vtok[j])
                act(junk, vtok[j], AF.Square, accum=st_[j][1])
            for j in range(NJ):
                ts(st_[j][2], st_[j][0], 1.0 / 1024, ALU.mult)
            for j in range(NJ):
                tt(st_[j][4], st_[j][2], st_[j][2], ALU.mult)
            for j in range(NJ):
                stt(st_[j][3], st_[j][1], 1.0 / 1024, st_[j][4], ALU.mult, ALU.subtract)
            for j in range(NJ):
                act(st_[j][3], st_[j][3], AF.Ln, bias=eps_b, scale=1.0)
            for j in range(NJ):
                act(st_[j][5], st_[j][3], AF.Exp, scale=-0.5)
            for j in range(NJ):
                ts(vn[j], vtok[j], st_[j][2], ALU.subtract, st_[j][5], ALU.mult)
            sm_p[0] = 0
            for hf in range(2):
                wt = wload(win3, 0, C_U + hf * 512)
                for m4 in range(4):
                    m = hf * 4 + m4
                    bk = bank()
                    for kc in range(8):
                        mm(bk, wl(wt, kc, m4 * 128, 128), hn[kc], kc == 0, kc == 7)
                    act(uT[m], bk, AF.Gelu_apprx_tanh)
            for g in range(8):
                bk = bank()
                for j in range(NJ):
                    mm(bk.s(bk.ap[:, j * 128:(j + 1) * 128]), vn[j].s(vn[j].ap[:, g * 128:(g + 1) * 128]),
                       B(gw_bf[l][:, g, :], PARAMS))
                stt(tmpA.s(tmpA.ap.rearrange("p (a b) -> p a b", a=NJ)), bk.s(bk.ap.rearrange("p (a b) -> p a b", a=NJ)),
                    pvc(l, PV_LNG + g), B(bc(Eg[l][:, g, :], [128, NJ, 128], 1), PARAMS), ALU.mult, ALU.add)
                tt(ya[g], tmpA, uT[g], ALU.mult)
            if STOP < 3:
                return
            mark('B')
            AR.reset()
            hseq4 = AR.f32(4 * T, shape=(4, T))
            xbufs = [AR.f32(T + 4) for _ in range(4)]
            accs = [AR.f32(T) for _ in range(4)]
            xcs = [AR.f32(T) for _ in range(4)]
            xc_bfs = [AR.bf16(T) for _ in range(2)]
            r4 = [AR.f32(T) for _ in range(4)]
            i2 = [AR.f32(T) for _ in range(2)]
            a4 = accs
            gg = xbufs[0].s(xbufs[0].ap[:, 0:T])
            for hf in range(2):
                wt = wload(win3, 0, C_XB + hf * 512)
                bks = []
                for m4 in range(4):
                    bk = bank()
                    for kc in range(8):
                        mm(bk, wl(wt, kc, m4 * 128, 128), hn[kc], kc == 0, kc == 7)
                    bks.append(bk)

                def stA(m4):
                    hd = hf * 4 + m4
                    xbuf, acc = xbufs[m4], accs[m4]
                    hl = B(hist_l[l][:, hd, 0:3], [("hl", l, hd)])
                    cp(xbuf.s(xbuf.ap[:, 0:3]), hl)
                    cp(hl, bks[m4].s(bks[m4].ap[:, T - 3:T]))
                    cp(xbuf.s(xbuf.ap[:, 3:3 + T]), bks[m4], eng="act")
                    act(acc, bks[m4], AF.Identity, bias=pvc(l, PV_LCB + hd), scale=pvc(l, PV_LCW + hd * 4 + 3))

                def stB2(ms):
                    for k in range(3):
                        for m4 in ms:
                            hd = hf * 4 + m4
                            dst = xcs[m4] if k == 2 else accs[m4]
                            stt(dst, xbufs[m4].s(xbufs[m4].ap[:, k:k + T]), pvc(l, PV_LCW + hd * 4 + k), accs[m4],
                                ALU.mult, ALU.add)
                    out = {}
                    for m4 in ms:
                        cp(xc_bfs[m4 % 2], xcs[m4], eng="act")
                    for m4 in ms:
                        hd = hf * 4 + m4
                        bkr = bank()
                        mm(bkr, B(wr_bf[l][:, hd, :], PARAMS), xc_bfs[m4 % 2])
                        bki = bank()
                        mm(bki, B(wi_bf[l][:, hd, :], PARAMS), xc_bfs[m4 % 2])
                        out[m4] = (bkr, bki)
                    return out

                ri = {}
                for m4 in range(4):
                    stA(m4)
                ri.update(stB2((0, 1)))
                ri.update(stB2((2, 3)))
                for m4 in range(4):
                    hd = hf * 4 + m4
                    bkr, bki = ri[m4]
                    act(r4[m4], bkr, AF.Sigmoid, bias=pvc(l, PV_BR + hd), scale=1.0)
                    act(i2[m4 % 2], bki, AF.Sigmoid, bias=pvc(l, PV_BI + hd), scale=1.0)
                    tt(xcs[m4], xcs[m4], i2[m4 % 2], ALU.mult)
                for m4 in range(4):
                    hd = hf * 4 + m4
                    act(a4[m4], r4[m4], AF.Exp, scale=B(clru[l][:, hd:hd + 1], PARAMS))
                for m4 in range(4):
                    hd = hf * 4 + m4
                    act(r4[m4], r4[m4], AF.Exp, scale=B(clru[l][:, 8 + hd:9 + hd], PARAMS))
                for m4 in range(4):
                    act(r4[m4], r4[m4], AF.Ln, bias=one_b, scale=-1.0)
                for m4 in range(4):
                    act(r4[m4], r4[m4], AF.Exp, scale=0.5)
                for m4 in range(4):
                    tt(xcs[m4], xcs[m4], r4[m4], ALU.mult)
                for m4 in range(4):
                    hd = hf * 4 + m4
                    hq = hseq4.s(hseq4.ap[:, m4, :], hseq4.keys[m4 * 4:(m4 + 1) * 4])
                    hs = B(hst[l][:, hd:hd + 1], [("hst", l, hd)])
                    scan(hq, a4[m4], xcs[m4], hs)
                    cp(hs, hq.s(hq.ap[:, T - 1:T]))
                wt = wload(win3, 0, C_GL + hf * 512)
                for m4 in range(4):
                    hd = hf * 4 + m4
                    bk = bank()
                    for kc in range(8):
                        mm(bk, wl(wt, kc, m4 * 128, 128), hn[kc], kc == 0, kc == 7)
                    act(gg, bk, AF.Gelu_apprx_tanh)
                    hq = hseq4.s(hseq4.ap[:, m4, :], hseq4.keys[m4 * 4:(m4 + 1) * 4])
                    tt(yb[hd], gg, hq, ALU.mult)
            if STOP < 4:
                return
            mark('Cconv')
            AR.reset()
            xsT = mg
            BT = AR.bf16(4 * T, shape=(4, T))
            CT = AR.bf16(4 * T, shape=(4, T))
            arena_mark = AR.p
            xbufs = [AR.f32(T + 4) for _ in range(4)]
            accs = [AR.f32(T) for _ in range(4)]
            xcvs = [AR.f32(T) for _ in range(4)]
            cbks = {}
            ctiles = (C_X, C_X + 512, C_B, C_C)
            nproj = [0]

            def cproj_upto(ch):
                while nproj[0] <= min(ch, 15):
                    wt = wload(win3, 0, ctiles[nproj[0] // 4])
                    for m4 in range(4):
                        bk = bank()
                        for kc in range(8):
                            mm(bk, wl(wt, kc, m4 * 128, 128), hn[kc], kc == 0, kc == 7)
                        cbks[nproj[0]] = bk
                        nproj[0] += 1

            def cstA(ch):
                xbuf, acc = xbufs[ch % 4], accs[ch % 4]
                hl = B(hist_s[l][:, ch, 0:3], [("hs", l, ch)])
                cp(xbuf.s(xbuf.ap[:, 0:3]), hl)
                cp(hl, cbks[ch].s(cbks[ch].ap[:, T - 3:T]))
                cp(xbuf.s(xbuf.ap[:, 3:3 + T]), cbks[ch], eng="act")
                act(acc, cbks[ch], AF.Identity, bias=pvc(l, PV_SCB + ch), scale=pvc(l, PV_SCW + ch * 4 + 3))

            def cstB2(chs):
                for k in range(3):
                    for ch in chs:
                        dst = xcvs[ch % 4] if k == 2 else accs[ch % 4]
                        stt(dst, xbufs[ch % 4].s(xbufs[ch % 4].ap[:, k:k + T]), pvc(l, PV_SCW + ch * 4 + k), accs[ch % 4],
                            ALU.mult, ALU.add)
                for ch in chs:
                    if ch < 8:
                        dst = xsT[ch]
                    elif ch < 12:
                        dst = BT.s(BT.ap[:, ch - 8, :])
                    else:
                        dst = CT.s(CT.ap[:, ch - 12, :])
                    act(dst, xcvs[ch % 4], AF.Silu)

            cproj_upto(1)
            cstA(0)
            cstA(1)
            for p in range(8):
                if p + 1 < 8:
                    cproj_upto(2 * p + 3)
                    cstA(2 * p + 2)
                    cstA(2 * p + 3)
                cstB2((2 * p, 2 * p + 1))
            mark('Cdt')
            sm_p[0] = 0
            NH4 = NJ * 16
            dtx, ax, ex, lgx, dt_, adt, cs_sb, ecs, tmc, ds, cd, dtds = [sm(NH4) for _ in range(12)]
            psd = bank()
            for j in range(NJ):
                for kc in range(8):
                    mm(psd.s(psd.ap[:, j * 16:(j + 1) * 16]), hn[kc].s(hn[kc].ap[:, j * 128:(j + 1) * 128]),
                       B(wdt_bf[l][:, kc, :], PARAMS), kc == 0, kc == 7)
            v4j = lambda b_: b_.s(b_.ap.rearrange("p (a b) -> p a b", a=NJ))
            tt(v4j(dtx), psd.s(psd.ap[:, 0:NH4].rearrange("p (a b) -> p a b", a=NJ)),
               B(bc(pr[l][:, PR_DTB:PR_DTB + 16], [128, NJ, 16], 1), PARAMS), ALU.add)
            ts(ax, dtx, -1.0, ALU.mult)
            tt(ax, ax, dtx, ALU.max)
            act(ex, ax, AF.Exp, scale=-1.0)
            act(lgx, ex, AF.Ln, bias=one_b, scale=1.0)
            ts(dt_, dtx, 0.0, ALU.max)
            tt(dt_, dt_, lgx, ALU.add)
            tt(v4j(adt), v4j(dt_), B(bc(arow[l][:], [128, NJ, 16], 1), PARAMS), ALU.mult)
            mark('Cz')
            szall = [B(uT_t[:, 2 * j:2 * j + 2, :].rearrange("p a b -> p (a b)"), [("uT", 2 * j), ("uT", 2 * j + 1)])
                     for j in range(NJ)]
            for hf in range(2):
                wt = wload(win3, 0, C_Z + hf * 512)
                for j in range(NJ):
                    bk = bank()
                    for kc in range(8):
                        mm(bk, hn[kc].s(hn[kc].ap[:, j * 128:(j + 1) * 128]), wl(wt, kc, 0, 512), kc == 0, kc == 7)
                    act(szall[j].s(szall[j].ap[:, hf * 512:(hf + 1) * 512]), bk, AF.Silu)
            mark('Cdt2')
            for j in range(NJ):
                mm(psd.s(psd.ap[:, NH4 + j * 16:NH4 + (j + 1) * 16]), B(U_f, PARAMS), adt.s(adt.ap[:, j * 16:(j + 1) * 16]))
                mm(psd.s(psd.ap[:, 2 * NH4 + j * 16:2 * NH4 + (j + 1) * 16]), B(ones_f[:], PARAMS),
                   adt.s(adt.ap[:, j * 16:(j + 1) * 16]))
            cp(cs_sb, psd.s(psd.ap[:, NH4:2 * NH4]))
            act(ecs, cs_sb, AF.Exp)
            tt(tmc, psd.s(psd.ap[:, 2 * NH4:3 * NH4]), cs_sb, ALU.subtract)
            act(ds, tmc, AF.Exp)
            act(cd, psd.s(psd.ap[:, 2 * NH4:3 * NH4]), AF.Exp)
            tt(dtds, dt_, ds, ALU.mult)
            AR.p = arena_mark
            Lb = AR.f32(16 * 128, shape=(16, 128))
            Lq = [Lb.s(Lb.ap[:, q * 4:(q + 1) * 4, :], Lb.keys[q * 4:(q + 1) * 4]) for q in range(4)]
            cbm = AR.f32(4 * 128, shape=(4, 128))
            FB = []
            for _ in range(2):
                FB.append(dict(Mb=AR.bf16(16 * 128, shape=(16, 128)), xs_bf=AR.bf16(1024), xdt=AR.bf16(1024),
                               xdd=AR.bf16(1024), Btok=AR.bf16(512)))
            ytmp = AR.f32(1024)
            yn = AR.bf16(1024)
            ss = sm(4)
            rs4 = sm(4)
            v3 = lambda b_: b_.s(b_.ap.rearrange("p (a b) -> p a b", a=16))
            v4 = lambda b_: b_.s(b_.ap.rearrange("p (a b) -> p a b", a=4))

            def hs(b_, j):
                return b_.s(b_.ap[:, j * 16:(j + 1) * 16])

            def front(j):
                js = slice(j * 128, (j + 1) * 128)
                f = FB[j % 2]
                adt_j, dt_j, dtds_j = hs(adt, j), hs(dt_, j), hs(dtds, j)
                tt(Lb, B(bc(Ls_f, [128, 16, 128], 1), PARAMS), adt_j.s(bc(adt_j.ap, [128, 16, 128], 2)), ALU.mult, eng="pool")
                psc = bank()
                for g in range(4):
                    mm(psc.s(psc.ap[:, g * 128:(g + 1) * 128]), BT.s(BT.ap[:, g, js]), CT.s(CT.ap[:, g, js]))
                tt(cbm, psc.s(psc.ap.rearrange("p (a b) -> p a b", a=4)), B(bc(U_f, [128, 4, 128], 1), PARAMS), ALU.mult)
                for q in range(4):
                    psg = bank()
                    for r in range(4):
                        hd = q * 4 + r
                        mm(psg.s(psg.ap[:, r * 128:(r + 1) * 128]), Lq[q].s(Lb.ap[:, hd, :]), B(U_f, PARAMS))
                    act(Lq[q], psg.s(psg.ap.rearrange("p (a b) -> p a b", a=4)), AF.Exp)
                    tt(f["Mb"].s(f["Mb"].ap[:, q * 4:(q + 1) * 4, :]), Lq[q], cbm.s(bc(cbm.ap[:, q, :], [128, 4, 128], 1)),
                       ALU.mult, eng="pool")
                pst = bank()
                pstb = pst.s(pst.ap.bitcast(BF16))
                for kc in range(8):
                    tr(pstb.s(pstb.ap[:, kc * 128:(kc + 1) * 128]), xsT[kc].s(xsT[kc].ap[:, js]))
                cp(f["xs_bf"], pstb, eng="act")
                tt(v3(f["xdt"]), v3(pstb), dt_j.s(bc(dt_j.ap, [128, 16, 64], 2)), ALU.mult)
                tt(v3(f["xdd"]), v3(pstb), dtds_j.s(bc(dtds_j.ap, [128, 16, 64], 2)), ALU.mult)
                psb = bank()
                psbb = psb.s(psb.ap.bitcast(BF16))
                for g in range(4):
                    tr(psbb.s(psbb.ap[:, g * 128:(g + 1) * 128]), BT.s(BT.ap[:, g, js]))
                cp(f["Btok"], psbb.s(psbb.ap[:, 0:512]), eng="act")

            def back(j):
                js = slice(j * 128, (j + 1) * 128)
                f = FB[j % 2]
                Mb, xs_bf, xdt, xdd, Btok = f["Mb"], f["xs_bf"], f["xdt"], f["xdd"], f["Btok"]
                ecs_j, cd_j = hs(ecs, j), hs(cd, j)
                yo = bank(2)
                for g in range(4):
                    mm(yo.s(yo.ap[:, g * 256:(g + 1) * 256]), CT.s(CT.ap[:, g, js]),
                       B(S_bf[l][:, g * 256:(g + 1) * 256], [("Sbf", l)]))
                yd = bank(2)
                for hd in range(16):
                    o_ = yd.s(yd.ap[:, hd * 64:(hd + 1) * 64])
                    mm(o_, Mb.s(Mb.ap[:, hd, :]), xdt.s(xdt.ap[:, hd * 64:(hd + 1) * 64]), True, False)
                    mm(o_, B(Dg[l][:, hd, :], PARAMS), xs_bf.s(xs_bf.ap[:, hd * 64:(hd + 1) * 64]), False, True)
                st = bank(2)
                for g in range(4):
                    mm(st.s(st.ap[:, g * 256:(g + 1) * 256]), Btok.s(Btok.ap[:, g * 128:(g + 1) * 128]),
                       xdd.s(xdd.ap[:, g * 256:(g + 1) * 256]))
                tt(v3(ytmp), v3(yo), ecs_j.s(bc(ecs_j.ap, [128, 16, 64], 2)), ALU.mult)
                tt(ytmp, yd, ytmp, ALU.add)
                Sb = B(S[l][:], [("S", l)])
                tt(v3(Sb), v3(Sb), cd_j.s(bc(cd_j.ap, [128, 16, 64], 2)), ALU.mult)
                tt(Sb, st, Sb, ALU.add)
                cp(B(S_bf[l][:], [("Sbf", l)]), Sb, eng="act")
                tt(ytmp, ytmp, szall[j], ALU.mult)
                memset(ss, 0.0)
                for g in range(4):
                    act(yn.s(yn.ap[:, g * 256:(g + 1) * 256]), ytmp.s(ytmp.ap[:, g * 256:(g + 1) * 256]), AF.Square,
                        accum=ss.s(ss.ap[:, g:g + 1]))
                act(rs4, ss, AF.Ln, bias=eps_b, scale=1.0 / 256)
                act(rs4, rs4, AF.Exp, scale=-0.5)
                tt(v4(yn), v4(ytmp), rs4.s(bc(rs4.ap, [128, 4, 256], 2)), ALU.mult)

            def back2(j):
                js = slice(j * 128, (j + 1) * 128)
                pyt = bank()
                pytb = pyt.s(pyt.ap.bitcast(BF16))
                for kc in range(8):
                    tr(pytb.s(pytb.ap[:, kc * 128:(kc + 1) * 128]), yn.s(yn.ap[:, kc * 128:(kc + 1) * 128]))
                for kc in range(8):
                    act(yc[kc].s(yc[kc].ap[:, js]), pytb.s(pytb.ap[:, kc * 128:(kc + 1) * 128]), AF.Copy,
                        scale=pvc(l, PV_SNG + kc))

            mark('Cchunks')
            front(0)
            front(1)
            back(0)
            for j in range(1, NJ):
                if j + 1 < NJ:
                    front(j + 1)
                back2(j - 1)
                back(j)
            back2(NJ - 1)
            if STOP < 5:
                return
            mark('merge')
            AR.reset()
            sm_p[0] = 0
            g4 = AR.f32(4 * T, shape=(4, T))
            tmpG2 = [AR.f32(T) for _ in range(2)]
            accm = uT
            for k, (wb, ybr) in enumerate(((w_ba, ya), (w_bb, yb), (w_bc, yc))):
                wb3 = w3(wb, l)
                for hf in range(2):
                    wt = wload(win3, 0, C_G + k * 1024 + hf * 512)
                    for m4 in range(4):
                        m = hf * 4 + m4
                        bk = bank()
                        for kc in range(8):
                            mm(bk, wl(wt, kc, m4 * 128, 128), hn[kc], kc == 0, kc == 7)
                        act(g4.s(g4.ap[:, m4, :], g4.keys[m4 * 4:(m4 + 1) * 4]), bk, AF.Sigmoid, bias=pvc(l, PV_BG + k * 8 + m), scale=1.0)
                    wt = wload(wb3, 0, hf * 512)
                    for pr_ in range(2):
                        bkp = []
                        for m4 in (2 * pr_, 2 * pr_ + 1):
                            bk = bank()
                            for kc in range(8):
                                mm(bk, wl(wt, kc, m4 * 128, 128), ybr[kc], kc == 0, kc == 7)
                            bkp.append((m4, bk))
                        for m4, bk in bkp:
                            m = hf * 4 + m4
                            gm = g4.s(g4.ap[:, m4, :], g4.keys[m4 * 4:(m4 + 1) * 4])
                            if k == 0:
                                tt(accm[m], bk, gm, ALU.mult)
                            else:
                                tt(tmpG2[m4 % 2], bk, gm, ALU.mult)
                        if k > 0:
                            for m4, bk in bkp:
                                m = hf * 4 + m4
                                tt(accm[m] if k == 1 else mg[m], accm[m], tmpG2[m4 % 2], ALU.add)
            if STOP < 6:
                return
            mark('outproj')
            wo3 = w3(w_out, l)
            sumsq_start()
            for hf in range(2):
                wt = wload(wo3, 0, hf * 512)
                for m4 in range(4):
                    m = hf * 4 + m4
                    bk = bank()
                    for kc in range(8):
                        mm(bk, wl(wt, kc, m4 * 128, 128), mg[kc], kc == 0, kc == 7)
                    tt(h[m], bk, h[m], ALU.add)
                    sumsq_chunk(m)
            if STOP < 7:
                return
            mark('mlp_up')
            rmsnorm(l, PV_NMLP)
            AR.reset()
            rl = [AR.f32(T) for _ in range(2)]
            wu3 = w3(w_up, l)
            for f4 in range(8):
                wt = wload(wu3, 0, f4 * 512)
                for m4 in range(4):
                    f = f4 * 4 + m4
                    bk = bank()
                    for kc in range(8):
                        mm(bk, wl(wt, kc, m4 * 128, 128), hn[kc], kc == 0, kc == 7)
                    act(rl[f % 2], bk, AF.Relu)
                    tt(up[f], rl[f % 2], rl[f % 2], ALU.mult)
            mark('mlp_down')
            wd3 = w3(w_dn, l)
            sumsq_start()
            for hf in range(2):
                bks = [bank() for _ in range(4)]
                for q in range(4):
                    wt = wload(wd3, q * 8, hf * 512)
                    for kc in range(8):
                        for m4 in range(4):
                            mm(bks[m4], wl(wt, kc, m4 * 128, 128), up[q * 8 + kc], (q == 0 and kc == 0), (q == 3 and kc == 7))
                for m4 in range(4):
                    m = hf * 4 + m4
                    tt(h[m], bks[m4], h[m], ALU.add)
                    sumsq_chunk(m)

        finals = []
        xT3 = xT.rearrange("(c p) t -> p c t", p=128)
        oT3 = outT.rearrange("(c p) t -> p c t", p=128)
        for ti in range(n_tiles):
            t0 = ti * T
            for c2 in range(2):
                P.dma("sp", (lambda c2, t0: (lambda e: e.dma_start(out=h_t[:, c2 * 4:(c2 + 1) * 4, :],
                                                                   in_=xT3[:, c2 * 4:(c2 + 1) * 4, t0:t0 + T])))(c2, t0),
                      "ld_x%d" % c2, writes=[("h", c) for c in range(c2 * 4, c2 * 4 + 4)])
            for l in range(DEPTH):
                layer(l, ti)
                if debug_taps and ti == 0:
                    tap("h_l%d" % l, B(h_t[:], [("h", c) for c in range(8)]), [128, 8, T])
            mark('final')
            rmsnorm(0, PV_FIN, out=uT)
            for c2 in range(2):
                finals.append(P.dma("sp", (lambda c2, t0: (lambda e: e.dma_start(out=oT3[:, c2 * 4:(c2 + 1) * 4, t0:t0 + T],
                                                                                 in_=uT_t[:, c2 * 4:(c2 + 1) * 4, :])))(c2, t0),
                                    "st_o%d" % c2, reads=[("uT", c) for c in range(c2 * 4, c2 * 4 + 4)]))
        finals.extend(taps.values())
        P.emit(final_waits=finals)
    return nc


def _vec8(v):
    return np.ascontiguousarray(v.reshape(-1, 128).T)


def prep_params(inp):
    f = lambda k: np.asarray(inp[k], dtype=np.float32)
    pvec = np.zeros((DEPTH, 128, NPV), np.float32)
    prow = np.zeros((DEPTH, 128, NPR), np.float32)
    for l in range(DEPTH):
        pvec[l, :, PV_NMIX:PV_NMIX + 8] = _vec8(f("norm_mix_g")[l])
        pvec[l, :, PV_NMLP:PV_NMLP + 8] = _vec8(f("norm_mlp_g")[l])
        pvec[l, :, PV_LNG:PV_LNG + 8] = _vec8(f("gmlp_ln_g")[l])
        pvec[l, :, PV_LNB:PV_LNB + 8] = _vec8(f("gmlp_ln_b")[l])
        cw = f("lru_conv_w")[l]
        pvec[l, :, PV_LCW:PV_LCW + 32] = cw.reshape(4, 8, 128).transpose(2, 1, 0).reshape(128, 32)
        pvec[l, :, PV_LCB:PV_LCB + 8] = _vec8(f("lru_conv_b")[l])
        pvec[l, :, PV_BR:PV_BR + 8] = _vec8(f("lru_b_r")[l])
        pvec[l, :, PV_BI:PV_BI + 8] = _vec8(f("lru_b_i")[l])
        pvec[l, :, PV_LAM:PV_LAM + 8] = _vec8(f("lru_lambda")[l])
        sw = f("ssd_conv_w")[l]
        pvec[l, :, PV_SCW:PV_SCW + 64] = sw.reshape(4, 16, 128).transpose(2, 1, 0).reshape(128, 64)
        pvec[l, :, PV_SCB:PV_SCB + 16] = _vec8(f("ssd_conv_b")[l])
        pvec[l, :, PV_BG:PV_BG + 24] = _vec8(f("b_gate")[l].reshape(-1))
        pvec[l, :, PV_SNG:PV_SNG + 8] = _vec8(f("ssd_norm_g")[l])
        pvec[l, :, PV_FIN:PV_FIN + 8] = _vec8(f("final_norm_g"))
        prow[l, :, PR_DTB:PR_DTB + 16] = f("ssd_dt_bias")[l][None, :]
        prow[l, :, PR_ALOG:PR_ALOG + 16] = f("ssd_a_log")[l][None, :]
        prow[l, :, PR_D:PR_D + 16] = f("ssd_d")[l][None, :]
    bsrow = np.ascontiguousarray(f("gmlp_b_s").reshape(DEPTH, 1, 1024))
    gwT = np.ascontiguousarray(f("gmlp_w_s").transpose(0, 3, 1, 2))
    wr = np.ascontiguousarray(f("lru_w_r").transpose(0, 2, 1, 3))
    wi = np.ascontiguousarray(f("lru_w_i").transpose(0, 2, 1, 3))
    consts = np.zeros((128, 3, 128), np.float32)
    consts[:, 0, :] = np.eye(128, dtype=np.float32)
    consts[:, 1, :] = np.triu(np.ones((128, 128), np.float32))
    consts[:, 2, :] = np.tril(np.ones((128, 128), np.float32), -1)
    return dict(pvec=pvec, prow=prow, bsrow=bsrow, gwT=gwT, wr=wr, wi=wi, consts=consts)


_CACHE = {}
PHASES = []


def kernel(**inputs):
    x = np.asarray(inputs["x"], dtype=np.float32)
    shared = prep_params(inputs)
    for k in ("w_in", "w_branch_a", "w_branch_b", "w_branch_c", "w_out", "w_mlp_up", "w_mlp_down"):
        shared[k] = np.ascontiguousarray(np.asarray(inputs[k], dtype=np.float32))
    n_tiles = SEQ // T
    if "nc" not in _CACHE:
        _CACHE["nc"] = build_program(n_tiles)
    nc = _CACHE["nc"]
    in_maps = []
    for core in range(N_CORES):
        b = core % BATCH
        m = dict(shared)
        m["xT"] = np.ascontiguousarray(x[b].T)
        in_maps.append(m)
    res = run_bass_kernel_spmd(nc, in_maps, core_ids=list(range(N_CORES)))
    out = np.empty((BATCH, SEQ, D), np.float32)
    for b in range(BATCH):
        out[b] = res.results[b]["outT"].T
    return out
```

```python
from contextlib import ExitStack

import numpy as np
import concourse.bass as bass
import concourse.mybir as mybir
from concourse.bass_utils import run_bass_kernel_spmd

F32 = mybir.dt.float32
BF16 = mybir.dt.bfloat16
AF = mybir.ActivationFunctionType
ALU = mybir.AluOpType
AX = mybir.AxisListType

D = 1024
SEQ = 4096
BATCH = 4
DEPTH = 2
D_IN = 10256
T = 512
NJ = T // 128
EPS = 1e-6
N_CORES = 8

PV_NMIX, PV_NMLP, PV_LNG, PV_LNB, PV_LCW, PV_LCB, PV_BR, PV_BI, PV_LAM = 0, 8, 16, 24, 32, 64, 72, 80, 88
PV_SCW, PV_SCB, PV_BG, PV_SNG, PV_FIN, NPV = 96, 160, 176, 200, 208, 216
PR_DTB, PR_ALOG, PR_D, NPR = 0, 16, 32, 48
C_U, C_V, C_XB, C_GL, C_Z, C_X, C_B, C_C, C_DT, C_G = 0, 1024, 2048, 3072, 4096, 5120, 6144, 6656, 7168, 7184


class _Op:
    __slots__ = ("eng", "fn", "deps", "signal", "count", "dma_sem", "dma_count")

    def __init__(self, eng, fn, deps):
        self.eng = eng
        self.fn = fn
        self.deps = deps
        self.signal = False
        self.count = None
        self.dma_sem = None
        self.dma_count = None


class Prog:
    ENGS = ("pe", "act", "dve", "pool", "sp")

    def __init__(self, nc):
        self.nc = nc
        self.ops = {e: [] for e in self.ENGS}
        self.last_writer = {}
        self.readers = {}
        self.dma_sems = {}

    def _deps(self, reads, writes):
        deps = []
        for k in reads:
            w = self.last_writer.get(k)
            if w is not None:
                deps.append(w)
        for k in writes:
            w = self.last_writer.get(k)
            if w is not None:
                deps.append(w)
            deps.extend(self.readers.get(k, {}).values())
        return deps

    def _commit(self, op, reads, writes):
        rk = op.eng if op.dma_sem is None else ("dma", id(op))
        for k in reads:
            self.readers.setdefault(k, {})[rk] = op
        for k in writes:
            self.last_writer[k] = op
            self.readers[k] = {}

    def op(self, eng, fn, reads=(), writes=()):
        psr = [k for k in reads if isinstance(k, tuple) and k[0] == "ps"]
        if psr:
            writes = list(writes) + [k for k in psr if k not in writes]
        deps = self._deps(reads, writes)
        o = _Op(eng, fn, deps)
        for d in deps:
            d.signal = True
        self.ops[eng].append(o)
        self._commit(o, reads, writes)
        return o

    def dma(self, eng, fn, sem, reads=(), writes=()):
        deps = self._deps(reads, writes)
        o = _Op(eng, fn, deps)
        for d in deps:
            d.signal = True
        c = self.dma_sems.get(sem, 0) + 16
        self.dma_sems[sem] = c
        o.dma_sem = sem
        o.dma_count = c
        self.ops[eng].append(o)
        self._commit(o, reads, writes)
        return o

    def emit(self, final_waits=()):
        nc = self.nc
        with ExitStack() as es:
            esem = {e: es.enter_context(nc.semaphore("s_" + e)) for e in self.ENGS}
            dsem = {n: es.enter_context(nc.semaphore("d_" + n)) for n in self.dma_sems}
            for o in final_waits:
                o.signal = True
            for e in self.ENGS:
                c = 0
                for o in self.ops[e]:
                    if o.dma_sem is None and o.signal:
                        c += 1
                        o.count = c
            block = es.enter_context(nc.Block())

            def run(e, engine):
                waited = {}
                for o in self.ops[e]:
                    need = {}
                    for d in o.deps:
                        if d.dma_sem is not None:
                            key = ("d", d.dma_sem)
                            val = d.dma_count
                        else:
                            if d.eng == e and e in ("pe", "sp"):
                                continue
                            key = ("e", d.eng)
                            val = d.count
                        if val > need.get(key, 0):
                            need[key] = val
                    for key, val in need.items():
                        if waited.get(key, 0) >= val:
                            continue
                        waited[key] = val
                        s = dsem[key[1]] if key[0] == "d" else esem[key[1]]
                        engine.wait_ge(s, val)
                    ins = o.fn(engine)
                    if o.dma_sem is not None:
                        ins.then_inc(dsem[o.dma_sem], 16)
                    elif o.signal:
                        ins.then_inc(esem[e], 1)
                if e == "sp":
                    for o in final_waits:
                        if o.dma_sem is not None:
                            engine.wait_ge(dsem[o.dma_sem], o.dma_count)
                        else:
                            engine.wait_ge(esem[o.eng], o.count)

            @block.tensor
            def _(eng):
                run("pe", eng)

            @block.scalar
            def _(eng):
                run("act", eng)

            @block.vector
            def _(eng):
                run("dve", eng)

            @block.gpsimd
            def _(eng):
                run("pool", eng)

            @block.sync
            def _(eng):
                run("sp", eng)


class B:
    __slots__ = ("ap", "keys")

    def __init__(self, ap, keys):
        self.ap = ap
        self.keys = list(keys)

    def s(self, ap, keys=None):
        return B(ap, self.keys if keys is None else keys)


def bc(ap, shape, axis):
    return ap.unsqueeze(axis).to_broadcast(list(shape))


def build_program(n_tiles, debug_taps=False):
    nc = bass.Bass("TRN2", target_bir_lowering=False)
    P = Prog(nc)

    def din(name, shape):
        return nc.dram_tensor(name, list(shape), F32, kind="ExternalInput").ap()

    xT = din("xT", [D, SEQ])
    w_in = din("w_in", [DEPTH, D, D_IN])
    w_ba = din("w_branch_a", [DEPTH, D, D])
    w_bb = din("w_branch_b", [DEPTH, D, D])
    w_bc = din("w_branch_c", [DEPTH, D, D])
    w_out = din("w_out", [DEPTH, D, D])
    w_up = din("w_mlp_up", [DEPTH, D, 4 * D])
    w_dn = din("w_mlp_down", [DEPTH, 4 * D, D])
    pvec_d = din("pvec", [DEPTH, 128, NPV])
    prow_d = din("prow", [DEPTH, 128, NPR])
    bsrow_d = din("bsrow", [DEPTH, 1, 1024])
    gwT_d = din("gwT", [DEPTH, 128, 8, 128])
    wr_d = din("wr", [DEPTH, 128, 8, 128])
    wi_d = din("wi", [DEPTH, 128, 8, 128])
    consts_d = din("consts", [128, 3, 128])
    outT = nc.dram_tensor("outT", [D, SEQ], F32, kind="ExternalOutput").ap()
    taps = {}

    with ExitStack() as es:
        def sb(name, shape, dt):
            return es.enter_context(nc.sbuf_tensor("sb_" + name, list(shape), dt))

        consts = sb("consts", [128, 3, 128], F32)
        ident_bf = sb("ident_bf", [128, 128], BF16)
        U_f = consts[:, 1, :]
        Ls_f = consts[:, 2, :]
        ones_bf = sb("ones_bf", [128, 128], BF16)
        ones_f = sb("ones_f", [128, 128], F32)
        cst = sb("cst", [128, 4], F32)
        pv = [sb("pv%d" % l, [128, NPV], F32) for l in range(DEPTH)]
        pr = [sb("pr%d" % l, [128, NPR], F32) for l in range(DEPTH)]
        gw_bf = [sb("gw%d" % l, [128, 8, 128], BF16) for l in range(DEPTH)]
        wr_bf = [sb("wr%d" % l, [128, 8, 128], BF16) for l in range(DEPTH)]
        wi_bf = [sb("wi%d" % l, [128, 8, 128], BF16) for l in range(DEPTH)]
        wdt_bf = [sb("wdt%d" % l, [128, 8, 16], BF16) for l in range(DEPTH)]
        Eg = [sb("Eg%d" % l, [128, 8, 128], F32) for l in range(DEPTH)]
        Dg = [sb("Dg%d" % l, [128, 16, 128], BF16) for l in range(DEPTH)]
        clru = [sb("clru%d" % l, [128, 16], F32) for l in range(DEPTH)]
        arow = [sb("arow%d" % l, [128, 16], F32) for l in range(DEPTH)]
        S = [sb("S%d" % l, [128, 1024], F32) for l in range(DEPTH)]
        S_bf = [sb("Sbf%d" % l, [128, 1024], BF16) for l in range(DEPTH)]
        hst = [sb("hst%d" % l, [128, 8], F32) for l in range(DEPTH)]
        hist_l = [sb("histl%d" % l, [128, 8, 4], F32) for l in range(DEPTH)]
        hist_s = [sb("hists%d" % l, [128, 16, 4], F32) for l in range(DEPTH)]
        h_t = sb("h", [128, 8, T], F32)
        hn_t = sb("hn", [128, 8, T], BF16)
        sqb_t = sb("sqb", [128, 2, T], BF16)
        rstd_t = sb("rstd", [128, T], F32)
        lnt_t = sb("lnt", [128, T], F32)
        uT_t = sb("uT", [128, 8, T], F32)
        up_t = sb("up", [128, 32, T], BF16)
        NW = 4
        wring = [sb("wr_ring%d" % i, [128, 8, 512], BF16) for i in range(NW)]
        ARENA_N = 12288
        arena = sb("arena", [128, ARENA_N], F32)
        small = sb("small", [128, 896], F32)
        ps = es.enter_context(nc.psum_tensor("ps", [128, 4096], F32))
        bsrow = arena

        class Arena:
            def __init__(self):
                self.p = 0

            def reset(self):
                self.p = 0

            def f32(self, n, shape=None):
                a, b = self.p, self.p + n
                assert b <= ARENA_N, "arena overflow"
                self.p = (b + 127) // 128 * 128
                ap = arena[:, a:b]
                if shape is not None:
                    ap = ap.rearrange("p (a b) -> p a b", a=shape[0])
                return B(ap, [("A", g) for g in range(a // 128, (b + 127) // 128)])

            def bf16(self, n, shape=None):
                w = (n + 1) // 2
                a, b = self.p, self.p + w
                assert b <= ARENA_N, "arena overflow"
                self.p = (b + 127) // 128 * 128
                ap = arena[:, a:b].bitcast(BF16)
                if shape is not None:
                    ap = ap.rearrange("p (a b) -> p a b", a=shape[0])
                return B(ap, [("A", g) for g in range(a // 128, (b + 127) // 128)])

        AR = Arena()
        sm_p = [0]

        def sm(n):
            a = sm_p[0]
            sm_p[0] += n
            assert sm_p[0] <= 896
            return B(small[:, a:a + n], [("sm", w) for w in range(a // 8, (a + n - 1) // 8 + 1)])

        bank_p = [0]

        def bank(n=1):
            p = bank_p[0]
            if p % n:
                p += n - p % n
            if p + n > 8:
                p = 0
            bank_p[0] = (p + n) % 8
            return B(ps[:, p * 512:(p + n) * 512], [("ps", p + i) for i in range(n)])

        h = [B(h_t[:, c, :], [("h", c)]) for c in range(8)]
        hn = [B(hn_t[:, c, :], [("hn", c)]) for c in range(8)]
        sqb = [B(sqb_t[:, i, :], [("sqb", i)]) for i in range(2)]
        rstd = B(rstd_t[:], ["rstd"])
        lnt = B(lnt_t[:], ["lnt"])
        uT = [B(uT_t[:, c, :], [("uT", c)]) for c in range(8)]
        up = [B(up_t[:, c, :], [("up", c)]) for c in range(32)]
        ya, yb, yc, mg = up[0:8], up[8:16], up[16:24], up[24:32]
        PARAMS = ["params"]

        def pvc(l, col, n=1):
            return B(pv[l][:, col:col + n], PARAMS)

        def mm(out, lhsT, rhs, start=True, stop=True):
            P.op("pe", lambda e: e.matmul(out.ap, lhsT=lhsT.ap, rhs=rhs.ap, start=start, stop=stop),
                 reads=lhsT.keys + rhs.keys, writes=out.keys)

        def tr(out, in_):
            P.op("pe", lambda e: e.transpose(out=out.ap, in_=in_.ap, identity=ident_bf[:]),
                 reads=in_.keys + PARAMS, writes=out.keys)

        def act(out, in_, func, bias=None, scale=None, accum=None):
            rd = list(in_.keys)
            wr = list(out.keys)
            kw = {}
            if bias is not None:
                if isinstance(bias, B):
                    rd += bias.keys
                    kw["bias"] = bias.ap
                else:
                    kw["bias"] = bias
            if scale is not None:
                if isinstance(scale, B):
                    rd += scale.keys
                    kw["scale"] = scale.ap
                else:
                    kw["scale"] = scale
            if accum is not None:
                wr += accum.keys
                kw["accum_out"] = accum.ap
            P.op("act", lambda e: e.activation(out=out.ap, in_=in_.ap, func=func, **kw), reads=rd, writes=wr)

        def tt(out, in0, in1, op, eng="dve"):
            P.op(eng, lambda e: e.tensor_tensor(out=out.ap, in0=in0.ap, in1=in1.ap, op=op),
                 reads=in0.keys + in1.keys, writes=out.keys)

        def ts(out, in0, s1, op0, s2=None, op1=None, eng="dve"):
            rd = list(in0.keys)
            a1 = s1
            a2 = s2
            if isinstance(s1, B):
                rd += s1.keys
                a1 = s1.ap
            if isinstance(s2, B):
                rd += s2.keys
                a2 = s2.ap
            if op1 is None:
                P.op(eng, lambda e: e.tensor_scalar(out=out.ap, in0=in0.ap, scalar1=a1, scalar2=None, op0=op0),
                     reads=rd, writes=out.keys)
            else:
                P.op(eng, lambda e: e.tensor_scalar(out=out.ap, in0=in0.ap, scalar1=a1, scalar2=a2, op0=op0, op1=op1),
                     reads=rd, writes=out.keys)

        def stt(out, in0, scalar, in1, op0, op1, eng="dve"):
            rd = in0.keys + in1.keys
            a = scalar
            if isinstance(scalar, B):
                rd = rd + scalar.keys
                a = scalar.ap
            P.op(eng, lambda e: e.scalar_tensor_tensor(out=out.ap, in0=in0.ap, scalar=a, in1=in1.ap, op0=op0, op1=op1),
                 reads=rd, writes=out.keys)

        def cp(out, in_, eng="dve"):
            if eng == "act":
                P.op("act", lambda e: e.activation(out=out.ap, in_=in_.ap, func=AF.Copy), reads=in_.keys, writes=out.keys)
            else:
                P.op(eng, lambda e: e.tensor_copy(out=out.ap, in_=in_.ap), reads=in_.keys, writes=out.keys)

        def memset(buf, val, eng="dve"):
            P.op(eng, lambda e: e.memset(buf.ap, val), writes=buf.keys)

        def red_sum(out, in_):
            P.op("dve", lambda e: e.tensor_reduce(out=out.ap, in_=in_.ap, axis=AX.X, op=ALU.add),
                 reads=in_.keys, writes=out.keys)

        def scan(out, d0, d1, init):
            P.op("dve", lambda e: e.tensor_tensor_scan(out=out.ap, data0=d0.ap, data1=d1.ap, initial=init.ap,
                                                       op0=ALU.mult, op1=ALU.add),
                 reads=d0.keys + d1.keys + init.keys, writes=out.keys)

        def tap(name, buf, shape):
            if not debug_taps:
                return
            t = nc.dram_tensor("tap_" + name, list(shape), buf.ap.dtype, kind="ExternalOutput").ap()
            taps[name] = P.dma("sp", lambda e: e.dma_start(out=t, in_=buf.ap), "tap_" + name, reads=buf.keys)

        def w3(w, l):
            return w[l].rearrange("(kc p) e -> p kc e", p=128)

        def wsched(l):
            win3 = w3(w_in, l)
            sch = []
            for c0 in (C_V, C_V + 512, C_U, C_U + 512, C_XB, C_GL, C_XB + 512, C_GL + 512, C_X, C_X + 512, C_B, C_C,
                       C_Z, C_Z + 512):
                sch.append((win3, 0, c0))
            for k, wb in enumerate((w_ba, w_bb, w_bc)):
                for hf in range(2):
                    sch.append((win3, 0, C_G + k * 1024 + hf * 512))
                    sch.append((w3(wb, l), 0, hf * 512))
            for hf in range(2):
                sch.append((w3(w_out, l), 0, hf * 512))
            for f4 in range(8):
                sch.append((w3(w_up, l), 0, f4 * 512))
            for hf in range(2):
                for q in range(4):
                    sch.append((w3(w_dn, l), q * 8, hf * 512))
            return sch

        WSTREAM = []
        for ti_ in range(n_tiles):
            for l_ in range(DEPTH):
                WSTREAM.extend(wsched(l_))
        PF = NW - 1
        wcur = [0]
        wissued = [0]

        def _issue(i):
            src3, kc0, c0 = WSTREAM[i]
            si = i % NW
            slot = wring[si]
            for hf in range(2):
                a, b = hf * 4, hf * 4 + 4
                dst = slot[:, a:b, :]
                src = src3[:, kc0 + a:kc0 + b, c0:c0 + 512]
                P.dma("pool", (lambda dst, src: (lambda e: e.dma_start(out=dst, in_=src)))(dst, src),
                      "w%d_%d" % (si, hf), writes=[("w", si, hf)])

        def wload(src3, kc0, c0, n=512):
            i = wcur[0]
            wcur[0] += 1
            assert WSTREAM[i][1] == kc0 and WSTREAM[i][2] == c0, ("weight schedule mismatch", i, kc0, c0, WSTREAM[i][1:])
            while wissued[0] < min(len(WSTREAM), i + PF + 1):
                _issue(wissued[0])
                wissued[0] += 1
            si = i % NW
            return wring[si], [("w", si, 0), ("w", si, 1)]

        def wl(slot_keys, kc, c0, n):
            slot, keys = slot_keys
            return B(slot[:, kc, c0:c0 + n], [keys[kc // 4]])

        P.dma("sp", lambda e: e.dma_start(out=consts[:], in_=consts_d), "ld_c", writes=PARAMS)
        for l in range(DEPTH):
            P.dma("sp", (lambda l: (lambda e: e.dma_start(out=pv[l][:], in_=pvec_d[l])))(l), "ld_pv%d" % l, writes=PARAMS)
            P.dma("sp", (lambda l: (lambda e: e.dma_start(out=pr[l][:], in_=prow_d[l])))(l), "ld_pr%d" % l, writes=PARAMS)
            P.dma("sp", (lambda l: (lambda e: e.dma_start(out=bsrow[0:1, l * 1024:(l + 1) * 1024], in_=bsrow_d[l])))(l),
                  "ld_bs%d" % l, writes=PARAMS)
            P.dma("pool", (lambda l: (lambda e: e.dma_start(out=gw_bf[l][:], in_=gwT_d[l])))(l), "ld_gw%d" % l, writes=PARAMS)
            P.dma("pool", (lambda l: (lambda e: e.dma_start(out=wr_bf[l][:], in_=wr_d[l])))(l), "ld_wr%d" % l, writes=PARAMS)
            P.dma("pool", (lambda l: (lambda e: e.dma_start(out=wi_bf[l][:], in_=wi_d[l])))(l), "ld_wi%d" % l, writes=PARAMS)
            P.dma("pool", (lambda l: (lambda e: e.dma_start(out=wdt_bf[l][:], in_=w3(w_in, l)[:, :, C_DT:C_DT + 16])))(l),
                  "ld_wdt%d" % l, writes=PARAMS)
        PB = B(None, PARAMS)
        P.op("dve", lambda e: e.memset(ones_bf[:], 1.0), reads=PARAMS, writes=PARAMS)
        P.op("dve", lambda e: e.memset(ones_f[:], 1.0), reads=PARAMS, writes=PARAMS)
        P.op("dve", lambda e: e.memset(cst[:, 0:1], EPS), reads=PARAMS, writes=PARAMS)
        P.op("dve", lambda e: e.memset(cst[:, 1:2], 1.0), reads=PARAMS, writes=PARAMS)
        P.op("dve", lambda e: e.tensor_copy(out=ident_bf[:], in_=consts[:, 0, :]), reads=PARAMS, writes=PARAMS)
        eps_b = B(cst[:, 0:1], PARAMS)
        one_b = B(cst[:, 1:2], PARAMS)
        for l in range(DEPTH):
            for t_, n_ in ((S[l], 1024), (hst[l], 8)):
                P.op("dve", (lambda t_: (lambda e: e.memset(t_[:], 0.0)))(t_), reads=PARAMS, writes=PARAMS)
            P.op("dve", (lambda l: (lambda e: e.memset(S_bf[l][:], 0.0)))(l), reads=PARAMS, writes=PARAMS)
            P.op("dve", (lambda l: (lambda e: e.memset(hist_l[l][:], 0.0)))(l), reads=PARAMS, writes=PARAMS)
            P.op("dve", (lambda l: (lambda e: e.memset(hist_s[l][:], 0.0)))(l), reads=PARAMS, writes=PARAMS)
            P.op("dve", (lambda l: (lambda e: e.tensor_tensor(out=gw_bf[l][:], in0=gw_bf[l][:], in1=bc(U_f, [128, 8, 128], 1),
                                                               op=ALU.mult)))(l), reads=PARAMS, writes=PARAMS)
            P.op("act", (lambda l: (lambda e: e.activation(out=clru[l][:, 0:8], in_=pv[l][:, PV_LAM:PV_LAM + 8], func=AF.Exp,
                                                           scale=-1.0)))(l), reads=PARAMS, writes=PARAMS)
            P.op("act", (lambda l: (lambda e: e.activation(out=clru[l][:, 0:8], in_=clru[l][:, 0:8], func=AF.Ln,
                                                           bias=cst[:, 1:2], scale=1.0)))(l), reads=PARAMS, writes=PARAMS)
            P.op("dve", (lambda l: (lambda e: e.tensor_scalar(out=clru[l][:, 8:16], in0=clru[l][:, 0:8], scalar1=-16.0,
                                                              scalar2=None, op0=ALU.mult)))(l), reads=PARAMS, writes=PARAMS)
            P.op("dve", (lambda l: (lambda e: e.tensor_scalar(out=clru[l][:, 0:8], in0=clru[l][:, 0:8], scalar1=-8.0,
                                                              scalar2=None, op0=ALU.mult)))(l), reads=PARAMS, writes=PARAMS)
            P.op("act", (lambda l: (lambda e: e.activation(out=arow[l][:], in_=pr[l][:, PR_ALOG:PR_ALOG + 16],
                                                           func=AF.Exp)))(l), reads=PARAMS, writes=PARAMS)
            P.op("dve", (lambda l: (lambda e: e.tensor_scalar(out=arow[l][:], in0=arow[l][:], scalar1=-1.0, scalar2=None,
                                                              op0=ALU.mult)))(l), reads=PARAMS, writes=PARAMS)
            P.op("dve", (lambda l: (lambda e: e.tensor_tensor(out=Dg[l][:], in0=bc(consts[:, 0, :], [128, 16, 128], 1),
                                                               in1=bc(pr[l][:, PR_D:PR_D + 16], [128, 16, 128], 2),
                                                               op=ALU.mult)))(l), reads=PARAMS, writes=PARAMS)
            bkA = bank(2)
            bkB = bank(2)
            for g in range(8):
                P.op("pe", (lambda l, g, bk: (lambda e: e.matmul(bk.ap[:, g * 128:(g + 1) * 128], lhsT=ones_bf[:],
                                                                 rhs=gw_bf[l][:, g, :], start=True, stop=True)))(l, g, bkA),
                     reads=PARAMS, writes=bkA.keys)
            for hf in range(2):
                P.op("pe", (lambda l, hf, bk: (lambda e: e.matmul(bk.ap[:, hf * 512:(hf + 1) * 512], lhsT=ones_f[0:1, :],
                                                                  rhs=bsrow[0:1, l * 1024 + hf * 512:l * 1024 + (hf + 1) * 512],
                                                                  start=True, stop=True)))(l, hf, bkB),
                     reads=PARAMS, writes=bkB.keys)
            P.op("act", (lambda l, bk: (lambda e: e.activation(out=Eg[l][:], in_=bk.ap.rearrange("p (a b) -> p a b", a=8),
                                                               func=AF.Copy)))(l, bkB), reads=bkB.keys + PARAMS, writes=PARAMS)
            for g in range(8):
                P.op("dve", (lambda l, g, bk: (lambda e: e.scalar_tensor_tensor(
                    out=Eg[l][:, g, :], in0=bk.ap[:, g * 128:(g + 1) * 128], scalar=pv[l][:, PV_LNB + g:PV_LNB + g + 1],
                    in1=Eg[l][:, g, :], op0=ALU.mult, op1=ALU.add)))(l, g, bkA), reads=bkA.keys + PARAMS, writes=PARAMS)

        def rmsnorm(l, gcol):
            bk = bank()
            for c in range(8):
                act(sqb[c % 2], h[c], AF.Square)
                mm(bk, B(ones_bf[:], PARAMS), sqb[c % 2], start=(c == 0), stop=(c == 7))
            act(lnt, bk, AF.Ln, bias=eps_b, scale=1.0 / D)
            act(rstd, lnt, AF.Exp, scale=-0.5)
            for c in range(8):
                stt(hn[c], h[c], pvc(l, gcol + c), rstd, ALU.mult, ALU.mult)

        def conv4(dst, xbuf, wcol, bcol, l, acc):
            ts(acc, xbuf.s(xbuf.ap[:, 0:T]), pvc(l, wcol + 0), ALU.mult, pvc(l, bcol), ALU.add)
            for k in (1, 2):
                stt(acc, xbuf.s(xbuf.ap[:, k:k + T]), pvc(l, wcol + k), acc, ALU.mult, ALU.add)
            stt(dst, xbuf.s(xbuf.ap[:, 3:3 + T]), pvc(l, wcol + 3), acc, ALU.mult, ALU.add)

        import os
        STOP = int(os.environ.get("MK_STOP", "99"))

        def mark(name):
            PHASES.append((name, len(P.ops["pe"])))

        def layer(l, ti):
            win3 = w3(w_in, l)
            mark("norm1")
            if STOP < 1:
                return
            rmsnorm(l, PV_NMIX)
            if STOP < 2:
                return
            mark('A')
            AR.reset()
            vtok = [AR.f32(1024) for _ in range(NJ)]
            vn = [AR.bf16(1024) for _ in range(NJ)]
            tmpA = AR.f32(T)
            junk = AR.bf16(1024)
            for hf in range(2):
                wt = wload(win3, 0, C_V + hf * 512)
                for j in range(NJ):
                    bk = bank()
                    for kc in range(8):
                        mm(bk, hn[kc].s(hn[kc].ap[:, j * 128:(j + 1) * 128]), wl(wt, kc, 0, 512), kc == 0, kc == 7)
                    act(vtok[j].s(vtok[j].ap[:, hf * 512:(hf + 1) * 512]), bk, AF.Gelu_apprx_tanh)
            def sm1():
                b_ = sm(8)
                return b_.s(b_.ap[:, 0:1])

            st_ = [[sm1() for _ in range(6)] for j in range(NJ)]
            for j in range(NJ):
                memset(st_[j][1], 0.0)
            for j in range(NJ):
                red_sum(st_[j][0], vtok[j])
                act(junk, vtok[j], AF.Square, accum=st_[j][1])
            for j in range(NJ):
                ts(st_[j][2], st_[j][0], 1.0 / 1024, ALU.mult)
            for j in range(NJ):
                tt(st_[j][4], st_[j][2], st_[j][2], ALU.mult)
            for j in range(NJ):
                stt(st_[j][3], st_[j][1], 1.0 / 1024, st_[j][4], ALU.mult, ALU.subtract)
            for j in range(NJ):
                act(st_[j][3], st_[j][3], AF.Ln, bias=eps_b, scale=1.0)
            for j in range(NJ):
                act(st_[j][5], st_[j][3], AF.Exp, scale=-0.5)
            for j in range(NJ):
                ts(vn[j], vtok[j], st_[j][2], ALU.subtract, st_[j][5], ALU.mult)
            sm_p[0] = 0
            for hf in range(2):
                wt = wload(win3, 0, C_U + hf * 512)
                for m4 in range(4):
                    m = hf * 4 + m4
                    bk = bank()
                    for kc in range(8):
                        mm(bk, wl(wt, kc, m4 * 128, 128), hn[kc], kc == 0, kc == 7)
                    act(uT[m], bk, AF.Gelu_apprx_tanh)
            for g in range(8):
                bk = bank()
                for j in range(NJ):
                    mm(bk.s(bk.ap[:, j * 128:(j + 1) * 128]), vn[j].s(vn[j].ap[:, g * 128:(g + 1) * 128]),
                       B(gw_bf[l][:, g, :], PARAMS))
                stt(tmpA.s(tmpA.ap.rearrange("p (a b) -> p a b", a=NJ)), bk.s(bk.ap.rearrange("p (a b) -> p a b", a=NJ)),
                    pvc(l, PV_LNG + g), B(bc(Eg[l][:, g, :], [128, NJ, 128], 1), PARAMS), ALU.mult, ALU.add)
                tt(ya[g], tmpA, uT[g], ALU.mult)
            if STOP < 3:
                return
            mark('B')
            AR.reset()
            hseq4 = AR.f32(4 * T, shape=(4, T))
            xbufs = [AR.f32(T + 4) for _ in range(4)]
            accs = [AR.f32(T) for _ in range(4)]
            xcs = [AR.f32(T) for _ in range(4)]
            xc_bfs = [AR.bf16(T) for _ in range(2)]
            r4 = [AR.f32(T) for _ in range(4)]
            i2 = [AR.f32(T) for _ in range(2)]
            a4 = accs
            gg = xbufs[0].s(xbufs[0].ap[:, 0:T])
            for hf in range(2):
                wt = wload(win3, 0, C_XB + hf * 512)
                bks = []
                for m4 in range(4):
                    bk = bank()
                    for kc in range(8):
                        mm(bk, wl(wt, kc, m4 * 128, 128), hn[kc], kc == 0, kc == 7)
                    bks.append(bk)

                def stA(m4):
                    hd = hf * 4 + m4
                    xbuf, acc = xbufs[m4], accs[m4]
                    hl = B(hist_l[l][:, hd, 0:3], [("hl", l, hd)])
                    cp(xbuf.s(xbuf.ap[:, 0:3]), hl)
                    cp(hl, bks[m4].s(bks[m4].ap[:, T - 3:T]))
                    cp(xbuf.s(xbuf.ap[:, 3:3 + T]), bks[m4], eng="act")
                    act(acc, bks[m4], AF.Identity, bias=pvc(l, PV_LCB + hd), scale=pvc(l, PV_LCW + hd * 4 + 3))

                def stB2(ms):
                    for k in range(3):
                        for m4 in ms:
                            hd = hf * 4 + m4
                            dst = xcs[m4] if k == 2 else accs[m4]
                            stt(dst, xbufs[m4].s(xbufs[m4].ap[:, k:k + T]), pvc(l, PV_LCW + hd * 4 + k), accs[m4],
                                ALU.mult, ALU.add)
                    out = {}
                    for m4 in ms:
                        cp(xc_bfs[m4 % 2], xcs[m4], eng="act")
                    for m4 in ms:
                        hd = hf * 4 + m4
                        bkr = bank()
                        mm(bkr, B(wr_bf[l][:, hd, :], PARAMS), xc_bfs[m4 % 2])
                        bki = bank()
                        mm(bki, B(wi_bf[l][:, hd, :], PARAMS), xc_bfs[m4 % 2])
                        out[m4] = (bkr, bki)
                    return out

                ri = {}
                for m4 in range(4):
                    stA(m4)
                ri.update(stB2((0, 1)))
                ri.update(stB2((2, 3)))
                for m4 in range(4):
                    hd = hf * 4 + m4
                    bkr, bki = ri[m4]
                    act(r4[m4], bkr, AF.Sigmoid, bias=pvc(l, PV_BR + hd), scale=1.0)
                    act(i2[m4 % 2], bki, AF.Sigmoid, bias=pvc(l, PV_BI + hd), scale=1.0)
                    tt(xcs[m4], xcs[m4], i2[m4 % 2], ALU.mult)
                for m4 in range(4):
                    hd = hf * 4 + m4
                    act(a4[m4], r4[m4], AF.Exp, scale=B(clru[l][:, hd:hd + 1], PARAMS))
                for m4 in range(4):
                    hd = hf * 4 + m4
                    act(r4[m4], r4[m4], AF.Exp, scale=B(clru[l][:, 8 + hd:9 + hd], PARAMS))
                for m4 in range(4):
                    act(r4[m4], r4[m4], AF.Ln, bias=one_b, scale=-1.0)
                for m4 in range(4):
                    act(r4[m4], r4[m4], AF.Exp, scale=0.5)
                for m4 in range(4):
                    tt(xcs[m4], xcs[m4], r4[m4], ALU.mult)
                for m4 in range(4):
                    hd = hf * 4 + m4
                    hq = hseq4.s(hseq4.ap[:, m4, :], hseq4.keys[m4 * 4:(m4 + 1) * 4])
                    hs = B(hst[l][:, hd:hd + 1], [("hst", l, hd)])
                    scan(hq, a4[m4], xcs[m4], hs)
                    cp(hs, hq.s(hq.ap[:, T - 1:T]))
                wt = wload(win3, 0, C_GL + hf * 512)
                for m4 in range(4):
                    hd = hf * 4 + m4
                    bk = bank()
                    for kc in range(8):
                        mm(bk, wl(wt, kc, m4 * 128, 128), hn[kc], kc == 0, kc == 7)
                    act(gg, bk, AF.Gelu_apprx_tanh)
                    hq = hseq4.s(hseq4.ap[:, m4, :], hseq4.keys[m4 * 4:(m4 + 1) * 4])
                    tt(yb[hd], gg, hq, ALU.mult)
            if STOP < 4:
                return
            mark('Cconv')
            AR.reset()
            xsT = mg
            BT = AR.bf16(4 * T, shape=(4, T))
            CT = AR.bf16(4 * T, shape=(4, T))
            arena_mark = AR.p
            xbufs = [AR.f32(T + 4) for _ in range(4)]
            accs = [AR.f32(T) for _ in range(4)]
            xcvs = [AR.f32(T) for _ in range(4)]
            cbks = {}
            ctiles = (C_X, C_X + 512, C_B, C_C)
            nproj = [0]

            def cproj_upto(ch):
                while nproj[0] <= min(ch, 15):
                    wt = wload(win3, 0, ctiles[nproj[0] // 4])
                    for m4 in range(4):
                        bk = bank()
                        for kc in range(8):
                            mm(bk, wl(wt, kc, m4 * 128, 128), hn[kc], kc == 0, kc == 7)
                        cbks[nproj[0]] = bk
                        nproj[0] += 1

            def cstA(ch):
                xbuf, acc = xbufs[ch % 4], accs[ch % 4]
                hl = B(hist_s[l][:, ch, 0:3], [("hs", l, ch)])
                cp(xbuf.s(xbuf.ap[:, 0:3]), hl)
                cp(hl, cbks[ch].s(cbks[ch].ap[:, T - 3:T]))
                cp(xbuf.s(xbuf.ap[:, 3:3 + T]), cbks[ch], eng="act")
                act(acc, cbks[ch], AF.Identity, bias=pvc(l, PV_SCB + ch), scale=pvc(l, PV_SCW + ch * 4 + 3))

            def cstB2(chs):
                for k in range(3):
                    for ch in chs:
                        dst = xcvs[ch % 4] if k == 2 else accs[ch % 4]
                        stt(dst, xbufs[ch % 4].s(xbufs[ch % 4].ap[:, k:k + T]), pvc(l, PV_SCW + ch * 4 + k), accs[ch % 4],
                            ALU.mult, ALU.add)
                for ch in chs:
                    if ch < 8:
                        dst = xsT[ch]
                    elif ch < 12:
                        dst = BT.s(BT.ap[:, ch - 8, :])
                    else:
                        dst = CT.s(CT.ap[:, ch - 12, :])
                    act(dst, xcvs[ch % 4], AF.Silu)

            cproj_upto(1)
            cstA(0)
            cstA(1)
            for p in range(8):
                if p + 1 < 8:
                    cproj_upto(2 * p + 3)
                    cstA(2 * p + 2)
                    cstA(2 * p + 3)
                cstB2((2 * p, 2 * p + 1))
            mark('Cdt')
            sm_p[0] = 0
            NH4 = NJ * 16
            dtx, ax, ex, lgx, dt_, adt, cs_sb, ecs, tmc, ds, cd, dtds = [sm(NH4) for _ in range(12)]
            psd = bank()
            for j in range(NJ):
                for kc in range(8):
                    mm(psd.s(psd.ap[:, j * 16:(j + 1) * 16]), hn[kc].s(hn[kc].ap[:, j * 128:(j + 1) * 128]),
                       B(wdt_bf[l][:, kc, :], PARAMS), kc == 0, kc == 7)
            v4j = lambda b_: b_.s(b_.ap.rearrange("p (a b) -> p a b", a=NJ))
            tt(v4j(dtx), psd.s(psd.ap[:, 0:NH4].rearrange("p (a b) -> p a b", a=NJ)),
               B(bc(pr[l][:, PR_DTB:PR_DTB + 16], [128, NJ, 16], 1), PARAMS), ALU.add)
            ts(ax, dtx, -1.0, ALU.mult)
            tt(ax, ax, dtx, ALU.max)
            act(ex, ax, AF.Exp, scale=-1.0)
            act(lgx, ex, AF.Ln, bias=one_b, scale=1.0)
            ts(dt_, dtx, 0.0, ALU.max)
            tt(dt_, dt_, lgx, ALU.add)
            tt(v4j(adt), v4j(dt_), B(bc(arow[l][:], [128, NJ, 16], 1), PARAMS), ALU.mult)
            mark('Cz')
            szall = [B(uT_t[:, 2 * j:2 * j + 2, :].rearrange("p a b -> p (a b)"), [("uT", 2 * j), ("uT", 2 * j + 1)])
                     for j in range(NJ)]
            for hf in range(2):
                wt = wload(win3, 0, C_Z + hf * 512)
                for j in range(NJ):
                    bk = bank()
                    for kc in range(8):
                        mm(bk, hn[kc].s(hn[kc].ap[:, j * 128:(j + 1) * 128]), wl(wt, kc, 0, 512), kc == 0, kc == 7)
                    act(szall[j].s(szall[j].ap[:, hf * 512:(hf + 1) * 512]), bk, AF.Silu)
            mark('Cdt2')
            for j in range(NJ):
                mm(psd.s(psd.ap[:, NH4 + j * 16:NH4 + (j + 1) * 16]), B(U_f, PARAMS), adt.s(adt.ap[:, j * 16:(j + 1) * 16]))
                mm(psd.s(psd.ap[:, 2 * NH4 + j * 16:2 * NH4 + (j + 1) * 16]), B(ones_f[:], PARAMS),
                   adt.s(adt.ap[:, j * 16:(j + 1) * 16]))
            cp(cs_sb, psd.s(psd.ap[:, NH4:2 * NH4]))
            act(ecs, cs_sb, AF.Exp)
            tt(tmc, psd.s(psd.ap[:, 2 * NH4:3 * NH4]), cs_sb, ALU.subtract)
            act(ds, tmc, AF.Exp)
            act(cd, psd.s(psd.ap[:, 2 * NH4:3 * NH4]), AF.Exp)
            tt(dtds, dt_, ds, ALU.mult)
            AR.p = arena_mark
            Lb = AR.f32(16 * 128, shape=(16, 128))
            Lq = [Lb.s(Lb.ap[:, q * 4:(q + 1) * 4, :], Lb.keys[q * 4:(q + 1) * 4]) for q in range(4)]
            cbm = AR.f32(4 * 128, shape=(4, 128))
            FB = []
            for _ in range(2):
                FB.append(dict(Mb=AR.bf16(16 * 128, shape=(16, 128)), xs_bf=AR.bf16(1024), xdt=AR.bf16(1024),
                               xdd=AR.bf16(1024), Btok=AR.bf16(512)))
            ytmp = AR.f32(1024)
            yn = AR.bf16(1024)
            ss = sm(4)
            rs4 = sm(4)
            v3 = lambda b_: b_.s(b_.ap.rearrange("p (a b) -> p a b", a=16))
            v4 = lambda b_: b_.s(b_.ap.rearrange("p (a b) -> p a b", a=4))

            def hs(b_, j):
                return b_.s(b_.ap[:, j * 16:(j + 1) * 16])

            def front(j):
                js = slice(j * 128, (j + 1) * 128)
                f = FB[j % 2]
                adt_j, dt_j, dtds_j = hs(adt, j), hs(dt_, j), hs(dtds, j)
                tt(Lb, B(bc(Ls_f, [128, 16, 128], 1), PARAMS), adt_j.s(bc(adt_j.ap, [128, 16, 128], 2)), ALU.mult, eng="pool")
                psc = bank()
                for g in range(4):
                    mm(psc.s(psc.ap[:, g * 128:(g + 1) * 128]), BT.s(BT.ap[:, g, js]), CT.s(CT.ap[:, g, js]))
                tt(cbm, psc.s(psc.ap.rearrange("p (a b) -> p a b", a=4)), B(bc(U_f, [128, 4, 128], 1), PARAMS), ALU.mult)
                for q in range(4):
                    psg = bank()
                    for r in range(4):
                        hd = q * 4 + r
                        mm(psg.s(psg.ap[:, r * 128:(r + 1) * 128]), Lq[q].s(Lb.ap[:, hd, :]), B(U_f, PARAMS))
                    act(Lq[q], psg.s(psg.ap.rearrange("p (a b) -> p a b", a=4)), AF.Exp)
                    tt(f["Mb"].s(f["Mb"].ap[:, q * 4:(q + 1) * 4, :]), Lq[q], cbm.s(bc(cbm.ap[:, q, :], [128, 4, 128], 1)),
                       ALU.mult, eng="pool")
                pst = bank()
                pstb = pst.s(pst.ap.bitcast(BF16))
                for kc in range(8):
                    tr(pstb.s(pstb.ap[:, kc * 128:(kc + 1) * 128]), xsT[kc].s(xsT[kc].ap[:, js]))
                cp(f["xs_bf"], pstb, eng="act")
                tt(v3(f["xdt"]), v3(pstb), dt_j.s(bc(dt_j.ap, [128, 16, 64], 2)), ALU.mult)
                tt(v3(f["xdd"]), v3(pstb), dtds_j.s(bc(dtds_j.ap, [128, 16, 64], 2)), ALU.mult)
                psb = bank()
                psbb = psb.s(psb.ap.bitcast(BF16))
                for g in range(4):
                    tr(psbb.s(psbb.ap[:, g * 128:(g + 1) * 128]), BT.s(BT.ap[:, g, js]))
                cp(f["Btok"], psbb.s(psbb.ap[:, 0:512]), eng="act")

            def back(j):
                js = slice(j * 128, (j + 1) * 128)
                f = FB[j % 2]
                Mb, xs_bf, xdt, xdd, Btok = f["Mb"], f["xs_bf"], f["xdt"], f["xdd"], f["Btok"]
                ecs_j, cd_j = hs(ecs, j), hs(cd, j)
                yo = bank(2)
                for g in range(4):
                    mm(yo.s(yo.ap[:, g * 256:(g + 1) * 256]), CT.s(CT.ap[:, g, js]),
                       B(S_bf[l][:, g * 256:(g + 1) * 256], [("Sbf", l)]))
                yd = bank(2)
                for hd in range(16):
                    o_ = yd.s(yd.ap[:, hd * 64:(hd + 1) * 64])
                    mm(o_, Mb.s(Mb.ap[:, hd, :]), xdt.s(xdt.ap[:, hd * 64:(hd + 1) * 64]), True, False)
                    mm(o_, B(Dg[l][:, hd, :], PARAMS), xs_bf.s(xs_bf.ap[:, hd * 64:(hd + 1) * 64]), False, True)
                st = bank(2)
                for g in range(4):
                    mm(st.s(st.ap[:, g * 256:(g + 1) * 256]), Btok.s(Btok.ap[:, g * 128:(g + 1) * 128]),
                       xdd.s(xdd.ap[:, g * 256:(g + 1) * 256]))
                tt(v3(ytmp), v3(yo), ecs_j.s(bc(ecs_j.ap, [128, 16, 64], 2)), ALU.mult)
                tt(ytmp, yd, ytmp, ALU.add)
                Sb = B(S[l][:], [("S", l)])
                tt(v3(Sb), v3(Sb), cd_j.s(bc(cd_j.ap, [128, 16, 64], 2)), ALU.mult)
                tt(Sb, st, Sb, ALU.add)
                cp(B(S_bf[l][:], [("Sbf", l)]), Sb, eng="act")
                tt(ytmp, ytmp, szall[j], ALU.mult)
                memset(ss, 0.0)
                for g in range(4):
                    act(yn.s(yn.ap[:, g * 256:(g + 1) * 256]), ytmp.s(ytmp.ap[:, g * 256:(g + 1) * 256]), AF.Square,
                        accum=ss.s(ss.ap[:, g:g + 1]))
                act(rs4, ss, AF.Ln, bias=eps_b, scale=1.0 / 256)
                act(rs4, rs4, AF.Exp, scale=-0.5)
                tt(v4(yn), v4(ytmp), rs4.s(bc(rs4.ap, [128, 4, 256], 2)), ALU.mult)

            def back2(j):
                js = slice(j * 128, (j + 1) * 128)
                pyt = bank()
                pytb = pyt.s(pyt.ap.bitcast(BF16))
                for kc in range(8):
                    tr(pytb.s(pytb.ap[:, kc * 128:(kc + 1) * 128]), yn.s(yn.ap[:, kc * 128:(kc + 1) * 128]))
                for kc in range(8):
                    act(yc[kc].s(yc[kc].ap[:, js]), pytb.s(pytb.ap[:, kc * 128:(kc + 1) * 128]), AF.Copy,
                        scale=pvc(l, PV_SNG + kc))

            mark('Cchunks')
            front(0)
            front(1)
            back(0)
            for j in range(1, NJ):
                if j + 1 < NJ:
                    front(j + 1)
                back2(j - 1)
                back(j)
            back2(NJ - 1)
            if STOP < 5:
                return
            mark('merge')
            AR.reset()
            sm_p[0] = 0
            g4 = AR.f32(4 * T, shape=(4, T))
            tmpG = AR.f32(T)
            accm = uT
            for k, (wb, ybr) in enumerate(((w_ba, ya), (w_bb, yb), (w_bc, yc))):
                wb3 = w3(wb, l)
                for hf in range(2):
                    wt = wload(win3, 0, C_G + k * 1024 + hf * 512)
                    for m4 in range(4):
                        m = hf * 4 + m4
                        bk = bank()
                        for kc in range(8):
                            mm(bk, wl(wt, kc, m4 * 128, 128), hn[kc], kc == 0, kc == 7)
                        act(g4.s(g4.ap[:, m4, :], g4.keys[m4 * 4:(m4 + 1) * 4]), bk, AF.Sigmoid, bias=pvc(l, PV_BG + k * 8 + m), scale=1.0)
                    wt = wload(wb3, 0, hf * 512)
                    for m4 in range(4):
                        m = hf * 4 + m4
                        bk = bank()
                        for kc in range(8):
                            mm(bk, wl(wt, kc, m4 * 128, 128), ybr[kc], kc == 0, kc == 7)
                        gm = g4.s(g4.ap[:, m4, :], g4.keys[m4 * 4:(m4 + 1) * 4])
                        if k == 0:
                            tt(accm[m], bk, gm, ALU.mult)
                        elif k == 1:
                            tt(tmpG, bk, gm, ALU.mult)
                            tt(accm[m], accm[m], tmpG, ALU.add)
                        else:
                            tt(tmpG, bk, gm, ALU.mult)
                            tt(mg[m], accm[m], tmpG, ALU.add)
            if STOP < 6:
                return
            mark('outproj')
            wo3 = w3(w_out, l)
            for hf in range(2):
                wt = wload(wo3, 0, hf * 512)
                for m4 in range(4):
                    m = hf * 4 + m4
                    bk = bank()
                    for kc in range(8):
                        mm(bk, wl(wt, kc, m4 * 128, 128), mg[kc], kc == 0, kc == 7)
                    tt(h[m], bk, h[m], ALU.add)
            if STOP < 7:
                return
            mark('mlp_up')
            rmsnorm(l, PV_NMLP)
            AR.reset()
            rl = [AR.f32(T) for _ in range(2)]
            wu3 = w3(w_up, l)
            for f4 in range(8):
                wt = wload(wu3, 0, f4 * 512)
                for m4 in range(4):
                    f = f4 * 4 + m4
                    bk = bank()
                    for kc in range(8):
                        mm(bk, wl(wt, kc, m4 * 128, 128), hn[kc], kc == 0, kc == 7)
                    act(rl[f % 2], bk, AF.Relu)
                    tt(up[f], rl[f % 2], rl[f % 2], ALU.mult)
            mark('mlp_down')
            wd3 = w3(w_dn, l)
            for hf in range(2):
                bks = [bank() for _ in range(4)]
                for q in range(4):
                    wt = wload(wd3, q * 8, hf * 512)
                    for kc in range(8):
                        for m4 in range(4):
                            mm(bks[m4], wl(wt, kc, m4 * 128, 128), up[q * 8 + kc], (q == 0 and kc == 0), (q == 3 and kc == 7))
                for m4 in range(4):
                    m = hf * 4 + m4
                    tt(h[m], bks[m4], h[m], ALU.add)

        finals = []
        xT3 = xT.rearrange("(c p) t -> p c t", p=128)
        oT3 = outT.rearrange("(c p) t -> p c t", p=128)
        for ti in range(n_tiles):
            t0 = ti * T
            for c2 in range(2):
                P.dma("sp", (lambda c2, t0: (lambda e: e.dma_start(out=h_t[:, c2 * 4:(c2 + 1) * 4, :],
                                                                   in_=xT3[:, c2 * 4:(c2 + 1) * 4, t0:t0 + T])))(c2, t0),
                      "ld_x%d" % c2, writes=[("h", c) for c in range(c2 * 4, c2 * 4 + 4)])
            for l in range(DEPTH):
                layer(l, ti)
                if debug_taps and ti == 0:
                    tap("h_l%d" % l, B(h_t[:], [("h", c) for c in range(8)]), [128, 8, T])
            mark('final')
            bk = bank()
            for c in range(8):
                act(sqb[c % 2], h[c], AF.Square)
                mm(bk, B(ones_bf[:], PARAMS), sqb[c % 2], start=(c == 0), stop=(c == 7))
            act(lnt, bk, AF.Ln, bias=eps_b, scale=1.0 / D)
            act(rstd, lnt, AF.Exp, scale=-0.5)
            for c in range(8):
                stt(uT[c], h[c], pvc(0, PV_FIN + c), rstd, ALU.mult, ALU.mult)
            for c2 in range(2):
                finals.append(P.dma("sp", (lambda c2, t0: (lambda e: e.dma_start(out=oT3[:, c2 * 4:(c2 + 1) * 4, t0:t0 + T],
                                                                                 in_=uT_t[:, c2 * 4:(c2 + 1) * 4, :])))(c2, t0),
                                    "st_o%d" % c2, reads=[("uT", c) for c in range(c2 * 4, c2 * 4 + 4)]))
        finals.extend(taps.values())
        P.emit(final_waits=finals)
    return nc


def _vec8(v):
    return np.ascontiguousarray(v.reshape(-1, 128).T)


def prep_params(inp):
    f = lambda k: np.asarray(inp[k], dtype=np.float32)
    pvec = np.zeros((DEPTH, 128, NPV), np.float32)
    prow = np.zeros((DEPTH, 128, NPR), np.float32)
    for l in range(DEPTH):
        pvec[l, :, PV_NMIX:PV_NMIX + 8] = _vec8(f("norm_mix_g")[l])
        pvec[l, :, PV_NMLP:PV_NMLP + 8] = _vec8(f("norm_mlp_g")[l])
        pvec[l, :, PV_LNG:PV_LNG + 8] = _vec8(f("gmlp_ln_g")[l])
        pvec[l, :, PV_LNB:PV_LNB + 8] = _vec8(f("gmlp_ln_b")[l])
        cw = f("lru_conv_w")[l]
        pvec[l, :, PV_LCW:PV_LCW + 32] = cw.reshape(4, 8, 128).transpose(2, 1, 0).reshape(128, 32)
        pvec[l, :, PV_LCB:PV_LCB + 8] = _vec8(f("lru_conv_b")[l])
        pvec[l, :, PV_BR:PV_BR + 8] = _vec8(f("lru_b_r")[l])
        pvec[l, :, PV_BI:PV_BI + 8] = _vec8(f("lru_b_i")[l])
        pvec[l, :, PV_LAM:PV_LAM + 8] = _vec8(f("lru_lambda")[l])
        sw = f("ssd_conv_w")[l]
        pvec[l, :, PV_SCW:PV_SCW + 64] = sw.reshape(4, 16, 128).transpose(2, 1, 0).reshape(128, 64)
        pvec[l, :, PV_SCB:PV_SCB + 16] = _vec8(f("ssd_conv_b")[l])
        pvec[l, :, PV_BG:PV_BG + 24] = _vec8(f("b_gate")[l].reshape(-1))
        pvec[l, :, PV_SNG:PV_SNG + 8] = _vec8(f("ssd_norm_g")[l])
        pvec[l, :, PV_FIN:PV_FIN + 8] = _vec8(f("final_norm_g"))
        prow[l, :, PR_DTB:PR_DTB + 16] = f("ssd_dt_bias")[l][None, :]
        prow[l, :, PR_ALOG:PR_ALOG + 16] = f("ssd_a_log")[l][None, :]
        prow[l, :, PR_D:PR_D + 16] = f("ssd_d")[l][None, :]
    bsrow = np.ascontiguousarray(f("gmlp_b_s").reshape(DEPTH, 1, 1024))
    gwT = np.ascontiguousarray(f("gmlp_w_s").transpose(0, 3, 1, 2))
    wr = np.ascontiguousarray(f("lru_w_r").transpose(0, 2, 1, 3))
    wi = np.ascontiguousarray(f("lru_w_i").transpose(0, 2, 1, 3))
    consts = np.zeros((128, 3, 128), np.float32)
    consts[:, 0, :] = np.eye(128, dtype=np.float32)
    consts[:, 1, :] = np.triu(np.ones((128, 128), np.float32))
    consts[:, 2, :] = np.tril(np.ones((128, 128), np.float32), -1)
    return dict(pvec=pvec, prow=prow, bsrow=bsrow, gwT=gwT, wr=wr, wi=wi, consts=consts)


_CACHE = {}
PHASES = []


def kernel(**inputs):
    x = np.asarray(inputs["x"], dtype=np.float32)
    shared = prep_params(inputs)
    for k in ("w_in", "w_branch_a", "w_branch_b", "w_branch_c", "w_out", "w_mlp_up", "w_mlp_down"):
        shared[k] = np.ascontiguousarray(np.asarray(inputs[k], dtype=np.float32))
    n_tiles = SEQ // T
    if "nc" not in _CACHE:
        _CACHE["nc"] = build_program(n_tiles)
    nc = _CACHE["nc"]
    in_maps = []
    for core in range(N_CORES):
        b = core % BATCH
        m = dict(shared)
        m["xT"] = np.ascontiguousarray(x[b].T)
        in_maps.append(m)
    res = run_bass_kernel_spmd(nc, in_maps, core_ids=list(range(N_CORES)))
    out = np.empty((BATCH, SEQ, D), np.float32)
    for b in range(BATCH):
        out[b] = res.results[b]["outT"].T
    return out
```
